# Optimizing a Trainium2 kernel written in Bass

```python
import math
import jax, jax.numpy as jnp
from jax import lax
import numpy as np

D_MODEL = 1024
BATCH = 8
SEQ = 4096
DEPTH = 4

A_HEADS = 8
A_KV_HEADS = 2
A_GROUP = A_HEADS // A_KV_HEADS
HEAD_DIM = 64
CMP_LEN = 32
CMP_STRIDE = 16
CMP_HIDDEN = 128
SLC_BLOCK = 64
SLC_TOPK = 8
WINDOW = 512
Q_BLOCK = 128
N_BRANCH = 3
ROPE_THETA = 10000.0
A_WIDTH = A_HEADS * HEAD_DIM
KV_COLS = A_KV_HEADS * HEAD_DIM
B_WIDTH = D_MODEL // 2
B_GROUPS = 8
SHORT_CONV = 3
EVEN_SPLITS = (A_WIDTH, 6 * KV_COLS, A_HEADS * N_BRANCH, 3 * B_WIDTH)
IN_COLS_EVEN = sum(EVEN_SPLITS)
MIX_WIDTH_EVEN = A_WIDTH + B_WIDTH
R_WIDTH = D_MODEL
R_BLOCKS = 8
R_BLOCK_DIM = R_WIDTH // R_BLOCKS
R_CONV = 4
LRU_C = 8.0
D_FF = 2816
FFN_CONV = 3
NORM_EPS = 1e-6
N_EVEN = (DEPTH + 1) // 2
N_ODD = DEPTH // 2

kernel_name = "hybrid_nsa_shortconv_rglru_convffn"


def rmsnorm(x, g):
    xf = x.astype(jnp.float32)
    y = xf * lax.rsqrt(jnp.mean(xf * xf, axis=-1, keepdims=True) + NORM_EPS)
    return (y * g.astype(jnp.float32)).astype(x.dtype)


def causal_dwconv(u, w, b=None):
    k = w.shape[0]
    s = u.shape[1]
    up = jnp.pad(u, ((0, 0), (k - 1, 0), (0, 0)))
    y = sum(w[j] * up[:, j:j + s] for j in range(k))
    if b is not None:
        y = y + b
    return y


def rope_tables(s):
    inv = 1.0 / (ROPE_THETA ** (jnp.arange(0, HEAD_DIM, 2, dtype=jnp.float32) / HEAD_DIM))
    ang = jnp.arange(s, dtype=jnp.float32)[:, None] * inv[None, :]
    ang = jnp.concatenate([ang, ang], axis=-1)
    return jnp.cos(ang), jnp.sin(ang)


def apply_rope(x, cos, sin):
    h = HEAD_DIM // 2
    rot = jnp.concatenate([-x[..., h:], x[..., :h]], axis=-1)
    return (x * cos[:, None, :] + rot * sin[:, None, :]).astype(x.dtype)


def masked_softmax(s, mask):
    s = jnp.where(mask, s.astype(jnp.float32), -1e30)
    m = jnp.max(s, axis=-1, keepdims=True)
    p = jnp.exp(s - m) * mask
    return p / jnp.maximum(jnp.sum(p, axis=-1, keepdims=True), 1e-30)


def compress(kv, pe, w1, w2):
    bsz, s = kv.shape[:2]
    r = CMP_LEN // CMP_STRIDE
    n = s // CMP_STRIDE
    ch = kv.reshape(bsz, n, CMP_STRIDE, A_KV_HEADS, HEAD_DIM)
    blocks = jnp.concatenate([ch[:, j:n - r + 1 + j] for j in range(r)], axis=2)
    blocks = blocks + pe[None, None, :, None, :]
    flat = blocks.transpose(0, 3, 1, 2, 4).reshape(bsz, A_KV_HEADS, n - r + 1, CMP_LEN * HEAD_DIM)
    return jax.nn.gelu(flat @ w1) @ w2


def nsa_attention(q, kc, vc, ks, vs, kw, vw, gates):
    bsz, s = q.shape[:2]
    nc = kc.shape[2]
    nb = s // SLC_BLOCK
    n_sel = min(SLC_TOPK, nb)
    n_qb = s // Q_BLOCK
    scale = HEAD_DIM ** -0.5
    qg = q.reshape(bsz, s, A_KV_HEADS, A_GROUP, HEAD_DIM).transpose(0, 2, 3, 1, 4)
    gg = gates.reshape(bsz, s, A_KV_HEADS, A_GROUP, N_BRANCH).transpose(0, 2, 3, 1, 4)
    ks_blk = ks.transpose(0, 2, 1, 3).reshape(bsz, A_KV_HEADS, nb, SLC_BLOCK, HEAD_DIM)
    vs_blk = vs.transpose(0, 2, 1, 3).reshape(bsz, A_KV_HEADS, nb, SLC_BLOCK, HEAD_DIM)
    pad = ((0, 0), (0, 0), (WINDOW, 0), (0, 0))
    kw_pad = jnp.pad(kw.transpose(0, 2, 1, 3), pad)
    vw_pad = jnp.pad(vw.transpose(0, 2, 1, 3), pad)
    cmp_start = jnp.arange(nc) * CMP_STRIDE
    cmp_end = cmp_start + CMP_LEN - 1
    bidx = jnp.arange(nb)
    blk_start = bidx * SLC_BLOCK
    overlap = ((cmp_start[:, None] <= blk_start[None, :] + SLC_BLOCK - 1)
               & (cmp_end[:, None] >= blk_start[None, :])).astype(jnp.float32)
    gather = jax.vmap(jax.vmap(lambda kb, ix: kb[ix]))

    def block_fn(c):
        t0 = c * Q_BLOCK
        pos = t0 + jnp.arange(Q_BLOCK)
        qb = lax.dynamic_slice_in_dim(qg, t0, Q_BLOCK, axis=3)
        gb = lax.dynamic_slice_in_dim(gg, t0, Q_BLOCK, axis=3)
        sc = jnp.einsum('bhgqd,bhnd->bhgqn', qb, kc) * scale
        p_cmp = masked_softmax(sc, cmp_end[None, :] <= pos[:, None])
        o_cmp = jnp.einsum('bhgqn,bhnd->bhgqd', p_cmp.astype(vc.dtype), vc)
        imp = jnp.einsum('bhqn,nk->bhqk', jnp.sum(p_cmp, axis=2), overlap)
        cur = pos // SLC_BLOCK
        forced = ((bidx[None, :] == 0) | (bidx[None, :] == cur[:, None])
                  | (bidx[None, :] == cur[:, None] - 1))
        valid = bidx[None, :] <= cur[:, None]
        score = jnp.where(forced, 1e4, jnp.where(valid, imp, -1e4))
        _, idx = lax.top_k(score, n_sel)
        ksel = gather(ks_blk, idx).reshape(bsz, A_KV_HEADS, Q_BLOCK, n_sel * SLC_BLOCK, HEAD_DIM)
        vsel = gather(vs_blk, idx).reshape(bsz, A_KV_HEADS, Q_BLOCK, n_sel * SLC_BLOCK, HEAD_DIM)
        kpos = (idx[..., None] * SLC_BLOCK + jnp.arange(SLC_BLOCK)).reshape(
            bsz, A_KV_HEADS, Q_BLOCK, n_sel * SLC_BLOCK)
        ss = jnp.einsum('bhgqd,bhqmd->bhgqm', qb, ksel) * scale
        p_slc = masked_softmax(ss, (kpos <= pos[:, None])[:, :, None])
        o_slc = jnp.einsum('bhgqm,bhqmd->bhgqd', p_slc.astype(vsel.dtype), vsel)
        kwb = lax.dynamic_slice_in_dim(kw_pad, t0, WINDOW + Q_BLOCK, axis=2)
        vwb = lax.dynamic_slice_in_dim(vw_pad, t0, WINDOW + Q_BLOCK, axis=2)
        wpos = t0 - WINDOW + jnp.arange(WINDOW + Q_BLOCK)
        diff = pos[:, None] - wpos[None, :]
        wmask = (diff >= 0) & (diff < WINDOW) & (wpos[None, :] >= 0)
        sw = jnp.einsum('bhgqd,bhkd->bhgqk', qb, kwb) * scale
        p_win = masked_softmax(sw, wmask)
        o_win = jnp.einsum('bhgqk,bhkd->bhgqd', p_win.astype(vwb.dtype), vwb)
        g = jax.nn.sigmoid(gb.astype(jnp.float32))
        o = g[..., 0:1] * o_cmp + g[..., 1:2] * o_slc + g[..., 2:3] * o_win
        return o.astype(q.dtype)

    out = lax.map(block_fn, jnp.arange(n_qb))
    return out.transpose(1, 0, 4, 2, 3, 5).reshape(bsz, s, A_WIDTH)


def even_mixer(h, w_in, cmp_pe, cmp_w1, cmp_w2, conv_w, w_out, cos, sin):
    bsz, s, _ = h.shape
    z = h @ w_in
    o1 = EVEN_SPLITS[0]
    o2 = o1 + EVEN_SPLITS[1]
    o3 = o2 + EVEN_SPLITS[2]
    q = apply_rope(z[..., :o1].reshape(bsz, s, A_HEADS, HEAD_DIM), cos, sin)
    kv = z[..., o1:o2].reshape(bsz, s, 6, A_KV_HEADS, HEAD_DIM)
    gates = z[..., o2:o3]
    bc = z[..., o3:]
    kc = compress(apply_rope(kv[:, :, 0], cos, sin), cmp_pe[0], cmp_w1[0], cmp_w2[0])
    vc = compress(kv[:, :, 1], cmp_pe[1], cmp_w1[1], cmp_w2[1])
    ks = apply_rope(kv[:, :, 2], cos, sin)
    kw = apply_rope(kv[:, :, 4], cos, sin)
    o_a = nsa_attention(q, kc, vc, ks, kv[:, :, 3], kw, kv[:, :, 5], gates)
    bg = bc[..., :B_WIDTH]
    cg = bc[..., B_WIDTH:2 * B_WIDTH]
    xg = bc[..., 2 * B_WIDTH:]
    o_b = bg * causal_dwconv(cg * xg, conv_w)
    return jnp.concatenate([o_a, o_b], axis=-1) @ w_out


def rglru_mixer(h, w_in, b_in, conv_w, conv_b, w_a, b_a, w_i, b_i, lam, w_out, b_out):
    bsz, s, _ = h.shape
    z = h @ w_in + b_in
    y = jax.nn.gelu(z[..., :R_WIDTH])
    u = causal_dwconv(z[..., R_WIDTH:], conv_w, conv_b)
    ub = u.reshape(bsz, s, R_BLOCKS, R_BLOCK_DIM)
    r = jax.nn.sigmoid(jnp.einsum('bsnd,nde->bsne', ub, w_a).reshape(bsz, s, R_WIDTH) + b_a)
    i = jax.nn.sigmoid(jnp.einsum('bsnd,nde->bsne', ub, w_i).reshape(bsz, s, R_WIDTH) + b_i)
    log_a = LRU_C * r.astype(jnp.float32) * jax.nn.log_sigmoid(lam.astype(jnp.float32))
    a = jnp.exp(log_a)
    mult = jnp.sqrt(-jnp.expm1(2.0 * log_a))
    bterm = mult * (i * u).astype(jnp.float32)

    def combine(left, right):
        a1, b1 = left
        a2, b2 = right
        return a2 * a1, a2 * b1 + b2

    _, hs = lax.associative_scan(combine, (a, bterm), axis=1)
    return (hs.astype(h.dtype) * y) @ w_out + b_out


def conv_ffn(h, w_up, conv_w, conv_b, w_down):
    u = causal_dwconv(h @ w_up, conv_w, conv_b)
    return (jax.nn.silu(u[..., :D_FF]) * u[..., D_FF:]) @ w_down


def setup_inputs(seed: int = 0) -> dict:
    key = jax.random.key(seed)
    k = jax.random.split(key, 26)
    f32 = jnp.float32

    def nrm(kk, shape, fan):
        return jax.random.normal(kk, shape, f32) * (fan ** -0.5)

    def small(kk, shape):
        return 0.01 * jax.random.normal(kk, shape, f32)

    a0 = jax.random.uniform(k[17], (N_ODD, R_WIDTH), f32, minval=0.9, maxval=0.999)
    return {
        "x": jax.random.normal(k[0], (BATCH, SEQ, D_MODEL), f32),
        "norm_mix": 1.0 + small(k[1], (DEPTH, D_MODEL)),
        "norm_ffn": 1.0 + small(k[2], (DEPTH, D_MODEL)),
        "norm_final": 1.0 + small(k[3], (D_MODEL,)),
        "a_w_in": nrm(k[4], (N_EVEN, D_MODEL, IN_COLS_EVEN), D_MODEL),
        "a_cmp_pe": 0.1 * jax.random.normal(k[5], (N_EVEN, 2, CMP_LEN, HEAD_DIM), f32),
        "a_cmp_w1": nrm(k[6], (N_EVEN, 2, CMP_LEN * HEAD_DIM, CMP_HIDDEN), CMP_LEN * HEAD_DIM),
        "a_cmp_w2": nrm(k[7], (N_EVEN, 2, CMP_HIDDEN, HEAD_DIM), CMP_HIDDEN),
        "a_conv_w": nrm(k[8], (N_EVEN, SHORT_CONV, B_WIDTH), SHORT_CONV),
        "a_w_out": nrm(k[9], (N_EVEN, MIX_WIDTH_EVEN, D_MODEL), MIX_WIDTH_EVEN),
        "c_w_in": nrm(k[10], (N_ODD, D_MODEL, 2 * R_WIDTH), D_MODEL),
        "c_b_in": small(k[11], (N_ODD, 2 * R_WIDTH)),
        "c_conv_w": nrm(k[12], (N_ODD, R_CONV, R_WIDTH), R_CONV),
        "c_conv_b": small(k[13], (N_ODD, R_WIDTH)),
        "c_w_a": nrm(k[14], (N_ODD, R_BLOCKS, R_BLOCK_DIM, R_BLOCK_DIM), R_BLOCK_DIM),
        "c_b_a": small(k[15], (N_ODD, R_WIDTH)),
        "c_w_i": nrm(k[16], (N_ODD, R_BLOCKS, R_BLOCK_DIM, R_BLOCK_DIM), R_BLOCK_DIM),
        "c_b_i": small(k[18], (N_ODD, R_WIDTH)),
        "c_lambda": jnp.log(a0) - jnp.log1p(-a0),
        "c_w_out": nrm(k[19], (N_ODD, R_WIDTH, D_MODEL), R_WIDTH),
        "c_b_out": small(k[20], (N_ODD, D_MODEL)),
        "f_w_up": nrm(k[21], (DEPTH, D_MODEL, 2 * D_FF), D_MODEL),
        "f_conv_w": nrm(k[22], (DEPTH, FFN_CONV, 2 * D_FF), FFN_CONV),
        "f_conv_b": small(k[23], (DEPTH, 2 * D_FF)),
        "f_w_down": nrm(k[24], (DEPTH, D_FF, D_MODEL), D_FF),
    }


def reference(x, norm_mix, norm_ffn, norm_final, a_w_in, a_cmp_pe, a_cmp_w1, a_cmp_w2,
              a_conv_w, a_w_out, c_w_in, c_b_in, c_conv_w, c_conv_b, c_w_a, c_b_a,
              c_w_i, c_b_i, c_lambda, c_w_out, c_b_out, f_w_up, f_conv_w, f_conv_b,
              f_w_down):
    cos, sin = rope_tables(x.shape[1])
    for layer in range(DEPTH):
        h = rmsnorm(x, norm_mix[layer])
        j = layer // 2
        if layer % 2 == 0:
            mix = even_mixer(h, a_w_in[j], a_cmp_pe[j], a_cmp_w1[j], a_cmp_w2[j],
                             a_conv_w[j], a_w_out[j], cos, sin)
        else:
            mix = rglru_mixer(h, c_w_in[j], c_b_in[j], c_conv_w[j], c_conv_b[j],
                              c_w_a[j], c_b_a[j], c_w_i[j], c_b_i[j], c_lambda[j],
                              c_w_out[j], c_b_out[j])
        x = x + mix
        x = x + conv_ffn(rmsnorm(x, norm_ffn[layer]), f_w_up[layer], f_conv_w[layer],
                         f_conv_b[layer], f_w_down[layer])
    return rmsnorm(x, norm_final)
```

```python
import contextlib
import os
import numpy as np
import concourse.bass as bass
import concourse.mybir as mybir
from concourse.bass_utils import run_bass_kernel_spmd

F32 = mybir.dt.float32
BF16 = mybir.dt.bfloat16
ALU = mybir.AluOpType
AF = mybir.ActivationFunctionType

S = 4096
D = 1024
NT = 8
TT = 512
DFF = 2816
NPAIR = 22
BIG = 30000.0
EPS = 1e-6


class T:
    __slots__ = ("name", "w", "r", "ds", "dram")

    def __init__(self, name="", dram=False):
        self.name = name
        self.w = {}
        self.r = {}
        self.ds = None
        self.dram = dram
        if not dram:
            _SCOPES[-1].append(self)


_SCOPES = [[]]


class Prog:
    def __init__(self, nc, es, n_dsem=72):
        self.nc = nc
        self.E = {"pe": nc.tensor, "dve": nc.vector, "act": nc.scalar, "pool": nc.gpsimd, "sp": nc.sync}
        self.sem = {k: es.enter_context(nc.semaphore("e_" + k)) for k in self.E}
        self.cnt = {k: 0 for k in self.E}
        self.seen = {k: {} for k in self.E}
        self.dpool = [[es.enter_context(nc.semaphore("d%d" % i)), 0] for i in range(n_dsem)]
        self.dperm = set()
        self.dfree = list(range(n_dsem))

    def dsem(self, t, perm=False):
        if t.ds is None:
            t.ds = self.dfree.pop()
            if perm:
                self.dperm.add(t.ds)
        return self.dpool[t.ds]

    def _waits(self, eng, r, w):
        need = {}
        seen = self.seen[eng]

        def add(evs):
            for key, (sem, val, e, small) in evs.items():
                if e == eng and not (small and self.cnt[eng] - val < 4):
                    continue
                if seen.get(key, 0) >= val:
                    continue
                if key not in need or need[key][1] < val:
                    need[key] = (sem, val)

        for t in r:
            add(t.w)
        for t in w:
            if not t.dram:
                add(t.w)
            add(t.r)
        for key, (sem, val) in need.items():
            self.E[eng].wait_ge(sem, val)
            seen[key] = val

    def op(self, eng, fn, r=(), w=(), n=512):
        self._waits(eng, r, w)
        ins = fn(self.E[eng])
        self.cnt[eng] += 1
        ins.then_inc(self.sem[eng], 1)
        key = "e_" + eng
        ev = (self.sem[eng], self.cnt[eng], eng, n < 256)
        for t in w:
            t.w = {key: ev}
            t.r = {}
        for t in r:
            t.r[key] = ev

    def dma(self, q, out, in_, sb, r=(), w=(), perm=False, **kw):
        self._waits(q, r, w)
        d = self.dsem(sb, perm)
        d[1] += 16
        self.E[q].dma_start(out=out, in_=in_, **kw).then_inc(d[0], 16)
        key = "d%d" % sb.ds
        ev = (d[0], d[1], "dma", False)
        for t in w:
            if t.dram:
                if t.r:
                    t.w = {}
                    t.r = {}
                t.w[key] = ev
            else:
                t.w = {key: ev}
                t.r = {}
        for t in r:
            t.r[key] = ev

    def barrier(self):
        sp = self.E["sp"]
        seen = self.seen["sp"]
        for k in self.E:
            if k != "sp" and self.cnt[k] > seen.get("e_" + k, 0):
                sp.wait_ge(self.sem[k], self.cnt[k])
        for i, (s, c) in enumerate(self.dpool):
            if c > seen.get("d%d" % i, 0):
                sp.wait_ge(s, c)
        self.cnt["sp"] += 1
        sp.sem_inc(self.sem["sp"], 1)
        for k in self.E:
            if k != "sp":
                self.E[k].wait_ge(self.sem["sp"], self.cnt["sp"])
            sn = self.seen[k]
            for k2 in self.E:
                sn["e_" + k2] = self.cnt[k2]
            for i, (s, c) in enumerate(self.dpool):
                sn["d%d" % i] = c

    def push(self):
        _SCOPES.append([])

    def pop(self):
        for t in _SCOPES.pop():
            if t.ds is not None and t.ds not in self.dperm:
                self.dfree.append(t.ds)
                t.ds = None


def _c(a):
    return np.ascontiguousarray(a, dtype=np.float32)


def _vec(a):
    a = np.asarray(a, dtype=np.float32)
    lead = a.shape[:-1]
    c = a.shape[-1] // 128
    a = a.reshape(lead + (c, 128))
    return _c(np.moveaxis(a, -1, 0))


def _wchunks(w, cols):
    k = w.shape[0]
    out = np.empty((len(cols), 128, k // 128, 128), np.float32)
    for i, ci in enumerate(cols):
        out[i] = w[:, ci].reshape(k // 128, 128, 128).transpose(1, 0, 2)
    return out


def _rows(w):
    k, n = w.shape
    return _c(w.reshape(k // 128, 128, n).transpose(1, 0, 2))


def make_consts():
    c = {}
    inv = 1.0 / (10000.0 ** (np.arange(0, 64, 2, dtype=np.float32) / 64.0))
    ang = np.arange(S, dtype=np.float32)[:, None] * inv[None, :]
    ang = np.concatenate([ang, ang], axis=-1).astype(np.float32)
    cos = np.cos(ang).astype(np.float32).T
    sin = np.sin(ang).astype(np.float32).T
    sgn = np.where(np.arange(64) < 32, -1.0, 1.0).astype(np.float32)[:, None]
    c["k_cos"] = _c(np.concatenate([cos, cos], 0))
    c["k_sin"] = _c(np.concatenate([sin * sgn, sin * sgn], 0))
    psw = np.zeros((128, 128), np.float32)
    for cp in range(128):
        base = (cp // 64) * 64
        psw[base + ((cp % 64) + 32) % 64, cp] = 1.0
    c["k_psw"] = psw
    c["k_ident"] = np.eye(128, dtype=np.float32)
    c["k_ones"] = np.ones((128, 128), np.float32)
    expb = np.zeros((128, S), np.float32)
    kk = np.arange(S)
    expb[kk // 64, kk] = BIG
    c["k_expb"] = expb
    j = np.arange(128)[:, None]
    i = np.arange(128)[None, :]
    tri = np.zeros((128, 2, 128), np.float32)
    tri[:, 0, :] = np.where(j > i, -BIG, 0.0)
    tri[:, 1, :] = np.where(j <= i, -BIG, 0.0)
    c["k_tri"] = tri
    shw = np.zeros((128, 512), np.float32)
    r = np.arange(512) - 256
    shw[0, r <= -2] = 1.0
    for jj in range(1, 9):
        shw[jj, r == jj - 2] = 1.0
    shw[9, r >= 7] = 1.0
    c["k_shw"] = shw
    pat = np.zeros((128, 128), np.float32)
    ii = np.arange(128)
    for jj in range(1, 9):
        pat[jj] = np.where(ii >= 31 + 16 * (jj - 2), 0.0, -BIG)
    pat[9] = -BIG
    c["k_pat"] = pat
    n = np.arange(256)[:, None]
    b = np.arange(64)[None, :]
    ov = ((16 * n <= 64 * b + 63) & (16 * n + 31 >= 64 * b)).astype(np.float32)
    ovc = np.zeros((128, 2, 65), np.float32)
    for jc in range(2):
        ovc[:, jc, :64] = ov[jc * 128:(jc + 1) * 128]
        ovc[:, jc, 64] = 1.0
    c["k_ovc"] = ovc
    pa = np.zeros((128, 128), np.float32)
    pb = np.zeros((128, 128), np.float32)
    for q in range(128):
        cr = q // 64
        for cc in range(127):
            rr = cc - 63
            if rr <= cr - 2:
                pa[q, cc] = 1.0
            elif rr <= cr:
                pb[q, cc] = 1e4
            else:
                pb[q, cc] = -1e4
    c["k_pa"] = pa
    c["k_pb"] = pb
    return c


def prep_shared(inp):
    g = {}
    f = lambda k: np.asarray(inp[k], dtype=np.float32)
    g["gmix"] = _vec(f("norm_mix"))
    g["gffn"] = _vec(f("norm_ffn"))
    g["gfin"] = _vec(f("norm_final"))
    wup = f("f_w_up")
    cols = [np.arange(j * 128, (j + 1) * 128) for j in range(44)]
    g["wup"] = np.stack([_wchunks(wup[l], cols) for l in range(4)])
    g["fcw"] = _c(np.moveaxis(f("f_conv_w").reshape(4, 3, 44, 128), 3, 0).transpose(0, 1, 3, 2))
    g["fcb"] = _vec(f("f_conv_b"))
    g["wdn"] = np.stack([_rows(f("f_w_down")[l]) for l in range(4)])
    cw = f("c_w_in")
    ccols = [np.concatenate([np.arange(n * 128, (n + 1) * 128), 1024 + np.arange(n * 128, (n + 1) * 128)]) for n in range(8)]
    cwin = np.empty((2, 8, 128, 8, 256), np.float32)
    for l in range(2):
        for n in range(8):
            cwin[l, n] = cw[l][:, ccols[n]].reshape(8, 128, 256).transpose(1, 0, 2)
    g["cwin"] = cwin
    g["cbin"] = _vec(f("c_b_in"))
    g["ccw"] = _c(np.moveaxis(f("c_conv_w").reshape(2, 4, 8, 128), 3, 0).transpose(0, 1, 3, 2))
    g["ccb"] = _vec(f("c_conv_b"))
    g["cwa"] = _c(f("c_w_a"))
    g["cwi"] = _c(f("c_w_i"))
    g["cba"] = _vec(f("c_b_a"))
    g["cbi"] = _vec(f("c_b_i"))
    g["clam"] = _vec(f("c_lambda"))
    g["cwout"] = np.stack([_rows(f("c_w_out")[l]) for l in range(2)])
    g["cbout"] = _vec(f("c_b_out"))
    aw = f("a_w_in")
    acols = []
    for i in range(4):
        acols.append(np.concatenate([np.arange(i * 64, (i + 1) * 64), np.arange((4 + i) * 64, (5 + i) * 64)]))
    for t in (0, 1, 2, 4):
        acols.append(512 + t * 128 + np.arange(128))
    for t in range(12):
        acols.append(1304 + t * 128 + np.arange(128))
    g["awin"] = np.stack([_wchunks(aw[l], acols) for l in range(2)])
    vcols = np.concatenate([512 + 3 * 128 + np.arange(128), 512 + 5 * 128 + np.arange(128), 1280 + np.arange(24)])
    g["awv"] = np.stack([_rows(aw[l][:, vcols]) for l in range(2)])
    g["acw"] = _c(np.moveaxis(f("a_conv_w").reshape(2, 3, 4, 128), 3, 0).transpose(0, 1, 3, 2))
    w1 = f("a_cmp_w1").reshape(2, 2, 32, 64, 128).transpose(0, 1, 3, 2, 4)
    g["aw1"] = _c(np.concatenate([w1, w1], axis=2))
    pe = f("a_cmp_pe").transpose(0, 1, 3, 2)
    g["ape"] = _c(np.concatenate([pe, np.zeros_like(pe)], axis=2))
    g["aw2"] = _c(f("a_cmp_w2"))
    g["awout"] = np.stack([_rows(f("a_w_out")[l]) for l in range(2)])
    g.update(make_consts())
    return {k: _c(v) for k, v in g.items()}


FULL_PLAN = [("even", 0), ("ffn", 0), ("odd", 1), ("ffn", 1), ("even", 2), ("ffn", 2), ("odd", 3), ("ffn", 3)]


def build(shapes, plan=None, final=True, dbg=None):
    plan = plan or FULL_PLAN
    nc = bass.Bass("TRN2", target_bir_lowering=False)
    es = contextlib.ExitStack()
    P = Prog(nc, es)
    dr = {}
    for name, shp in shapes.items():
        dr[name] = nc.dram_tensor(name, list(shp), F32, kind="ExternalInput").ap()
    out_d = nc.dram_tensor("out", [8, 128, S], F32, kind="ExternalOutput").ap()
    xr_d = nc.dram_tensor("xr", [8, 128, S], F32, kind="Internal").ap()
    os_d = nc.dram_tensor("osc", [8, 128, S], BF16, kind="Internal").ap()
    as_d = nc.dram_tensor("asc", [NPAIR, 128, S], BF16, kind="Internal").ap()
    qs_d = nc.dram_tensor("qsc", [4, 128, S], BF16, kind="Internal").ap()
    x_d = dr["xT"].rearrange("(c p) s -> c p s", p=128)
    XR = [T("xr%d" % t, dram=True) for t in range(NT)]
    OS = [T("os%d" % t, dram=True) for t in range(NT)]
    AS = [T("as%d" % t, dram=True) for t in range(NT)]
    QS = [T("qs%d" % t, dram=True) for t in range(NT)]
    XIN = T("xin", dram=True)
    OUT = T("out", dram=True)

    ps = [es.enter_context(nc.psum_tensor("ps%d" % i, [128, 512], F32)) for i in range(8)]
    PS = [T("ps%d" % i) for i in range(8)]

    uid = [0]

    def sb(name, shape, dt=F32):
        uid[0] += 1
        return nc.sbuf_tensor("s%d_%s" % (uid[0], name), list(shape), dt)

    hT = es.enter_context(sb("hT", [128, 8, S], BF16))
    HT = [[T("h%d_%d" % (c, t)) for t in range(NT)] for c in range(8)]
    ones_f = es.enter_context(sb("ones_f", [128, 128]))
    ident_f = es.enter_context(sb("ident_f", [128, 128]))
    gmix = es.enter_context(sb("gmix", [128, 4, 8]))
    gffn = es.enter_context(sb("gffn", [128, 4, 8]))
    gfin = es.enter_context(sb("gfin", [128, 8]))
    CONST = T("const")
    for t_, nm in ((ones_f, "k_ones"), (ident_f, "k_ident"), (gmix, "gmix"), (gffn, "gffn"), (gfin, "gfin")):
        tt_ = T(nm)
        P.dma("sp", t_[:], dr[nm], tt_, w=[tt_], perm=True)
        CONST.w.update(tt_.w)

    def mm(out, lhsT, rhs, start, stop):
        return lambda e: e.matmul(out, lhsT, rhs, start=start, stop=stop, skip_group_check=True)

    def phase_resid(mode, w_dram, kc, o_dram, OT, bias_ap, g_ap, final, w_pre=None):
        P.push()
        with contextlib.ExitStack() as ph:
            xt = [ph.enter_context(sb("xt%d" % i, [128, 8, TT])) for i in range(2)]
            XT = [[T("xt%d_%d" % (i, m)) for m in range(8)] for i in range(2)]
            sq = [ph.enter_context(sb("sq%d" % i, [128, TT], BF16)) for i in range(2)]
            SQ = [T("sq%d" % i) for i in range(2)]
            rt = ph.enter_context(sb("rt", [128, TT]))
            RT = T("rt")
            rstd = [ph.enter_context(sb("rstd%d" % i, [128, TT])) for i in range(2)]
            RSTD = [T("rstd%d" % i) for i in range(2)]
            pend_tiles = []
            if mode != "init":
                wt, WT = w_pre
                nob = 2
                ot = [ph.enter_context(sb("ot%d" % i, [128, kc, TT], BF16)) for i in range(nob)]
                OTL = [T("ot%d" % i) for i in range(nob)]
            for tt in range(NT):
                ts = slice(tt * TT, (tt + 1) * TT)
                b = tt % 2
                if mode == "init":
                    P.dma("sp", xt[b][:], x_d[:, :, ts].rearrange("c p s -> p c s"), XT[b][0], r=[XIN], w=XT[b])
                else:
                    ob = tt % nob
                    if not (os.environ.get("NOOT") and tt > 1):
                        P.dma("sp", ot[ob][:], o_dram[:, :, ts].rearrange("c p s -> p c s"), OTL[ob], r=[OT[tt]], w=[OTL[ob]])
                    P.dma("sp", xt[b][:], xr_d[:, :, ts].rearrange("c p s -> p c s"), XT[b][0], r=[XR[tt]], w=XT[b])
                def stat(m):
                    sb_ = m % 2
                    P.op("pe", mm(ps[4][:], ones_b[:], sq[sb_][:], m == 0, m == 7), r=[SQ[sb_], CONST], w=[PS[4]])

                for m in range(8):
                    if mode != "init":
                        pb_ = m % 4
                        for k in range(kc):
                            P.op("pe", mm(ps[pb_][:], wt[:, k, m * 128:(m + 1) * 128], ot[ob][:, k, :], k == 0, k == kc - 1),
                                 r=[WT, OTL[ob]], w=[PS[pb_]])
                        bsc = bias_ap[:, m:m + 1] if bias_ap is not None else 0.0
                        P.op("dve", lambda e, m=m, pb_=pb_, bsc=bsc: e.scalar_tensor_tensor(
                            out=xt[b][:, m, :], in0=ps[pb_][:], scalar=bsc, in1=xt[b][:, m, :], op0=ALU.add, op1=ALU.add),
                            r=[PS[pb_], CONST], w=[XT[b][m]])
                    sb_ = m % 2
                    P.op("act", lambda e, m=m, sb_=sb_: e.activation(out=sq[sb_][:], in_=xt[b][:, m, :], func=AF.Square),
                         r=[XT[b][m]], w=[SQ[sb_]])
                    if pend_tiles:
                        pend_tiles[-1](m)
                    if m >= 1:
                        stat(m - 1)
                stat(7)
                P.op("act", lambda e: e.activation(out=rt[:], in_=ps[4][:], func=AF.Sqrt, scale=1.0 / D, bias=eps_t[:, 0:1]),
                     r=[PS[4], CONST], w=[RT])
                P.op("dve", lambda e, b=b: e.reciprocal(out=rstd[b][:], in_=rt[:]), r=[RT], w=[RSTD[b]])
                if not final:
                    P.dma("pool", xr_d[:, :, ts].rearrange("c p s -> p c s"), xt[b][:], XT[b][0], r=XT[b], w=[XR[tt]])

                def hop(m, b=b, ts=ts, tt=tt):
                    if not final:
                        P.op("dve", lambda e: e.scalar_tensor_tensor(
                            out=hT[:, m, ts], in0=xt[b][:, m, :], scalar=g_ap[:, m:m + 1], in1=rstd[b][:], op0=ALU.mult, op1=ALU.mult),
                            r=[XT[b][m], RSTD[b], CONST], w=[HT[m][tt]])
                    else:
                        P.op("dve", lambda e: e.scalar_tensor_tensor(
                            out=xt[b][:, m, :], in0=xt[b][:, m, :], scalar=g_ap[:, m:m + 1], in1=rstd[b][:], op0=ALU.mult, op1=ALU.mult),
                            r=[XT[b][m], RSTD[b], CONST], w=[XT[b][m]])
                        if m == 7:
                            P.dma("pool", out_d[:, :, ts].rearrange("c p s -> p c s"), xt[b][:], XT[b][0], r=XT[b], w=[OUT])
                pend_tiles.append(hop)
            for m in range(8):
                pend_tiles[-1](m)
            P.barrier()
        P.pop()

    ones_b = es.enter_context(sb("ones_b", [128, 128], BF16))
    P.op("dve", lambda e: e.memset(ones_b[:], 1.0), w=[CONST])
    eps_t = es.enter_context(sb("eps_t", [128, 1]))
    one_t = es.enter_context(sb("one_t", [128, 1]))
    P.op("dve", lambda e: e.memset(eps_t[:], EPS), w=[CONST])
    P.op("dve", lambda e: e.memset(one_t[:], 1.0), w=[CONST])

    def phase_ffn1(l, prefetch=None):
        P.push()
        with contextlib.ExitStack() as ph:
            cw = ph.enter_context(sb("fcw", [128, 44, 3]))
            cb = ph.enter_context(sb("fcb", [128, 44]))
            CW = T("fcw")
            P.dma("sp", cw[:], dr["fcw"][:, l], CW, w=[CW])
            P.dma("sp", cb[:], dr["fcb"][:, l], CW, w=[CW])
            wg = [ph.enter_context(sb("wg%d" % i, [128, 2, 8, 128], BF16)) for i in range(2)]
            WG = [T("wg%d" % i) for i in range(2)]
            U = [[ph.enter_context(sb("u%d_%d" % (k, i), [128, TT + 2])) for i in range(2)] for k in range(2)]
            UT = [[T("u%d_%d" % (k, i)) for i in range(2)] for k in range(2)]
            t1 = [ph.enter_context(sb("t1_%d" % k, [128, TT])) for k in range(2)]
            T1 = [T("t1_%d" % k) for k in range(2)]
            y = [ph.enter_context(sb("y_%d" % k, [128, TT])) for k in range(2)]
            Y = [T("y_%d" % k) for k in range(2)]
            sg = ph.enter_context(sb("sg", [128, TT]))
            SG = T("sg")
            ao = [ph.enter_context(sb("ao%d" % i, [128, TT], BF16)) for i in range(3)]
            AO = [T("ao%d" % i) for i in range(3)]

            def loadw(c):
                b = c % 2
                P.dma("pool", wg[b][:, 0], dr["wup"][l, c], WG[b], w=[WG[b]])
                P.dma("pool", wg[b][:, 1], dr["wup"][l, NPAIR + c], WG[b], w=[WG[b]])

            loadw(0)
            it = 0
            for c in range(NPAIR):
                if c + 1 < NPAIR:
                    loadw(c + 1)
                if c == 0 and prefetch:
                    prefetch()
                wb = c % 2
                for tt in range(NT):
                    ts = slice(tt * TT, (tt + 1) * TT)
                    ub = tt % 2
                    for k in range(2):
                        pb_ = (2 * it + k) % 4
                        j = c + k * NPAIR
                        for kk in range(8):
                            P.op("pe", mm(ps[pb_][:], wg[wb][:, k, kk, :], hT[:, kk, ts], kk == 0, kk == 7),
                                 r=[WG[wb], HT[kk][tt]], w=[PS[pb_]])
                        u = U[k][ub]
                        if tt == 0:
                            P.op("pool", lambda e, u=u: e.memset(u[:, 0:2], 0.0), w=[UT[k][ub]])
                        else:
                            pu = U[k][1 - ub]
                            P.op("pool", lambda e, u=u, pu=pu: e.tensor_copy(out=u[:, 0:2], in_=pu[:, TT:TT + 2]),
                                 r=[UT[k][1 - ub]], w=[UT[k][ub]])
                        P.op("act", lambda e, u=u, pb_=pb_: e.activation(out=u[:, 2:TT + 2], in_=ps[pb_][:], func=AF.Identity),
                             r=[PS[pb_]], w=[UT[k][ub]])
                        P.op("act", lambda e, pb_=pb_, j=j, k=k: e.activation(out=t1[k][:], in_=ps[pb_][:], func=AF.Identity,
                                                                            scale=cw[:, j, 2:3], bias=cb[:, j:j + 1]),
                             r=[PS[pb_], CW], w=[T1[k]])
                        P.op("dve", lambda e, u=u, j=j, k=k: e.scalar_tensor_tensor(
                            out=t1[k][:], in0=u[:, 1:TT + 1], scalar=cw[:, j, 1:2], in1=t1[k][:], op0=ALU.mult, op1=ALU.add),
                            r=[UT[k][ub], CW], w=[T1[k]])
                        P.op("dve", lambda e, u=u, j=j, k=k: e.scalar_tensor_tensor(
                            out=y[k][:], in0=u[:, 0:TT], scalar=cw[:, j, 0:1], in1=t1[k][:], op0=ALU.mult, op1=ALU.add),
                            r=[UT[k][ub], T1[k], CW], w=[Y[k]])
                    P.op("act", lambda e: e.activation(out=sg[:], in_=y[0][:], func=AF.Silu), r=[Y[0]], w=[SG])
                    ab = it % 3
                    P.op("dve", lambda e, ab=ab: e.tensor_tensor(out=ao[ab][:], in0=sg[:], in1=y[1][:], op=ALU.mult),
                         r=[SG, Y[1]], w=[AO[ab]])
                    P.dma("sp", as_d[c, :, ts], ao[ab][:], AO[ab], r=[AO[ab]], w=[AS[tt]])
                    it += 1
            P.barrier()
        P.pop()

    def phase_odd(jl, prefetch=None):
        P.push()
        with contextlib.ExitStack() as ph:
            def vt(name, shape):
                return ph.enter_context(sb(name, shape))
            bin_ = vt("cbin", [128, 16]); ccw = vt("ccw", [128, 8, 4]); ccb = vt("ccb", [128, 8])
            cba = vt("cba", [128, 8]); cbi = vt("cbi", [128, 8]); lam = vt("clam", [128, 8]); cl = vt("cl", [128, 8]); bhy = vt("bhy", [128, 8])
            CV = T("cvec")
            for t_, nm in ((bin_, "cbin"), (ccw, "ccw"), (ccb, "ccb"), (cba, "cba"), (cbi, "cbi"), (lam, "clam")):
                P.dma("sp", t_[:], dr[nm][:, jl], CV, w=[CV])
            P.op("act", lambda e: e.activation(out=cl[:], in_=lam[:], func=AF.Exp, scale=-1.0), r=[CV], w=[CV], n=8)
            P.op("act", lambda e: e.activation(out=cl[:], in_=cl[:], func=AF.Ln, bias=one_t[:, 0:1]), r=[CV], w=[CV], n=8)
            P.op("dve", lambda e: e.tensor_scalar(out=cl[:], in0=cl[:], scalar1=-4.0, scalar2=None, op0=ALU.mult), r=[CV], w=[CV], n=8)
            P.op("dve", lambda e: e.tensor_scalar(out=cba[:], in0=cba[:], scalar1=0.5, scalar2=None, op0=ALU.mult), r=[CV], w=[CV], n=8)
            P.op("dve", lambda e: e.tensor_scalar(out=cbi[:], in0=cbi[:], scalar1=0.5, scalar2=None, op0=ALU.mult), r=[CV], w=[CV], n=8)
            P.op("dve", lambda e: e.tensor_scalar(out=bhy[:], in0=bin_[:, 0:8], scalar1=0.5, scalar2=None, op0=ALU.mult), r=[CV], w=[CV], n=8)
            win = [ph.enter_context(sb("cwin%d" % i, [128, 8, 256], BF16)) for i in range(2)]
            WIN = [T("cwin%d" % i) for i in range(2)]
            wai = [ph.enter_context(sb("cwai%d" % i, [128, 2, 128])) for i in range(2)]
            WAI = [T("cwai%d" % i) for i in range(2)]
            U = [vt("cu%d" % i, [128, TT + 3]) for i in range(2)]
            UT = [T("cu%d" % i) for i in range(2)]
            xs = [vt("xs%d" % i, [128, TT]) for i in range(2)]; XS = [T("xs%d" % i) for i in range(2)]
            x2 = vt("x2", [128, TT]); X2 = T("x2")
            sgm = [vt("sgm%d" % i, [128, TT]) for i in range(2)]; SGM = [T("sgm%d" % i) for i in range(2)]
            uu = [vt("uu%d" % i, [128, TT]) for i in range(2)]; UU = [T("uu%d" % i) for i in range(2)]
            rr = vt("rr", [128, TT]); RR = T("rr")
            ig = vt("ig", [128, TT]); IG = T("ig")
            aa = vt("aa", [128, TT]); AA = T("aa")
            mmul = vt("mmul", [128, TT]); MM = T("mmul")
            hs = [vt("hs%d" % i, [128, TT]) for i in range(2)]
            HS = [T("hs%d" % i) for i in range(2)]
            oo = [ph.enter_context(sb("oo%d" % i, [128, TT], BF16)) for i in range(3)]
            OO = [T("oo%d" % i) for i in range(3)]

            def loadw(n):
                b = n % 2
                P.dma("pool", win[b][:], dr["cwin"][jl, n], WIN[b], w=[WIN[b]])
                P.dma("sp", wai[b][:, 0, :], dr["cwa"][jl, n], WAI[b], w=[WAI[b]])
                P.dma("sp", wai[b][:, 1, :], dr["cwi"][jl, n], WAI[b], w=[WAI[b]])

            def stage1(it, n, tt):
                ts = slice(tt * TT, (tt + 1) * TT)
                wb = n % 2
                ub = tt % 2
                db = it % 2
                py, pu_ = (2 * it) % 4, (2 * it + 1) % 4
                for k, pb_ in ((0, py), (1, pu_)):
                    for kk in range(8):
                        P.op("pe", mm(ps[pb_][:], win[wb][:, kk, k * 128:(k + 1) * 128], hT[:, kk, ts], kk == 0, kk == 7),
                             r=[WIN[wb], HT[kk][tt]], w=[PS[pb_]])
                P.op("act", lambda e: e.activation(out=xs[db][:], in_=ps[py][:], func=AF.Identity, scale=0.5, bias=bhy[:, n:n + 1]),
                     r=[PS[py], CV], w=[XS[db]])
                P.op("act", lambda e: e.activation(out=x2[:], in_=ps[py][:], func=AF.Square, bias=bin_[:, n:n + 1]),
                     r=[PS[py], CV], w=[X2])
                u = U[ub]
                if tt == 0:
                    P.op("pool", lambda e: e.memset(u[:, 0:3], 0.0), w=[UT[ub]])
                else:
                    pu = U[1 - ub]
                    P.op("pool", lambda e: e.tensor_copy(out=u[:, 0:3], in_=pu[:, TT:TT + 3]), r=[UT[1 - ub]], w=[UT[ub]])
                P.op("act", lambda e: e.activation(out=u[:, 3:TT + 3], in_=ps[pu_][:], func=AF.Identity, bias=bin_[:, 8 + n:9 + n]),
                     r=[PS[pu_], CV], w=[UT[ub]])
                P.op("dve", lambda e: e.tensor_scalar(out=x2[:], in0=x2[:], scalar1=0.044715, scalar2=1.0, op0=ALU.mult, op1=ALU.add),
                     r=[X2], w=[X2])
                P.op("pool", lambda e: e.tensor_tensor(out=x2[:], in0=x2[:], in1=xs[db][:], op=ALU.mult), r=[X2, XS[db]], w=[X2])
                P.op("dve", lambda e: e.tensor_scalar(out=uu[db][:], in0=u[:, 3:TT + 3], scalar1=ccw[:, n, 3:4], scalar2=ccb[:, n:n + 1],
                                                      op0=ALU.mult, op1=ALU.add), r=[UT[ub], CV], w=[UU[db]])
                for j in range(3):
                    P.op("dve", lambda e, j=j: e.scalar_tensor_tensor(out=uu[db][:], in0=u[:, j:TT + j], scalar=ccw[:, n, j:j + 1], in1=uu[db][:],
                                                                      op0=ALU.mult, op1=ALU.add), r=[UT[ub], CV], w=[UU[db]])
                P.op("act", lambda e: e.activation(out=sgm[db][:], in_=x2[:], func=AF.Tanh, scale=1.5957691216), r=[X2], w=[SGM[db]])

            def stage2(it, n, tt):
                ts = slice(tt * TT, (tt + 1) * TT)
                wb = n % 2
                db = it % 2
                pr, pi_ = 4 + (2 * it) % 4, 4 + (2 * it + 1) % 4
                P.op("pe", mm(ps[pr][:], wai[wb][:, 0, :], uu[db][:], True, True), r=[WAI[wb], UU[db]], w=[PS[pr]])
                P.op("pe", mm(ps[pi_][:], wai[wb][:, 1, :], uu[db][:], True, True), r=[WAI[wb], UU[db]], w=[PS[pi_]])
                P.op("act", lambda e: e.activation(out=rr[:], in_=ps[pr][:], func=AF.Tanh, scale=0.5, bias=cba[:, n:n + 1]), r=[PS[pr], CV], w=[RR])
                P.op("act", lambda e: e.activation(out=ig[:], in_=ps[pi_][:], func=AF.Tanh, scale=0.5, bias=cbi[:, n:n + 1]), r=[PS[pi_], CV], w=[IG])
                P.op("act", lambda e: e.activation(out=aa[:], in_=rr[:], func=AF.Exp, scale=cl[:, n:n + 1], bias=cl[:, n:n + 1]), r=[RR, CV], w=[AA])
                P.op("act", lambda e: e.activation(out=mmul[:], in_=aa[:], func=AF.Square), r=[AA], w=[MM])
                P.op("act", lambda e: e.activation(out=mmul[:], in_=mmul[:], func=AF.Sqrt, scale=-1.0, bias=one_t[:, 0:1]), r=[MM], w=[MM])
                P.op("dve", lambda e: e.scalar_tensor_tensor(out=ig[:], in0=ig[:], scalar=1.0, in1=uu[db][:], op0=ALU.add, op1=ALU.mult), r=[IG, UU[db]], w=[IG])
                P.op("dve", lambda e: e.scalar_tensor_tensor(out=ig[:], in0=ig[:], scalar=0.5, in1=mmul[:], op0=ALU.mult, op1=ALU.mult), r=[IG, MM], w=[IG])
                hb = tt % 2
                init = 0.0 if tt == 0 else hs[1 - hb][:, TT - 1:TT]
                rl = [AA, IG] + ([HS[1 - hb]] if tt else [])
                P.op("dve", lambda e: e.tensor_tensor_scan(out=hs[hb][:], data0=aa[:], data1=ig[:], initial=init,
                                                           op0=ALU.mult, op1=ALU.add), r=rl, w=[HS[hb]])
                P.op("pool", lambda e: e.tensor_tensor(out=xs[db][:], in0=xs[db][:], in1=hs[hb][:], op=ALU.mult), r=[XS[db], HS[hb]], w=[XS[db]])
                ob = it % 3
                P.op("dve", lambda e: e.scalar_tensor_tensor(out=oo[ob][:], in0=sgm[db][:], scalar=1.0, in1=xs[db][:], op0=ALU.add, op1=ALU.mult),
                     r=[XS[db], SGM[db]], w=[OO[ob]])
                P.dma("sp", os_d[n, :, ts], oo[ob][:], OO[ob], r=[OO[ob]], w=[OS[tt]])

            loadw(0)
            if prefetch:
                prefetch()
            its = [(n, tt) for n in range(8) for tt in range(NT)]
            for it, (n, tt) in enumerate(its):
                stage1(it, n, tt)
                if it >= 1:
                    stage2(it - 1, *its[it - 1])
                if tt == 0 and n + 1 < 8:
                    loadw(n + 1)
            stage2(len(its) - 1, *its[-1])
            P.barrier()
        P.pop()

    def dump(idx, ap, n, TL):
        P.push()
        with contextlib.ExitStack() as dd:
            stg = dd.enter_context(sb("stg", [128, n]))
            STG = T("stg")
            P.op("dve", lambda e: e.tensor_copy(out=stg[:, 0:n], in_=ap), r=TL, w=[STG])
            P.dma("sp", out_d[idx, :, 0:n], stg[:, 0:n], STG, r=[STG], w=[OUT])
            P.barrier()
        P.pop()

    def phase_even(jl, dbg=None, prefetch=None):
        P.push()
        try:
            _phase_even(jl, dbg, prefetch)
        finally:
            P.pop()

    def _phase_even(jl, dbg=None, prefetch=None):
        rot = {"s": 0}

        def sbank():
            rot["s"] = (rot["s"] + 1) % 3
            return rot["s"]

        with contextlib.ExitStack() as ph:
            def vt(name, shape, dt=F32):
                return ph.enter_context(sb(name, shape, dt))
            psw = vt("psw", [128, 128]); PSW = T("psw")
            P.dma("sp", psw[:], dr["k_psw"], PSW, w=[PSW])
            kcP = vt("kcP", [128, 2, 256], BF16); KCP = T("kcP")
            rc = vt("rc", [128, 2, 2, 129], BF16); RC = T("rc")
            P.op("dve", lambda e: e.memset(kcP[:], 0.0), w=[KCP])
            P.op("dve", lambda e: e.memset(rc[:], 0.0), w=[RC])
            for h in range(2):
                P.dma("pool", rc[:, h, :, 64:129], dr["k_ovc"], RC, r=[], w=[RC])
            cs = [vt("cs%d" % i, [128, 2, TT]) for i in range(2)]
            CS = [T("cs%d" % i) for i in range(2)]
            wq = [vt("wq%d" % i, [128, 8, 128], BF16) for i in range(3)]
            WQ = [T("wq%d" % i) for i in range(3)]
            zt = [vt("zt%d" % i, [128, TT]) for i in range(2)]
            ZT = [T("zt%d" % i) for i in range(2)]
            r1 = vt("r1", [128, TT]); R1 = T("r1")
            r2 = vt("r2", [128, TT]); R2 = T("r2")
            wcnt = {"n": 0, "it": 0}

            def inproj(chunk_ids, post, group=1):
                ng = len(chunk_ids) // group

                def loadw(gi):
                    bl = []
                    for k in range(group):
                        b = wcnt["n"] % 3
                        wcnt["n"] += 1
                        P.dma("pool", wq[b][:], dr["awin"][jl, chunk_ids[gi * group + k]], WQ[b], w=[WQ[b]])
                        bl.append(b)
                    return bl
                cur = loadw(0)
                for gi in range(ng):
                    nxt = loadw(gi + 1) if (gi + 1 < ng and group == 1) else None
                    for tt in range(NT):
                        ts = slice(tt * TT, (tt + 1) * TT)
                        banks = []
                        for k in range(group):
                            pb_ = (wcnt["it"]) % 6 if group > 1 else sbank()
                            wcnt["it"] += 1
                            for kk in range(8):
                                P.op("pe", mm(ps[pb_][:], wq[cur[k]][:, kk, :], hT[:, kk, ts], kk == 0, kk == 7),
                                     r=[WQ[cur[k]], HT[kk][tt]], w=[PS[pb_]])
                            banks.append(pb_)
                        post(chunk_ids[gi * group:(gi + 1) * group], tt, banks)
                    if group > 1 and gi + 1 < ng:
                        nxt = loadw(gi + 1)
                    cur = nxt

            ropec = {"n": 0}

            def rope_post(dst):
                def post(cil, tt, banks):
                    ts = slice(tt * TT, (tt + 1) * TT)
                    pb_ = banks[0]
                    n = ropec["n"]
                    ropec["n"] += 1
                    cb_ = n % 2
                    P.dma("pool", cs[cb_][:, 0, :], dr["k_cos"][:, ts], CS[cb_], w=[CS[cb_]])
                    P.dma("pool", cs[cb_][:, 1, :], dr["k_sin"][:, ts], CS[cb_], w=[CS[cb_]])
                    z = zt[cb_]
                    P.op("act", lambda e: e.activation(out=z[:], in_=ps[pb_][:], func=AF.Identity), r=[PS[pb_]], w=[ZT[cb_]])
                    sw = 6 + cb_
                    P.op("pe", mm(ps[sw][:], psw[:], z[:], True, True), r=[PSW, ZT[cb_]], w=[PS[sw]])
                    P.op("pool", lambda e: e.tensor_tensor(out=r1[:], in0=z[:], in1=cs[cb_][:, 0, :], op=ALU.mult), r=[ZT[cb_], CS[cb_]], w=[R1])
                    P.op("dve", lambda e: e.tensor_tensor(out=r2[:], in0=ps[sw][:], in1=cs[cb_][:, 1, :], op=ALU.mult), r=[PS[sw], CS[cb_]], w=[R2])
                    dst(cil[0], tt, ts)
                return post

            P.push()
            with contextlib.ExitStack() as st:
                st.enter_context(nc.named_scope("evA%d" % jl))
                def va(name, shape, dt=F32):
                    return st.enter_context(sb(name, shape, dt))
                kin = va("kin", [128, 2, 2, S], BF16)
                KIN = [[T("kin%d_%d" % (kv, t)) for t in range(NT)] for kv in range(2)]
                for kv in range(2):
                    P.op("dve", lambda e, kv=kv: e.memset(kin[:, kv], 0.0), w=KIN[kv])
                w1t = [va("w1t%d" % kv, [128, 32, 128], BF16) for kv in range(2)]
                W1T = [T("w1t%d" % kv) for kv in range(2)]
                pet = va("pet", [128, 2, 32], BF16); PET = T("pet")
                w2p = va("w2p", [128, 2, 128], BF16); W2P = T("w2p")
                w2v = va("w2v", [128, 64], BF16); W2V = T("w2v")
                P.op("dve", lambda e: e.memset(w2p[:], 0.0), w=[W2P])
                for kv in range(2):
                    P.dma("pool", w1t[kv][:], dr["aw1"][jl, kv], W1T[kv], w=[W1T[kv]], max_dma_last_dim=4096)
                    P.dma("pool", pet[:, kv, :], dr["ape"][jl, kv], PET, w=[PET])
                for h in range(2):
                    P.dma("pool", w2p[:, h, 64 * h:64 * h + 64], dr["aw2"][jl, 0], W2P, w=[W2P])
                P.dma("pool", w2v[:], dr["aw2"][jl, 1], W2V, w=[W2V])

                def dst_kc(ci, tt, ts):
                    for h in range(2):
                        hs_ = slice(64 * h, 64 * h + 64)
                        P.op("dve", lambda e, h=h, hs_=hs_: e.tensor_tensor(out=kin[hs_, 0, h, ts], in0=r1[hs_, :], in1=r2[hs_, :], op=ALU.add),
                             r=[R1, R2], w=[KIN[0][tt]])
                inproj([4], rope_post(dst_kc))

                def post_vc(cil, tt, banks):
                    ts = slice(tt * TT, (tt + 1) * TT)
                    for h in range(2):
                        hs_ = slice(64 * h, 64 * h + 64)
                        P.op("act", lambda e, h=h, hs_=hs_: e.activation(out=kin[hs_, 1, h, ts], in_=ps[banks[0]][hs_, :], func=AF.Identity),
                             r=[PS[banks[0]]], w=[KIN[1][tt]])
                inproj([5], post_vc)

                hx = va("hx", [128, 256]); HX = T("hx")
                h2 = va("h2", [128, 256]); H2 = T("h2")
                hg = va("hg", [128, 256]); HG = T("hg")
                hidb = va("hidb", [128, 256], BF16); HB = T("hidb")
                P.op("dve", lambda e: e.memset(hidb[:], 0.0), w=[HB])
                for kv in range(2):
                    for h in range(2):
                        pb_ = sbank()
                        for l in range(32):
                            P.op("pe", mm(ps[pb_][:, 0:255], w1t[kv][:, l, :], kin[:, kv, h, l:l + 4065:16], l == 0, False),
                                 r=[W1T[kv]] + KIN[kv], w=[PS[pb_]])
                        for l in range(32):
                            P.op("pe", mm(ps[pb_][:, 0:255], w1t[kv][:, l, :], pet[:, kv, l:l + 1].to_broadcast([128, 255]), False, l == 31),
                                 r=[W1T[kv], PET], w=[PS[pb_]])
                        P.op("act", lambda e: e.activation(out=hx[:, 0:255], in_=ps[pb_][:, 0:255], func=AF.Identity), r=[PS[pb_]], w=[HX])
                        P.op("act", lambda e: e.activation(out=h2[:, 0:255], in_=ps[pb_][:, 0:255], func=AF.Square), r=[PS[pb_]], w=[H2])
                        P.op("dve", lambda e: e.tensor_scalar(out=h2[:, 0:255], in0=h2[:, 0:255], scalar1=0.044715, scalar2=1.0, op0=ALU.mult, op1=ALU.add),
                             r=[H2], w=[H2])
                        P.op("dve", lambda e: e.tensor_tensor(out=h2[:, 0:255], in0=h2[:, 0:255], in1=hx[:, 0:255], op=ALU.mult), r=[H2, HX], w=[H2])
                        P.op("act", lambda e: e.activation(out=hg[:, 0:255], in_=h2[:, 0:255], func=AF.Sigmoid, scale=1.5957691216), r=[H2], w=[HG])
                        P.op("dve", lambda e: e.tensor_tensor(out=hidb[:, 0:255], in0=hx[:, 0:255], in1=hg[:, 0:255], op=ALU.mult), r=[HX, HG], w=[HB])
                        if kv == 0:
                            P.op("pe", mm(ps[3][:, 0:255], w2p[:, h, :], hidb[:, 0:255], True, True), r=[W2P, HB], w=[PS[3]])
                            P.op("act", lambda e, h=h: e.activation(out=kcP[:, h, 0:255], in_=ps[3][:, 0:255], func=AF.Identity), r=[PS[3]], w=[KCP])
                        else:
                            for jc in range(2):
                                P.op("pe", mm(ps[3][:, jc * 64:(jc + 1) * 64], hidb[:, jc * 128:(jc + 1) * 128], w2v[:], True, True),
                                     r=[W2V, HB], w=[PS[3]])
                            P.op("act", lambda e, h=h: e.activation(out=rc[:, h, :, 0:64], in_=ps[3][:, 0:128].rearrange("p (a b) -> p a b", a=2),
                                                                    func=AF.Identity), r=[PS[3]], w=[RC])
                if dbg == "A":
                    dump(0, kcP[:].rearrange("p a b -> p (a b)"), 512, [KCP])
                    dump(1, rc[:].rearrange("p a b c -> p (a b c)"), 516, [RC])
                    dump(2, kin[:, 0, 0, :], 4096, KIN[0])
                    dump(3, kin[:, 1, 1, :], 4096, KIN[1])
                P.barrier()
            P.pop()
            if dbg == "A":
                return

            scB = nc.named_scope("evB%d" % jl)
            scB.__enter__()
            if prefetch:
                prefetch()
            ksP = vt("ksP", [128, 2, S], BF16); KS = [T("ks%d" % t) for t in range(NT)]
            kwP = vt("kwP", [128, 2, S], BF16); KW = [T("kw%d" % t) for t in range(NT)]
            P.op("dve", lambda e: e.memset(ksP[:], 0.0), w=KS)
            P.op("dve", lambda e: e.memset(kwP[:], 0.0), w=KW)
            vtok = vt("vtok", [128, 32, 2, 2, 65], BF16); VT = [T("vt%d" % t) for t in range(NT)]
            P.op("dve", lambda e: e.memset(vtok[:], 1.0), w=VT)
            gsig = vt("gsig", [128, 32, 24]); GS = [T("gs%d" % t) for t in range(NT)]
            expb = vt("expb", [128, S], BF16); tri = vt("tri", [128, 2, 128], BF16)
            shw = vt("shw", [128, 512], BF16); pat = vt("pat", [128, 128], BF16)
            identb = vt("identb", [128, 128], BF16)
            pa = vt("pa", [128, 128]); pbt = vt("pbt", [128, 128])
            AC = T("attconst")
            for t_, nm, q_ in ((expb, "k_expb", "pool"), (tri, "k_tri", "pool"), (shw, "k_shw", "pool"), (pat, "k_pat", "pool"),
                               (identb, "k_ident", "pool"), (pa, "k_pa", "sp"), (pbt, "k_pb", "sp")):
                tq_ = T(nm)
                P.dma(q_, t_[:], dr[nm], tq_, w=[tq_], max_dma_last_dim=4096)
                AC.w.update(tq_.w)
            wv = vt("wv", [128, 8, 280], BF16); WV = T("wv")
            P.dma("pool", wv[:], dr["awv"][jl], WV, w=[WV], max_dma_last_dim=4096)

            def dst_k(buf, TL):
                def dst(ci, tt, ts):
                    for h in range(2):
                        hs_ = slice(64 * h, 64 * h + 64)
                        P.op("dve", lambda e, h=h, hs_=hs_: e.tensor_tensor(out=buf[hs_, h, ts], in0=r1[hs_, :], in1=r2[hs_, :], op=ALU.add),
                             r=[R1, R2], w=[TL[tt]])
                return dst
            inproj([6], rope_post(dst_k(ksP, KS)))
            inproj([7], rope_post(dst_k(kwP, KW)))
            qo = [vt("qo%d" % i, [128, TT], BF16) for i in range(2)]
            QO = [T("qo%d" % i) for i in range(2)]
            qc = {"n": 0}

            def dst_q(ci, tt, ts):
                b = qc["n"] % 2
                qc["n"] += 1
                P.op("dve", lambda e: e.tensor_tensor(out=qo[b][:], in0=r1[:], in1=r2[:], op=ALU.add), r=[R1, R2], w=[QO[b]])
                P.dma("sp", qs_d[ci, :, ts], qo[b][:], QO[b], r=[QO[b]], w=[QS[tt]])
            inproj([0, 1, 2, 3], rope_post(dst_q))
            for tq in range(32):
                pb_ = sbank()
                for kk in range(8):
                    P.op("pe", mm(ps[pb_][:, 0:280], hT[:, kk, tq * 128:(tq + 1) * 128], wv[:, kk, :], kk == 0, kk == 7),
                         r=[WV, HT[kk][tq // 4]], w=[PS[pb_]])
                P.op("act", lambda e, tq=tq, pb_=pb_: e.activation(out=vtok[:, tq, :, :, 0:64],
                                                                  in_=ps[pb_][:, 0:256].rearrange("p (a b c) -> p a b c", a=2, b=2), func=AF.Identity),
                     r=[PS[pb_]], w=[VT[tq // 4]])
                P.op("act", lambda e, tq=tq, pb_=pb_: e.activation(out=gsig[:, tq, :], in_=ps[pb_][:, 256:280], func=AF.Sigmoid),
                     r=[PS[pb_]], w=[GS[tq // 4]])
            acw = vt("acw", [128, 4, 3]); ACW = T("acw")
            P.dma("sp", acw[:], dr["acw"][:, jl], ACW, w=[ACW])
            xg = vt("xg", [128, TT]); XG = T("xg")
            Ub = [vt("ub%d" % i, [128, TT + 2]) for i in range(2)]
            UB = [T("ub%d" % i) for i in range(2)]
            yb = vt("yb", [128, TT]); YB = T("yb")
            obo = [vt("obo%d" % i, [128, TT], BF16) for i in range(2)]
            OBO = [T("obo%d" % i) for i in range(2)]
            cc = {"n": 0}

            def post_conv(cil, tt, banks):
                ts = slice(tt * TT, (tt + 1) * TT)
                i = cil[0] - 8
                pbb, pbc, pbx = banks
                ub = tt % 2
                u = Ub[ub]
                P.op("act", lambda e: e.activation(out=xg[:], in_=ps[pbx][:], func=AF.Identity), r=[PS[pbx]], w=[XG])
                if tt == 0:
                    P.op("pool", lambda e: e.memset(u[:, 0:2], 0.0), w=[UB[ub]])
                else:
                    pu = Ub[1 - ub]
                    P.op("pool", lambda e: e.tensor_copy(out=u[:, 0:2], in_=pu[:, TT:TT + 2]), r=[UB[1 - ub]], w=[UB[ub]])
                P.op("dve", lambda e: e.tensor_tensor(out=u[:, 2:TT + 2], in0=ps[pbc][:], in1=xg[:], op=ALU.mult), r=[PS[pbc], XG], w=[UB[ub]])
                P.op("dve", lambda e: e.tensor_scalar(out=yb[:], in0=u[:, 2:TT + 2], scalar1=acw[:, i, 2:3], scalar2=None, op0=ALU.mult),
                     r=[UB[ub], ACW], w=[YB])
                for j in range(2):
                    P.op("dve", lambda e, j=j: e.scalar_tensor_tensor(out=yb[:], in0=u[:, j:TT + j], scalar=acw[:, i, j:j + 1], in1=yb[:],
                                                                      op0=ALU.mult, op1=ALU.add), r=[UB[ub], ACW], w=[YB])
                b = cc["n"] % 2
                cc["n"] += 1
                P.op("dve", lambda e: e.tensor_tensor(out=obo[b][:], in0=ps[pbb][:], in1=yb[:], op=ALU.mult), r=[PS[pbb], YB], w=[OBO[b]])
                P.dma("sp", os_d[4 + i, :, ts], obo[b][:], OBO[b], r=[OBO[b]], w=[OS[tt]])
            inproj([8, 12, 16, 9, 13, 17, 10, 14, 18, 11, 15, 19], post_conv, group=3)

            if dbg == "B":
                dump(0, ksP[:, 0, :], 4096, KS)
                dump(1, kwP[:, 1, :], 4096, KW)
                dump(2, vtok[:, 0:15].rearrange("p a b c d -> p (a b c d)"), 3900, VT)
                dump(3, gsig[:].rearrange("p a b -> p (a b)"), 768, GS)
                return
            scB.__exit__(None, None, None)
            ph.enter_context(nc.named_scope("evC%d" % jl))
            qt_ = [vt("qt%d" % i, [128, 4, 128], BF16) for i in range(2)]
            QT = [T("qt%d" % i) for i in range(2)]
            et = [vt("et%d" % i, [128, 512], BF16) for i in range(3)]
            ET = [T("et%d" % i) for i in range(3)]
            ec = {"n": 0}
            den = vt("den", [128, 12]); rden = vt("rden", [128, 12]); cf = vt("cf", [128, 12])
            DEN = T("den")
            imp = vt("imp", [128, 64]); IMP = T("imp")
            sc = vt("sc", [128, 64]); SC = T("sc")
            m8 = vt("m8", [128, 8]); M8 = T("m8")
            selq = vt("selq", [128, 128]); SELQ = T("selq")
            P.op("dve", n=64, fn=lambda e: e.memset(selq[:], 0.0), w=[SELQ])
            selT = vt("selT", [128, 128], BF16); SELT = T("selT")
            otok = [vt("otok%d" % i, [128, 512]) for i in range(2)]
            OTOK = [T("otok%d" % i) for i in range(2)]
            obt = [vt("obt%d" % i, [128, 4, 128], BF16) for i in range(2)]
            OBT = [T("obt%d" % i) for i in range(2)]

            def bc4(ap):
                return ap.unsqueeze(1).to_broadcast([128, 4, 128])

            def loadq(qt):
                b = qt % 2
                P.dma("sp", qt_[b][:], qs_d[:, :, qt * 128:(qt + 1) * 128].rearrange("c p s -> p c s"), QT[b], r=[QS[qt // 4]], w=[QT[b]])

            def score_exp(kP, KT_, kc, h, qb, masks):
                sb_ = sbank()
                q512 = qt_[qb][:]
                P.op("pe", mm(ps[sb_][:], kP, q512, True, len(masks) == 0), r=KT_ + [QT[qb]], w=[PS[sb_]])
                for mi, (lh, rh, rl) in enumerate(masks):
                    P.op("pe", mm(ps[sb_][:], lh, rh, False, mi == len(masks) - 1), r=[AC] + rl, w=[PS[sb_]])
                eb = ec["n"] % 3
                ec["n"] += 1
                P.op("act", lambda e: e.activation(out=et[eb][:], in_=ps[sb_][:], func=AF.Exp, scale=0.125), r=[PS[sb_]], w=[ET[eb]])
                return eb

            fifo = []

            def push_pv(fn):
                fifo.append(fn)
                while len(fifo) > 2:
                    fifo.pop(0)()

            def flush():
                while fifo:
                    fifo.pop(0)()

            QN = 2 if (dbg and dbg.startswith("C")) else 32

            def combine(qt, h, bank, br):
                c0 = 4 * br
                tq4 = qt // 4
                ot_ = otok[qt % 2]
                OT_ = OTOK[qt % 2]
                if dbg in ("C0", "C1", "C2") and dbg != "C%d" % br:
                    return
                P.op("dve", n=64, fn=lambda e: e.tensor_scalar(out=den[:, c0:c0 + 4], in0=ps[bank][:, 0:260].rearrange("p (a b) -> p a b", a=4)[:, :, 64],
                                                      scalar1=1e-30, scalar2=None, op0=ALU.max), r=[PS[bank]], w=[DEN])
                P.op("dve", n=64, fn=lambda e: e.reciprocal(out=rden[:, c0:c0 + 4], in_=den[:, c0:c0 + 4]), r=[DEN], w=[DEN])
                P.op("dve", n=64, fn=lambda e: e.tensor_tensor(out=cf[:, c0:c0 + 4], in0=rden[:, c0:c0 + 4], in1=gsig[:, qt, h * 12 + br:h * 12 + 12:3], op=ALU.mult),
                     r=[DEN, GS[tq4]], w=[DEN])
                for g in range(4):
                    oc = h * 256 + g * 64
                    P.op("dve", n=64, fn=lambda e, g=g, oc=oc: e.scalar_tensor_tensor(out=ot_[:, oc:oc + 64], in0=ps[bank][:, g * 65:g * 65 + 64],
                                                                            scalar=cf[:, c0 + g:c0 + g + 1], in1=ot_[:, oc:oc + 64],
                                                                            op0=ALU.mult, op1=ALU.add), r=[PS[bank], DEN], w=[OT_])

            def part1(qt, h):
                qb = qt % 2
                tq4 = qt // 4
                ot_ = otok[qt % 2]
                OT_ = OTOK[qt % 2]
                jcs = [0, 1] if qt >= 16 else [0]
                for jc in jcs:
                    st_ = 128 * jc - 8 * qt + 256
                    eb = score_exp(kcP[:, h, jc * 128:(jc + 1) * 128], [KCP], jc, h, qb,
                                   [(shw[:, st_:st_ + 128], bc4(pat[:]), [])])

                    def pv_cmp(eb=eb, jc=jc, h=h, last=(jc == jcs[-1])):
                        for g in range(4):
                            bk = 6 + g // 2
                            reg = slice((g % 2) * 129, (g % 2) * 129 + 129)
                            P.op("pe", mm(ps[bk][:, reg], et[eb][:, g * 128:(g + 1) * 128], rc[:, h, jc, :],
                                          jc == 0 and g % 2 == 0, last), r=[ET[eb], RC], w=[PS[bk]])
                    push_pv(pv_cmp)
                kcs = list(range(max(0, qt - 4), qt + 1))
                for kc in kcs:
                    masks = []
                    if kc == qt - 4:
                        masks.append((identb[:], bc4(tri[:, 1, :]), []))
                    if kc == qt:
                        masks.append((identb[:], bc4(tri[:, 0, :]), []))
                    eb = score_exp(kwP[:, h, kc * 128:(kc + 1) * 128], [KW[kc // 4]], kc, h, qb, masks)

                    def pv_win(eb=eb, kc=kc, h=h, first=(kc == kcs[0]), last=(kc == qt)):
                        for g in range(4):
                            P.op("pe", mm(ps[5][:, g * 65:(g + 1) * 65], et[eb][:, g * 128:(g + 1) * 128], vtok[:, kc, 1, h, :],
                                          first and g == 0, last), r=[ET[eb], VT[kc // 4]], w=[PS[5]])
                    push_pv(pv_win)
                flush()
                for bk in range(2):
                    P.op("dve", n=64, fn=lambda e, bk=bk: e.tensor_scalar(out=den[:, 2 * bk:2 * bk + 2],
                                                                in0=ps[6 + bk][:, 0:258].rearrange("p (a b) -> p a b", a=2)[:, :, 128],
                                                                scalar1=1e-30, scalar2=None, op0=ALU.max), r=[PS[6 + bk]], w=[DEN])
                P.op("dve", n=64, fn=lambda e: e.reciprocal(out=rden[:, 0:4], in_=den[:, 0:4]), r=[DEN], w=[DEN])
                for g in range(4):
                    bk = 6 + g // 2
                    o0 = (g % 2) * 129
                    if g == 0:
                        P.op("dve", n=64, fn=lambda e: e.tensor_scalar(out=imp[:], in0=ps[6][:, 64:128], scalar1=rden[:, 0:1], scalar2=None, op0=ALU.mult),
                             r=[PS[6], DEN], w=[IMP])
                    else:
                        P.op("dve", n=64, fn=lambda e, bk=bk, o0=o0, g=g: e.scalar_tensor_tensor(out=imp[:], in0=ps[bk][:, o0 + 64:o0 + 128], scalar=rden[:, g:g + 1],
                                                                                       in1=imp[:], op0=ALU.mult, op1=ALU.add), r=[PS[bk], DEN], w=[IMP])
                w0 = 63 - 2 * qt
                P.op("dve", n=64, fn=lambda e: e.tensor_tensor(out=sc[:], in0=imp[:], in1=pa[:, w0:w0 + 64], op=ALU.mult), r=[IMP, AC], w=[SC])
                P.op("dve", n=64, fn=lambda e: e.tensor_tensor(out=sc[:], in0=sc[:], in1=pbt[:, w0:w0 + 64], op=ALU.add), r=[SC, AC], w=[SC])
                P.op("dve", n=64, fn=lambda e: e.memset(sc[:, 0:1], 1e4), w=[SC])
                P.op("dve", n=64, fn=lambda e: e.max(out=m8[:], in_=sc[:]), r=[SC], w=[M8])
                P.op("dve", n=64, fn=lambda e: e.tensor_scalar(out=selq[:, 0:64], in0=sc[:], scalar1=m8[:, 7:8], scalar2=1.0, op0=ALU.is_ge, op1=ALU.subtract),
                     r=[SC, M8], w=[SELQ])
                P.op("dve", n=64, fn=lambda e: e.tensor_tensor(out=cf[:, 0:4], in0=rden[:, 0:4], in1=gsig[:, qt, h * 12 + 0:h * 12 + 12:3], op=ALU.mult),
                     r=[DEN, GS[tq4]], w=[DEN])
                for g in range(4):
                    bk = 6 + g // 2
                    o0 = (g % 2) * 129
                    oc = h * 256 + g * 64
                    P.op("dve", n=64, fn=lambda e, bk=bk, o0=o0, g=g, oc=oc: e.tensor_scalar(out=ot_[:, oc:oc + 64], in0=ps[bk][:, o0:o0 + 64],
                                                                                   scalar1=cf[:, g:g + 1], scalar2=None, op0=ALU.mult),
                         r=[PS[bk], DEN], w=[OT_])
                if dbg in ("C1", "C2"):
                    P.op("dve", n=64, fn=lambda e: e.memset(ot_[:, h * 256:h * 256 + 256], 0.0), w=[OT_])
                combine(qt, h, 5, 2)

            def part2(qt, h):
                P.op("pe", n=128, fn=lambda e: e.transpose(out=ps[3][:, 0:128], in_=selq[:], identity=ident_f[:]), r=[SELQ, CONST], w=[PS[3]])
                P.op("dve", n=64, fn=lambda e: e.tensor_copy(out=selT[:], in_=ps[3][:, 0:128]), r=[PS[3]], w=[SELT])

            def part3a(qt, h):
                qb = qt % 2
                for kc in range(qt + 1):
                    masks = [(expb[:, kc * 128:(kc + 1) * 128], bc4(selT[:]), [SELT])]
                    if kc == qt:
                        masks.append((identb[:], bc4(tri[:, 0, :]), []))
                    eb = score_exp(ksP[:, h, kc * 128:(kc + 1) * 128], [KS[kc // 4]], kc, h, qb, masks)

                    def pv_slc(eb=eb, kc=kc, h=h, last=(kc == qt)):
                        for g in range(4):
                            P.op("pe", mm(ps[4][:, g * 65:(g + 1) * 65], et[eb][:, g * 128:(g + 1) * 128], vtok[:, kc, 0, h, :],
                                          kc == 0 and g == 0, last), r=[ET[eb], VT[kc // 4]], w=[PS[4]])
                    push_pv(pv_slc)
                flush()

            def part3b(qt, h):
                combine(qt, h, 4, 1)
                if h == 1:
                    ob_ = qt % 2
                    for c4 in range(4):
                        P.op("pe", lambda e, c4=c4: e.transpose(out=ps[3][:, c4 * 128:(c4 + 1) * 128], in_=otok[ob_][:, c4 * 128:(c4 + 1) * 128], identity=ident_f[:]),
                             r=[OTOK[ob_], CONST], w=[PS[3]])
                    P.op("dve", n=64, fn=lambda e: e.tensor_copy(out=obt[ob_][:], in_=ps[3][:].rearrange("p (a b) -> p a b", a=4)), r=[PS[3]], w=[OBT[ob_]])
                    P.dma("sp", os_d[0:4, :, qt * 128:(qt + 1) * 128].rearrange("c p s -> p c s"), obt[ob_][:], OBT[ob_], r=[OBT[ob_]], w=[OS[qt // 4]])

            iters = [(qt, h) for qt in range(QN) for h in range(2)]
            loadq(0)
            if QN > 1:
                loadq(1)
            part1(*iters[0])
            part2(*iters[0])
            for i, (qt, h) in enumerate(iters):
                nxt = iters[i + 1] if i + 1 < len(iters) else None
                if nxt:
                    part1(*nxt)
                part3a(qt, h)
                if nxt:
                    part2(*nxt)
                part3b(qt, h)
                if h == 1 and qt + 2 < QN:
                    loadq(qt + 2)
            if dbg and dbg.startswith("C"):
                dump(0, otok[0][:], 512, [OTOK[0]])
                dump(1, otok[1][:], 512, [OTOK[1]])
                dump(2, imp[:], 64, [IMP])
                dump(3, sc[:], 64, [SC])
                dump(4, selq[:], 128, [SELQ])
                dump(5, den[:], 12, [DEN])
                dump(6, rden[:], 12, [DEN])
                dump(7, cf[:], 12, [DEN])
            P.barrier()

    def emit_x_out():
        P.push()
        with contextlib.ExitStack() as ph:
            xt = ph.enter_context(sb("xdbg", [128, 8, TT]))
            XT = T("xdbg")
            for tt in range(NT):
                ts = slice(tt * TT, (tt + 1) * TT)
                P.dma("sp", xt[:], xr_d[:, :, ts].rearrange("c p s -> p c s"), XT, r=[XR[tt]], w=[XT])
                P.dma("sp", out_d[:, :, ts].rearrange("c p s -> p c s"), xt[:], XT, r=[XT], w=[OUT])
            P.barrier()
        P.pop()

    def gain(step):
        kind, l = step
        return gffn[:, l, :] if kind == "ffn" else gmix[:, l, :]

    cbos = []
    for jl_ in range(2):
        cbo_ = es.enter_context(sb("cbo%d" % jl_, [128, 8]))
        tb = T("cbo")
        P.dma("sp", cbo_[:], dr["cbout"][:, jl_], tb, w=[tb], perm=True)
        CONST.w.update(tb.w)
        cbos.append(cbo_)
    phase_resid("init", None, 0, None, None, None, gain(plan[0]), False)
    for si, (kind, l) in enumerate(plan):
        jl = l // 2
        is_last = si == len(plan) - 1
        last = is_last and final
        gn = gfin[:, :] if last else (gain(plan[si + 1]) if not is_last else gfin[:, :])
        kc_ = NPAIR if kind == "ffn" else 8
        w_dram = {"even": dr["awout"], "odd": dr["cwout"], "ffn": dr["wdn"]}[kind][jl if kind != "ffn" else l]
        with contextlib.ExitStack() as lay:
            P.push()
            wt = lay.enter_context(sb("wres", [128, kc_, D], BF16))
            WT = T("wres")

            def prefetch(wt=wt, WT=WT, w_dram=w_dram, kc_=kc_):
                half = kc_ // 2
                P.dma("pool", wt[:, :half, :], w_dram[:, :half, :], WT, w=[WT], max_dma_last_dim=4096)
                P.dma("pool", wt[:, half:, :], w_dram[:, half:, :], WT, w=[WT], max_dma_last_dim=4096)
            if kind == "even":
                phase_even(jl, dbg, prefetch)
                if dbg:
                    P.pop()
                    break
                with nc.named_scope("resmix%d" % l):
                    phase_resid("mix", None, 8, os_d, OS, None, gn, last, (wt, WT))
            elif kind == "odd":
                cbo = cbos[jl]
                with nc.named_scope("odd%d" % l):
                    phase_odd(jl, prefetch)
                with nc.named_scope("resmix%d" % l):
                    phase_resid("mix", None, 8, os_d, OS, cbo, gn, last, (wt, WT))
            else:
                with nc.named_scope("ffn%d" % l):
                    phase_ffn1(l, prefetch)
                with nc.named_scope("resffn%d" % l):
                    phase_resid("ffn", None, NPAIR, as_d, AS, None, gn, last, (wt, WT))
            P.pop()
    if not final and not dbg:
        emit_x_out()
    P.barrier()
    es.close()
    return nc


SHAPES = None


def _shapes(shared):
    s = {k: v.shape for k, v in shared.items()}
    s["xT"] = (D, S)
    return s


def kernel(**inputs):
    shared = prep_shared(inputs)
    x = np.asarray(inputs["x"], dtype=np.float32)
    nb = x.shape[0]
    nc = build(_shapes(shared))
    in_maps = []
    for b in range(nb):
        m = dict(shared)
        m["xT"] = _c(x[b].T)
        in_maps.append(m)
    res = run_bass_kernel_spmd(nc, in_maps, core_ids=list(range(nb)))
    out = np.stack([np.asarray(r["out"]).reshape(D, S).T for r in res.results], axis=0)
    return np.ascontiguousarray(out.astype(np.float32))
```

```python
import contextlib
import os
import numpy as np
import concourse.bass as bass
import concourse.mybir as mybir
from concourse.bass_utils import run_bass_kernel_spmd

F32 = mybir.dt.float32
BF16 = mybir.dt.bfloat16
ALU = mybir.AluOpType
AF = mybir.ActivationFunctionType

S = 4096
D = 1024
NT = 8
TT = 512
DFF = 2816
NPAIR = 22
BIG = 30000.0
EPS = 1e-6


class T:
    __slots__ = ("name", "w", "r", "ds", "dram")

    def __init__(self, name="", dram=False):
        self.name = name
        self.w = {}
        self.r = {}
        self.ds = None
        self.dram = dram
        if not dram:
            _SCOPES[-1].append(self)


_SCOPES = [[]]


class Prog:
    def __init__(self, nc, es, n_dsem=72):
        self.nc = nc
        self.E = {"pe": nc.tensor, "dve": nc.vector, "act": nc.scalar, "pool": nc.gpsimd, "sp": nc.sync}
        self.sem = {k: es.enter_context(nc.semaphore("e_" + k)) for k in self.E}
        self.cnt = {k: 0 for k in self.E}
        self.seen = {k: {} for k in self.E}
        self.dpool = [[es.enter_context(nc.semaphore("d%d" % i)), 0] for i in range(n_dsem)]
        self.dperm = set()
        self.dfree = list(range(n_dsem))

    def dsem(self, t, perm=False):
        if t.ds is None:
            t.ds = self.dfree.pop()
            if perm:
                self.dperm.add(t.ds)
        return self.dpool[t.ds]

    def _waits(self, eng, r, w):
        need = {}
        seen = self.seen[eng]

        def add(evs):
            for key, (sem, val, e, small) in evs.items():
                if e == eng and not (small and self.cnt[eng] - val < 4):
                    continue
                if seen.get(key, 0) >= val:
                    continue
                if key not in need or need[key][1] < val:
                    need[key] = (sem, val)

        for t in r:
            add(t.w)
        for t in w:
            if not t.dram:
                add(t.w)
            add(t.r)
        for key, (sem, val) in need.items():
            self.E[eng].wait_ge(sem, val)
            seen[key] = val

    def op(self, eng, fn, r=(), w=(), n=512):
        self._waits(eng, r, w)
        ins = fn(self.E[eng])
        self.cnt[eng] += 1
        ins.then_inc(self.sem[eng], 1)
        key = "e_" + eng
        ev = (self.sem[eng], self.cnt[eng], eng, n < 256)
        for t in w:
            t.w = {key: ev}
            t.r = {}
        for t in r:
            t.r[key] = ev

    def dma(self, q, out, in_, sb, r=(), w=(), perm=False, **kw):
        self._waits(q, r, w)
        d = self.dsem(sb, perm)
        d[1] += 16
        self.E[q].dma_start(out=out, in_=in_, **kw).then_inc(d[0], 16)
        key = "d%d" % sb.ds
        ev = (d[0], d[1], "dma", False)
        for t in w:
            if t.dram:
                if t.r:
                    t.w = {}
                    t.r = {}
                t.w[key] = ev
            else:
                t.w = {key: ev}
                t.r = {}
        for t in r:
            t.r[key] = ev

    def barrier(self):
        sp = self.E["sp"]
        seen = self.seen["sp"]
        for k in self.E:
            if k != "sp" and self.cnt[k] > seen.get("e_" + k, 0):
                sp.wait_ge(self.sem[k], self.cnt[k])
        for i, (s, c) in enumerate(self.dpool):
            if c > seen.get("d%d" % i, 0):
                sp.wait_ge(s, c)
        self.cnt["sp"] += 1
        sp.sem_inc(self.sem["sp"], 1)
        for k in self.E:
            if k != "sp":
                self.E[k].wait_ge(self.sem["sp"], self.cnt["sp"])
            sn = self.seen[k]
            for k2 in self.E:
                sn["e_" + k2] = self.cnt[k2]
            for i, (s, c) in enumerate(self.dpool):
                sn["d%d" % i] = c

    def push(self):
        _SCOPES.append([])

    def pop(self):
        for t in _SCOPES.pop():
            if t.ds is not None and t.ds not in self.dperm:
                self.dfree.append(t.ds)
                t.ds = None


def _c(a):
    return np.ascontiguousarray(a, dtype=np.float32)


def _vec(a):
    a = np.asarray(a, dtype=np.float32)
    lead = a.shape[:-1]
    c = a.shape[-1] // 128
    a = a.reshape(lead + (c, 128))
    return _c(np.moveaxis(a, -1, 0))


def _wchunks(w, cols):
    k = w.shape[0]
    out = np.empty((len(cols), 128, k // 128, 128), np.float32)
    for i, ci in enumerate(cols):
        out[i] = w[:, ci].reshape(k // 128, 128, 128).transpose(1, 0, 2)
    return out


def _rows(w):
    k, n = w.shape
    return _c(w.reshape(k // 128, 128, n).transpose(1, 0, 2))


def make_consts():
    c = {}
    inv = 1.0 / (10000.0 ** (np.arange(0, 64, 2, dtype=np.float32) / 64.0))
    ang = np.arange(S, dtype=np.float32)[:, None] * inv[None, :]
    ang = np.concatenate([ang, ang], axis=-1).astype(np.float32)
    cos = np.cos(ang).astype(np.float32).T
    sin = np.sin(ang).astype(np.float32).T
    sgn = np.where(np.arange(64) < 32, -1.0, 1.0).astype(np.float32)[:, None]
    c["k_cos"] = _c(np.concatenate([cos, cos], 0))
    c["k_sin"] = _c(np.concatenate([sin * sgn, sin * sgn], 0))
    psw = np.zeros((128, 128), np.float32)
    for cp in range(128):
        base = (cp // 64) * 64
        psw[base + ((cp % 64) + 32) % 64, cp] = 1.0
    c["k_psw"] = psw
    c["k_ident"] = np.eye(128, dtype=np.float32)
    c["k_ones"] = np.ones((128, 128), np.float32)
    expb = np.zeros((128, S), np.float32)
    kk = np.arange(S)
    expb[kk // 64, kk] = BIG
    c["k_expb"] = expb
    j = np.arange(128)[:, None]
    i = np.arange(128)[None, :]
    tri = np.zeros((128, 2, 128), np.float32)
    tri[:, 0, :] = np.where(j > i, -BIG, 0.0)
    tri[:, 1, :] = np.where(j <= i, -BIG, 0.0)
    c["k_tri"] = tri
    shw = np.zeros((128, 512), np.float32)
    r = np.arange(512) - 256
    shw[0, r <= -2] = 1.0
    for jj in range(1, 9):
        shw[jj, r == jj - 2] = 1.0
    shw[9, r >= 7] = 1.0
    c["k_shw"] = shw
    pat = np.zeros((128, 128), np.float32)
    ii = np.arange(128)
    for jj in range(1, 9):
        pat[jj] = np.where(ii >= 31 + 16 * (jj - 2), 0.0, -BIG)
    pat[9] = -BIG
    c["k_pat"] = pat
    n = np.arange(256)[:, None]
    b = np.arange(64)[None, :]
    ov = ((16 * n <= 64 * b + 63) & (16 * n + 31 >= 64 * b)).astype(np.float32)
    ovc = np.zeros((128, 2, 65), np.float32)
    for jc in range(2):
        ovc[:, jc, :64] = ov[jc * 128:(jc + 1) * 128]
        ovc[:, jc, 64] = 1.0
    c["k_ovc"] = ovc
    pa = np.zeros((128, 128), np.float32)
    pb = np.zeros((128, 128), np.float32)
    for q in range(128):
        cr = q // 64
        for cc in range(127):
            rr = cc - 63
            if rr <= cr - 2:
                pa[q, cc] = 1.0
            elif rr <= cr:
                pb[q, cc] = 1e4
            else:
                pb[q, cc] = -1e4
    c["k_pa"] = pa
    c["k_pb"] = pb
    return c


def prep_shared(inp):
    g = {}
    f = lambda k: np.asarray(inp[k], dtype=np.float32)
    g["gmix"] = _vec(f("norm_mix"))
    g["gffn"] = _vec(f("norm_ffn"))
    g["gfin"] = _vec(f("norm_final"))
    wup = f("f_w_up")
    cols = [np.arange(j * 128, (j + 1) * 128) for j in range(44)]
    g["wup"] = np.stack([_wchunks(wup[l], cols) for l in range(4)])
    g["fcw"] = _c(np.moveaxis(f("f_conv_w").reshape(4, 3, 44, 128), 3, 0).transpose(0, 1, 3, 2))
    g["fcb"] = _vec(f("f_conv_b"))
    g["wdn"] = np.stack([_rows(f("f_w_down")[l]) for l in range(4)])
    cw = f("c_w_in")
    ccols = [np.concatenate([np.arange(n * 128, (n + 1) * 128), 1024 + np.arange(n * 128, (n + 1) * 128)]) for n in range(8)]
    cwin = np.empty((2, 8, 128, 8, 256), np.float32)
    for l in range(2):
        for n in range(8):
            cwin[l, n] = cw[l][:, ccols[n]].reshape(8, 128, 256).transpose(1, 0, 2)
    g["cwin"] = cwin
    g["cbin"] = _vec(f("c_b_in"))
    g["ccw"] = _c(np.moveaxis(f("c_conv_w").reshape(2, 4, 8, 128), 3, 0).transpose(0, 1, 3, 2))
    g["ccb"] = _vec(f("c_conv_b"))
    g["cwa"] = _c(f("c_w_a"))
    g["cwi"] = _c(f("c_w_i"))
    g["cba"] = _vec(f("c_b_a"))
    g["cbi"] = _vec(f("c_b_i"))
    g["clam"] = _vec(f("c_lambda"))
    g["cwout"] = np.stack([_rows(f("c_w_out")[l]) for l in range(2)])
    g["cbout"] = _vec(f("c_b_out"))
    aw = f("a_w_in")
    acols = []
    for i in range(4):
        acols.append(np.concatenate([np.arange(i * 64, (i + 1) * 64), np.arange((4 + i) * 64, (5 + i) * 64)]))
    for t in (0, 1, 2, 4):
        acols.append(512 + t * 128 + np.arange(128))
    for t in range(12):
        acols.append(1304 + t * 128 + np.arange(128))
    g["awin"] = np.stack([_wchunks(aw[l], acols) for l in range(2)])
    vcols = np.concatenate([512 + 3 * 128 + np.arange(128), 512 + 5 * 128 + np.arange(128), 1280 + np.arange(24)])
    g["awv"] = np.stack([_rows(aw[l][:, vcols]) for l in range(2)])
    g["acw"] = _c(np.moveaxis(f("a_conv_w").reshape(2, 3, 4, 128), 3, 0).transpose(0, 1, 3, 2))
    w1 = f("a_cmp_w1").reshape(2, 2, 32, 64, 128).transpose(0, 1, 3, 2, 4)
    g["aw1"] = _c(np.concatenate([w1, w1], axis=2))
    pe = f("a_cmp_pe").transpose(0, 1, 3, 2)
    g["ape"] = _c(np.concatenate([pe, np.zeros_like(pe)], axis=2))
    g["aw2"] = _c(f("a_cmp_w2"))
    g["awout"] = np.stack([_rows(f("a_w_out")[l]) for l in range(2)])
    g.update(make_consts())
    return {k: _c(v) for k, v in g.items()}


FULL_PLAN = [("even", 0), ("ffn", 0), ("odd", 1), ("ffn", 1), ("even", 2), ("ffn", 2), ("odd", 3), ("ffn", 3)]


def build(shapes, plan=None, final=True, dbg=None):
    plan = plan or FULL_PLAN
    nc = bass.Bass("TRN2", target_bir_lowering=False)
    es = contextlib.ExitStack()
    P = Prog(nc, es)
    dr = {}
    for name, shp in shapes.items():
        dr[name] = nc.dram_tensor(name, list(shp), F32, kind="ExternalInput").ap()
    out_d = nc.dram_tensor("out", [8, 128, S], F32, kind="ExternalOutput").ap()
    xr_d = nc.dram_tensor("xr", [8, 128, S], F32, kind="Internal").ap()
    os_d = nc.dram_tensor("osc", [8, 128, S], BF16, kind="Internal").ap()
    as_d = nc.dram_tensor("asc", [NPAIR, 128, S], BF16, kind="Internal").ap()
    qs_d = nc.dram_tensor("qsc", [4, 128, S], BF16, kind="Internal").ap()
    x_d = dr["xT"].rearrange("(c p) s -> c p s", p=128)
    XR = [T("xr%d" % t, dram=True) for t in range(NT)]
    OS = [T("os%d" % t, dram=True) for t in range(NT)]
    AS = [T("as%d" % t, dram=True) for t in range(NT)]
    QS = [T("qs%d" % t, dram=True) for t in range(NT)]
    XIN = T("xin", dram=True)
    OUT = T("out", dram=True)

    ps = [es.enter_context(nc.psum_tensor("ps%d" % i, [128, 512], F32)) for i in range(8)]
    PS = [T("ps%d" % i) for i in range(8)]

    uid = [0]

    def sb(name, shape, dt=F32):
        uid[0] += 1
        return nc.sbuf_tensor("s%d_%s" % (uid[0], name), list(shape), dt)

    hT = es.enter_context(sb("hT", [128, 8, S], BF16))
    HT = [[T("h%d_%d" % (c, t)) for t in range(NT)] for c in range(8)]
    ones_f = es.enter_context(sb("ones_f", [128, 128]))
    ident_f = es.enter_context(sb("ident_f", [128, 128]))
    gmix = es.enter_context(sb("gmix", [128, 4, 8]))
    gffn = es.enter_context(sb("gffn", [128, 4, 8]))
    gfin = es.enter_context(sb("gfin", [128, 8]))
    CONST = T("const")
    for t_, nm in ((ones_f, "k_ones"), (ident_f, "k_ident"), (gmix, "gmix"), (gffn, "gffn"), (gfin, "gfin")):
        tt_ = T(nm)
        P.dma("sp", t_[:], dr[nm], tt_, w=[tt_], perm=True)
        CONST.w.update(tt_.w)

    def mm(out, lhsT, rhs, start, stop):
        return lambda e: e.matmul(out, lhsT, rhs, start=start, stop=stop, skip_group_check=True)

    def phase_resid(mode, w_dram, kc, o_dram, OT, bias_ap, g_ap, final, w_pre=None):
        P.push()
        with contextlib.ExitStack() as ph:
            xt = [ph.enter_context(sb("xt%d" % i, [128, 8, TT])) for i in range(2)]
            XT = [[T("xt%d_%d" % (i, m)) for m in range(8)] for i in range(2)]
            sq = [ph.enter_context(sb("sq%d" % i, [128, TT], BF16)) for i in range(2)]
            SQ = [T("sq%d" % i) for i in range(2)]
            rt = ph.enter_context(sb("rt", [128, TT]))
            RT = T("rt")
            rstd = [ph.enter_context(sb("rstd%d" % i, [128, TT])) for i in range(2)]
            RSTD = [T("rstd%d" % i) for i in range(2)]
            pend_tiles = []
            if mode != "init":
                wt, WT = w_pre
                nob = 2
                ot = [ph.enter_context(sb("ot%d" % i, [128, kc, TT], BF16)) for i in range(nob)]
                OTL = [T("ot%d" % i) for i in range(nob)]
            for tt in range(NT):
                ts = slice(tt * TT, (tt + 1) * TT)
                b = tt % 2
                if mode == "init":
                    P.dma("sp", xt[b][:], x_d[:, :, ts].rearrange("c p s -> p c s"), XT[b][0], r=[XIN], w=XT[b])
                else:
                    ob = tt % nob
                    if not (os.environ.get("NOOT") and tt > 1):
                        P.dma("sp", ot[ob][:], o_dram[:, :, ts].rearrange("c p s -> p c s"), OTL[ob], r=[OT[tt]], w=[OTL[ob]])
                    P.dma("sp", xt[b][:], xr_d[:, :, ts].rearrange("c p s -> p c s"), XT[b][0], r=[XR[tt]], w=XT[b])
                def stat(m):
                    sb_ = m % 2
                    P.op("pe", mm(ps[4][:], ones_b[:], sq[sb_][:], m == 0, m == 7), r=[SQ[sb_], CONST], w=[PS[4]])

                for m in range(8):
                    if mode != "init":
                        pb_ = m % 4
                        for k in range(kc):
                            P.op("pe", mm(ps[pb_][:], wt[:, k, m * 128:(m + 1) * 128], ot[ob][:, k, :], k == 0, k == kc - 1),
                                 r=[WT, OTL[ob]], w=[PS[pb_]])
                        bsc = bias_ap[:, m:m + 1] if bias_ap is not None else 0.0
                        P.op("dve", lambda e, m=m, pb_=pb_, bsc=bsc: e.scalar_tensor_tensor(
                            out=xt[b][:, m, :], in0=ps[pb_][:], scalar=bsc, in1=xt[b][:, m, :], op0=ALU.add, op1=ALU.add),
                            r=[PS[pb_], CONST], w=[XT[b][m]])
                    sb_ = m % 2
                    P.op("act", lambda e, m=m, sb_=sb_: e.activation(out=sq[sb_][:], in_=xt[b][:, m, :], func=AF.Square),
                         r=[XT[b][m]], w=[SQ[sb_]])
                    if pend_tiles:
                        pend_tiles[-1](m)
                    if m >= 1:
                        stat(m - 1)
                stat(7)
                P.op("act", lambda e: e.activation(out=rt[:], in_=ps[4][:], func=AF.Sqrt, scale=1.0 / D, bias=eps_t[:, 0:1]),
                     r=[PS[4], CONST], w=[RT])
                P.op("dve", lambda e, b=b: e.reciprocal(out=rstd[b][:], in_=rt[:]), r=[RT], w=[RSTD[b]])
                if not final:
                    P.dma("pool", xr_d[:, :, ts].rearrange("c p s -> p c s"), xt[b][:], XT[b][0], r=XT[b], w=[XR[tt]])

                def hop(m, b=b, ts=ts, tt=tt):
                    if not final:
                        P.op("dve", lambda e: e.scalar_tensor_tensor(
                            out=hT[:, m, ts], in0=xt[b][:, m, :], scalar=g_ap[:, m:m + 1], in1=rstd[b][:], op0=ALU.mult, op1=ALU.mult),
                            r=[XT[b][m], RSTD[b], CONST], w=[HT[m][tt]])
                    else:
                        P.op("dve", lambda e: e.scalar_tensor_tensor(
                            out=xt[b][:, m, :], in0=xt[b][:, m, :], scalar=g_ap[:, m:m + 1], in1=rstd[b][:], op0=ALU.mult, op1=ALU.mult),
                            r=[XT[b][m], RSTD[b], CONST], w=[XT[b][m]])
                        if m == 7:
                            P.dma("pool", out_d[:, :, ts].rearrange("c p s -> p c s"), xt[b][:], XT[b][0], r=XT[b], w=[OUT])
                pend_tiles.append(hop)
            for m in range(8):
                pend_tiles[-1](m)
            P.barrier()
        P.pop()

    ones_b = es.enter_context(sb("ones_b", [128, 128], BF16))
    P.op("dve", lambda e: e.memset(ones_b[:], 1.0), w=[CONST])
    eps_t = es.enter_context(sb("eps_t", [128, 1]))
    one_t = es.enter_context(sb("one_t", [128, 1]))
    P.op("dve", lambda e: e.memset(eps_t[:], EPS), w=[CONST])
    P.op("dve", lambda e: e.memset(one_t[:], 1.0), w=[CONST])

    def phase_ffn1(l, prefetch=None):
        P.push()
        with contextlib.ExitStack() as ph:
            cw = ph.enter_context(sb("fcw", [128, 44, 3]))
            cb = ph.enter_context(sb("fcb", [128, 44]))
            CW = T("fcw")
            P.dma("sp", cw[:], dr["fcw"][:, l], CW, w=[CW])
            P.dma("sp", cb[:], dr["fcb"][:, l], CW, w=[CW])
            wg = [ph.enter_context(sb("wg%d" % i, [128, 2, 8, 128], BF16)) for i in range(2)]
            WG = [T("wg%d" % i) for i in range(2)]
            U = [[ph.enter_context(sb("u%d_%d" % (k, i), [128, TT + 2])) for i in range(2)] for k in range(2)]
            UT = [[T("u%d_%d" % (k, i)) for i in range(2)] for k in range(2)]
            t1 = [ph.enter_context(sb("t1_%d" % k, [128, TT])) for k in range(2)]
            T1 = [T("t1_%d" % k) for k in range(2)]
            y = [ph.enter_context(sb("y_%d" % k, [128, TT])) for k in range(2)]
            Y = [T("y_%d" % k) for k in range(2)]
            sg = ph.enter_context(sb("sg", [128, TT]))
            SG = T("sg")
            ao = [ph.enter_context(sb("ao%d" % i, [128, TT], BF16)) for i in range(3)]
            AO = [T("ao%d" % i) for i in range(3)]

            def loadw(c):
                b = c % 2
                P.dma("pool", wg[b][:, 0], dr["wup"][l, c], WG[b], w=[WG[b]])
                P.dma("pool", wg[b][:, 1], dr["wup"][l, NPAIR + c], WG[b], w=[WG[b]])

            loadw(0)
            it = 0
            for c in range(NPAIR):
                if c + 1 < NPAIR:
                    loadw(c + 1)
                if c == 0 and prefetch:
                    prefetch()
                wb = c % 2
                for tt in range(NT):
                    ts = slice(tt * TT, (tt + 1) * TT)
                    ub = tt % 2
                    for k in range(2):
                        pb_ = (2 * it + k) % 4
                        j = c + k * NPAIR
                        for kk in range(8):
                            P.op("pe", mm(ps[pb_][:], wg[wb][:, k, kk, :], hT[:, kk, ts], kk == 0, kk == 7),
                                 r=[WG[wb], HT[kk][tt]], w=[PS[pb_]])
                        u = U[k][ub]
                        if tt == 0:
                            P.op("pool", lambda e, u=u: e.memset(u[:, 0:2], 0.0), w=[UT[k][ub]])
                        else:
                            pu = U[k][1 - ub]
                            P.op("pool", lambda e, u=u, pu=pu: e.tensor_copy(out=u[:, 0:2], in_=pu[:, TT:TT + 2]),
                                 r=[UT[k][1 - ub]], w=[UT[k][ub]])
                        P.op("act", lambda e, u=u, pb_=pb_: e.activation(out=u[:, 2:TT + 2], in_=ps[pb_][:], func=AF.Identity),
                             r=[PS[pb_]], w=[UT[k][ub]])
                        P.op("act", lambda e, pb_=pb_, j=j, k=k: e.activation(out=t1[k][:], in_=ps[pb_][:], func=AF.Identity,
                                                                            scale=cw[:, j, 2:3], bias=cb[:, j:j + 1]),
                             r=[PS[pb_], CW], w=[T1[k]])
                        P.op("dve", lambda e, u=u, j=j, k=k: e.scalar_tensor_tensor(
                            out=t1[k][:], in0=u[:, 1:TT + 1], scalar=cw[:, j, 1:2], in1=t1[k][:], op0=ALU.mult, op1=ALU.add),
                            r=[UT[k][ub], CW], w=[T1[k]])
                        P.op("dve", lambda e, u=u, j=j, k=k: e.scalar_tensor_tensor(
                            out=y[k][:], in0=u[:, 0:TT], scalar=cw[:, j, 0:1], in1=t1[k][:], op0=ALU.mult, op1=ALU.add),
                            r=[UT[k][ub], T1[k], CW], w=[Y[k]])
                    P.op("act", lambda e: e.activation(out=sg[:], in_=y[0][:], func=AF.Silu), r=[Y[0]], w=[SG])
                    ab = it % 3
                    P.op("dve", lambda e, ab=ab: e.tensor_tensor(out=ao[ab][:], in0=sg[:], in1=y[1][:], op=ALU.mult),
                         r=[SG, Y[1]], w=[AO[ab]])
                    P.dma("sp", as_d[c, :, ts], ao[ab][:], AO[ab], r=[AO[ab]], w=[AS[tt]])
                    it += 1
            P.barrier()
        P.pop()

    def phase_odd(jl, prefetch=None):
        P.push()
        with contextlib.ExitStack() as ph:
            def vt(name, shape):
                return ph.enter_context(sb(name, shape))
            bin_ = vt("cbin", [128, 16]); ccw = vt("ccw", [128, 8, 4]); ccb = vt("ccb", [128, 8])
            cba = vt("cba", [128, 8]); cbi = vt("cbi", [128, 8]); lam = vt("clam", [128, 8]); cl = vt("cl", [128, 8]); bhy = vt("bhy", [128, 8])
            CV = T("cvec")
            for t_, nm in ((bin_, "cbin"), (ccw, "ccw"), (ccb, "ccb"), (cba, "cba"), (cbi, "cbi"), (lam, "clam")):
                P.dma("sp", t_[:], dr[nm][:, jl], CV, w=[CV])
            P.op("act", lambda e: e.activation(out=cl[:], in_=lam[:], func=AF.Exp, scale=-1.0), r=[CV], w=[CV], n=8)
            P.op("act", lambda e: e.activation(out=cl[:], in_=cl[:], func=AF.Ln, bias=one_t[:, 0:1]), r=[CV], w=[CV], n=8)
            P.op("dve", lambda e: e.tensor_scalar(out=cl[:], in0=cl[:], scalar1=-4.0, scalar2=None, op0=ALU.mult), r=[CV], w=[CV], n=8)
            P.op("dve", lambda e: e.tensor_scalar(out=cba[:], in0=cba[:], scalar1=0.5, scalar2=None, op0=ALU.mult), r=[CV], w=[CV], n=8)
            P.op("dve", lambda e: e.tensor_scalar(out=cbi[:], in0=cbi[:], scalar1=0.5, scalar2=None, op0=ALU.mult), r=[CV], w=[CV], n=8)
            P.op("dve", lambda e: e.tensor_scalar(out=bhy[:], in0=bin_[:, 0:8], scalar1=0.5, scalar2=None, op0=ALU.mult), r=[CV], w=[CV], n=8)
            win = [ph.enter_context(sb("cwin%d" % i, [128, 8, 256], BF16)) for i in range(2)]
            WIN = [T("cwin%d" % i) for i in range(2)]
            wai = [ph.enter_context(sb("cwai%d" % i, [128, 2, 128])) for i in range(2)]
            WAI = [T("cwai%d" % i) for i in range(2)]
            U = [vt("cu%d" % i, [128, TT + 3]) for i in range(2)]
            UT = [T("cu%d" % i) for i in range(2)]
            xs = [vt("xs%d" % i, [128, TT]) for i in range(3)]; XS = [T("xs%d" % i) for i in range(3)]
            x2l = [vt("x2_%d" % i, [128, TT]) for i in range(2)]; X2L = [T("x2_%d" % i) for i in range(2)]
            sgm = [vt("sgm%d" % i, [128, TT]) for i in range(2)]; SGM = [T("sgm%d" % i) for i in range(2)]
            uu = [vt("uu%d" % i, [128, TT]) for i in range(2)]; UU = [T("uu%d" % i) for i in range(2)]
            rr = vt("rr", [128, TT]); RR = T("rr")
            ig = vt("ig", [128, TT]); IG = T("ig")
            aa = vt("aa", [128, TT]); AA = T("aa")
            mmul = vt("mmul", [128, TT]); MM = T("mmul")
            hs = [vt("hs%d" % i, [128, TT]) for i in range(2)]
            HS = [T("hs%d" % i) for i in range(2)]
            oo = [ph.enter_context(sb("oo%d" % i, [128, TT], BF16)) for i in range(3)]
            OO = [T("oo%d" % i) for i in range(3)]

            def loadw(n):
                b = n % 2
                P.dma("pool", win[b][:], dr["cwin"][jl, n], WIN[b], w=[WIN[b]])
                P.dma("sp", wai[b][:, 0, :], dr["cwa"][jl, n], WAI[b], w=[WAI[b]])
                P.dma("sp", wai[b][:, 1, :], dr["cwi"][jl, n], WAI[b], w=[WAI[b]])

            def stage1(it, n, tt):
                ts = slice(tt * TT, (tt + 1) * TT)
                wb = n % 2
                ub = tt % 2
                db = it % 2
                xb = it % 3
                x2 = x2l[db]
                X2 = X2L[db]
                py, pu_ = (2 * it) % 4, (2 * it + 1) % 4
                for k, pb_ in ((0, py), (1, pu_)):
                    for kk in range(8):
                        P.op("pe", mm(ps[pb_][:], win[wb][:, kk, k * 128:(k + 1) * 128], hT[:, kk, ts], kk == 0, kk == 7),
                             r=[WIN[wb], HT[kk][tt]], w=[PS[pb_]])
                P.op("act", lambda e: e.activation(out=xs[xb][:], in_=ps[py][:], func=AF.Identity, scale=0.5, bias=bhy[:, n:n + 1]),
                     r=[PS[py], CV], w=[XS[xb]])
                P.op("act", lambda e: e.activation(out=x2[:], in_=ps[py][:], func=AF.Square, bias=bin_[:, n:n + 1]),
                     r=[PS[py], CV], w=[X2])
                u = U[ub]
                if tt == 0:
                    P.op("pool", lambda e: e.memset(u[:, 0:3], 0.0), w=[UT[ub]])
                else:
                    pu = U[1 - ub]
                    P.op("pool", lambda e: e.tensor_copy(out=u[:, 0:3], in_=pu[:, TT:TT + 3]), r=[UT[1 - ub]], w=[UT[ub]])
                P.op("act", lambda e: e.activation(out=u[:, 3:TT + 3], in_=ps[pu_][:], func=AF.Identity, bias=bin_[:, 8 + n:9 + n]),
                     r=[PS[pu_], CV], w=[UT[ub]])

            def stage2(it, n, tt):
                ts = slice(tt * TT, (tt + 1) * TT)
                wb = n % 2
                ub = tt % 2
                db = it % 2
                xb = it % 3
                x2 = x2l[db]
                X2 = X2L[db]
                u = U[ub]
                P.op("dve", lambda e: e.tensor_scalar(out=x2[:], in0=x2[:], scalar1=0.044715, scalar2=1.0, op0=ALU.mult, op1=ALU.add),
                     r=[X2], w=[X2])
                P.op("dve", lambda e: e.tensor_tensor(out=x2[:], in0=x2[:], in1=xs[xb][:], op=ALU.mult), r=[X2, XS[xb]], w=[X2])
                P.op("dve", lambda e: e.tensor_scalar(out=uu[db][:], in0=u[:, 3:TT + 3], scalar1=ccw[:, n, 3:4], scalar2=ccb[:, n:n + 1],
                                                      op0=ALU.mult, op1=ALU.add), r=[UT[ub], CV], w=[UU[db]])
                for j in range(3):
                    P.op("dve", lambda e, j=j: e.scalar_tensor_tensor(out=uu[db][:], in0=u[:, j:TT + j], scalar=ccw[:, n, j:j + 1], in1=uu[db][:],
                                                                      op0=ALU.mult, op1=ALU.add), r=[UT[ub], CV], w=[UU[db]])

                P.op("act", lambda e: e.activation(out=sgm[db][:], in_=x2[:], func=AF.Tanh, scale=1.5957691216), r=[X2], w=[SGM[db]])
                pr, pi_ = 4 + (2 * it) % 4, 4 + (2 * it + 1) % 4
                P.op("pe", mm(ps[pr][:], wai[wb][:, 0, :], uu[db][:], True, True), r=[WAI[wb], UU[db]], w=[PS[pr]])
                P.op("pe", mm(ps[pi_][:], wai[wb][:, 1, :], uu[db][:], True, True), r=[WAI[wb], UU[db]], w=[PS[pi_]])

            def stage3(it, n, tt):
                ts = slice(tt * TT, (tt + 1) * TT)
                wb = n % 2
                db = it % 2
                xb = it % 3
                pr, pi_ = 4 + (2 * it) % 4, 4 + (2 * it + 1) % 4
                P.op("act", lambda e: e.activation(out=rr[:], in_=ps[pr][:], func=AF.Tanh, scale=0.5, bias=cba[:, n:n + 1]), r=[PS[pr], CV], w=[RR])
                P.op("act", lambda e: e.activation(out=ig[:], in_=ps[pi_][:], func=AF.Tanh, scale=0.5, bias=cbi[:, n:n + 1]), r=[PS[pi_], CV], w=[IG])
                P.op("act", lambda e: e.activation(out=aa[:], in_=rr[:], func=AF.Exp, scale=cl[:, n:n + 1], bias=cl[:, n:n + 1]), r=[RR, CV], w=[AA])
                P.op("act", lambda e: e.activation(out=mmul[:], in_=aa[:], func=AF.Square), r=[AA], w=[MM])
                P.op("act", lambda e: e.activation(out=mmul[:], in_=mmul[:], func=AF.Sqrt, scale=-1.0, bias=one_t[:, 0:1]), r=[MM], w=[MM])
                P.op("dve", lambda e: e.scalar_tensor_tensor(out=ig[:], in0=ig[:], scalar=1.0, in1=uu[db][:], op0=ALU.add, op1=ALU.mult), r=[IG, UU[db]], w=[IG])
                P.op("dve", lambda e: e.scalar_tensor_tensor(out=ig[:], in0=ig[:], scalar=0.5, in1=mmul[:], op0=ALU.mult, op1=ALU.mult), r=[IG, MM], w=[IG])
                hb = tt % 2
                init = 0.0 if tt == 0 else hs[1 - hb][:, TT - 1:TT]
                rl = [AA, IG] + ([HS[1 - hb]] if tt else [])
                P.op("dve", lambda e: e.tensor_tensor_scan(out=hs[hb][:], data0=aa[:], data1=ig[:], initial=init,
                                                           op0=ALU.mult, op1=ALU.add), r=rl, w=[HS[hb]])
                P.op("dve", lambda e: e.tensor_tensor(out=xs[xb][:], in0=xs[xb][:], in1=hs[hb][:], op=ALU.mult), r=[XS[xb], HS[hb]], w=[XS[xb]])
                ob = it % 3
                P.op("dve", lambda e: e.scalar_tensor_tensor(out=oo[ob][:], in0=sgm[db][:], scalar=1.0, in1=xs[xb][:], op0=ALU.add, op1=ALU.mult),
                     r=[XS[xb], SGM[db]], w=[OO[ob]])
                P.dma("sp", os_d[n, :, ts], oo[ob][:], OO[ob], r=[OO[ob]], w=[OS[tt]])

            loadw(0)
            if prefetch:
                prefetch()
            its = [(n, tt) for n in range(8) for tt in range(NT)]
            NI = len(its)
            for k in range(NI + 2):
                if k < NI:
                    stage1(k, *its[k])
                if 1 <= k <= NI:
                    stage2(k - 1, *its[k - 1])
                if 2 <= k:
                    stage3(k - 2, *its[k - 2])
                if k >= 2 and k - 2 < NI and its[k - 2][1] == NT - 1 and its[k - 2][0] + 2 < 8:
                    loadw(its[k - 2][0] + 2)
                if k == 0:
                    loadw(1)
            P.barrier()
        P.pop()

    def dump(idx, ap, n, TL):
        P.push()
        with contextlib.ExitStack() as dd:
            stg = dd.enter_context(sb("stg", [128, n]))
            STG = T("stg")
            P.op("dve", lambda e: e.tensor_copy(out=stg[:, 0:n], in_=ap), r=TL, w=[STG])
            P.dma("sp", out_d[idx, :, 0:n], stg[:, 0:n], STG, r=[STG], w=[OUT])
            P.barrier()
        P.pop()

    def phase_even(jl, dbg=None, prefetch=None):
        P.push()
        try:
            _phase_even(jl, dbg, prefetch)
        finally:
            P.pop()

    def _phase_even(jl, dbg=None, prefetch=None):
        rot = {"s": 0}

        def sbank():
            rot["s"] = (rot["s"] + 1) % 3
            return rot["s"]

        with contextlib.ExitStack() as ph:
            def vt(name, shape, dt=F32):
                return ph.enter_context(sb(name, shape, dt))
            psw = vt("psw", [128, 128]); PSW = T("psw")
            P.dma("sp", psw[:], dr["k_psw"], PSW, w=[PSW])
            kcP = vt("kcP", [128, 2, 256], BF16); KCP = T("kcP")
            rc = vt("rc", [128, 2, 2, 129], BF16); RC = T("rc")
            P.op("dve", lambda e: e.memset(kcP[:], 0.0), w=[KCP])
            P.op("dve", lambda e: e.memset(rc[:], 0.0), w=[RC])
            for h in range(2):
                P.dma("pool", rc[:, h, :, 64:129], dr["k_ovc"], RC, r=[], w=[RC])
            cs = [vt("cs%d" % i, [128, 2, TT]) for i in range(2)]
            CS = [T("cs%d" % i) for i in range(2)]
            wq = [vt("wq%d" % i, [128, 8, 128], BF16) for i in range(3)]
            WQ = [T("wq%d" % i) for i in range(3)]
            zt = [vt("zt%d" % i, [128, TT]) for i in range(2)]
            ZT = [T("zt%d" % i) for i in range(2)]
            r1 = vt("r1", [128, TT]); R1 = T("r1")
            r2 = vt("r2", [128, TT]); R2 = T("r2")
            wcnt = {"n": 0, "it": 0}

            def inproj(chunk_ids, post, group=1):
                ng = len(chunk_ids) // group

                def loadw(gi):
                    bl = []
                    for k in range(group):
                        b = wcnt["n"] % 3
                        wcnt["n"] += 1
                        P.dma("pool", wq[b][:], dr["awin"][jl, chunk_ids[gi * group + k]], WQ[b], w=[WQ[b]])
                        bl.append(b)
                    return bl
                cur = loadw(0)
                for gi in range(ng):
                    nxt = loadw(gi + 1) if (gi + 1 < ng and group == 1) else None
                    for tt in range(NT):
                        ts = slice(tt * TT, (tt + 1) * TT)
                        banks = []
                        for k in range(group):
                            pb_ = (wcnt["it"]) % 6 if group > 1 else sbank()
                            wcnt["it"] += 1
                            for kk in range(8):
                                P.op("pe", mm(ps[pb_][:], wq[cur[k]][:, kk, :], hT[:, kk, ts], kk == 0, kk == 7),
                                     r=[WQ[cur[k]], HT[kk][tt]], w=[PS[pb_]])
                            banks.append(pb_)
                        post(chunk_ids[gi * group:(gi + 1) * group], tt, banks)
                    if group > 1 and gi + 1 < ng:
                        nxt = loadw(gi + 1)
                    cur = nxt

            ropec = {"n": 0}

            def rope_post(dst):
                def post(cil, tt, banks):
                    ts = slice(tt * TT, (tt + 1) * TT)
                    pb_ = banks[0]
                    n = ropec["n"]
                    ropec["n"] += 1
                    cb_ = n % 2
                    P.dma("sp", cs[cb_][:, 0, :], dr["k_cos"][:, ts], CS[cb_], w=[CS[cb_]])
                    P.dma("sp", cs[cb_][:, 1, :], dr["k_sin"][:, ts], CS[cb_], w=[CS[cb_]])
                    z = zt[cb_]
                    P.op("act", lambda e: e.activation(out=z[:], in_=ps[pb_][:], func=AF.Identity), r=[PS[pb_]], w=[ZT[cb_]])
                    sw = 6 + cb_
                    P.op("pe", mm(ps[sw][:], psw[:], z[:], True, True), r=[PSW, ZT[cb_]], w=[PS[sw]])
                    P.op("pool", lambda e: e.tensor_tensor(out=r1[:], in0=z[:], in1=cs[cb_][:, 0, :], op=ALU.mult), r=[ZT[cb_], CS[cb_]], w=[R1])
                    P.op("dve", lambda e: e.tensor_tensor(out=r2[:], in0=ps[sw][:], in1=cs[cb_][:, 1, :], op=ALU.mult), r=[PS[sw], CS[cb_]], w=[R2])
                    dst(cil[0], tt, ts)
                return post

            P.push()
            with contextlib.ExitStack() as st:
                st.enter_context(nc.named_scope("evA%d" % jl))
                def va(name, shape, dt=F32):
                    return st.enter_context(sb(name, shape, dt))
                kin = va("kin", [128, 2, 2, S], BF16)
                KIN = [[T("kin%d_%d" % (kv, t)) for t in range(NT)] for kv in range(2)]
                for kv in range(2):
                    P.op("dve", lambda e, kv=kv: e.memset(kin[:, kv], 0.0), w=KIN[kv])
                w1t = [va("w1t%d" % kv, [128, 32, 128], BF16) for kv in range(2)]
                W1T = [T("w1t%d" % kv) for kv in range(2)]
                pet = va("pet", [128, 2, 32], BF16); PET = T("pet")
                w2p = va("w2p", [128, 2, 128], BF16); W2P = T("w2p")
                w2v = va("w2v", [128, 64], BF16); W2V = T("w2v")
                P.op("dve", lambda e: e.memset(w2p[:], 0.0), w=[W2P])
                for kv in range(2):
                    P.dma("pool", w1t[kv][:], dr["aw1"][jl, kv], W1T[kv], w=[W1T[kv]], max_dma_last_dim=4096)
                    P.dma("pool", pet[:, kv, :], dr["ape"][jl, kv], PET, w=[PET])
                for h in range(2):
                    P.dma("pool", w2p[:, h, 64 * h:64 * h + 64], dr["aw2"][jl, 0], W2P, w=[W2P])
                P.dma("pool", w2v[:], dr["aw2"][jl, 1], W2V, w=[W2V])

                def dst_kc(ci, tt, ts):
                    for h in range(2):
                        hs_ = slice(64 * h, 64 * h + 64)
                        P.op("dve", lambda e, h=h, hs_=hs_: e.tensor_tensor(out=kin[hs_, 0, h, ts], in0=r1[hs_, :], in1=r2[hs_, :], op=ALU.add),
                             r=[R1, R2], w=[KIN[0][tt]])
                inproj([4], rope_post(dst_kc))

                def post_vc(cil, tt, banks):
                    ts = slice(tt * TT, (tt + 1) * TT)
                    for h in range(2):
                        hs_ = slice(64 * h, 64 * h + 64)
                        P.op("act", lambda e, h=h, hs_=hs_: e.activation(out=kin[hs_, 1, h, ts], in_=ps[banks[0]][hs_, :], func=AF.Identity),
                             r=[PS[banks[0]]], w=[KIN[1][tt]])
                inproj([5], post_vc)

                hx = va("hx", [128, 256]); HX = T("hx")
                h2 = va("h2", [128, 256]); H2 = T("h2")
                hg = va("hg", [128, 256]); HG = T("hg")
                hidb = va("hidb", [128, 256], BF16); HB = T("hidb")
                P.op("dve", lambda e: e.memset(hidb[:], 0.0), w=[HB])
                for kv in range(2):
                    for h in range(2):
                        pb_ = sbank()
                        for l in range(32):
                            P.op("pe", mm(ps[pb_][:, 0:255], w1t[kv][:, l, :], kin[:, kv, h, l:l + 4065:16], l == 0, False),
                                 r=[W1T[kv]] + KIN[kv], w=[PS[pb_]])
                        for l in range(32):
                            P.op("pe", mm(ps[pb_][:, 0:255], w1t[kv][:, l, :], pet[:, kv, l:l + 1].to_broadcast([128, 255]), False, l == 31),
                                 r=[W1T[kv], PET], w=[PS[pb_]])
                        P.op("act", lambda e: e.activation(out=hx[:, 0:255], in_=ps[pb_][:, 0:255], func=AF.Identity), r=[PS[pb_]], w=[HX])
                        P.op("act", lambda e: e.activation(out=h2[:, 0:255], in_=ps[pb_][:, 0:255], func=AF.Square), r=[PS[pb_]], w=[H2])
                        P.op("dve", lambda e: e.tensor_scalar(out=h2[:, 0:255], in0=h2[:, 0:255], scalar1=0.044715, scalar2=1.0, op0=ALU.mult, op1=ALU.add),
                             r=[H2], w=[H2])
                        P.op("dve", lambda e: e.tensor_tensor(out=h2[:, 0:255], in0=h2[:, 0:255], in1=hx[:, 0:255], op=ALU.mult), r=[H2, HX], w=[H2])
                        P.op("act", lambda e: e.activation(out=hg[:, 0:255], in_=h2[:, 0:255], func=AF.Sigmoid, scale=1.5957691216), r=[H2], w=[HG])
                        P.op("dve", lambda e: e.tensor_tensor(out=hidb[:, 0:255], in0=hx[:, 0:255], in1=hg[:, 0:255], op=ALU.mult), r=[HX, HG], w=[HB])
                        if kv == 0:
                            P.op("pe", mm(ps[3][:, 0:255], w2p[:, h, :], hidb[:, 0:255], True, True), r=[W2P, HB], w=[PS[3]])
                            P.op("act", lambda e, h=h: e.activation(out=kcP[:, h, 0:255], in_=ps[3][:, 0:255], func=AF.Identity), r=[PS[3]], w=[KCP])
                        else:
                            for jc in range(2):
                                P.op("pe", mm(ps[3][:, jc * 64:(jc + 1) * 64], hidb[:, jc * 128:(jc + 1) * 128], w2v[:], True, True),
                                     r=[W2V, HB], w=[PS[3]])
                            P.op("act", lambda e, h=h: e.activation(out=rc[:, h, :, 0:64], in_=ps[3][:, 0:128].rearrange("p (a b) -> p a b", a=2),
                                                                    func=AF.Identity), r=[PS[3]], w=[RC])
                if dbg == "A":
                    dump(0, kcP[:].rearrange("p a b -> p (a b)"), 512, [KCP])
                    dump(1, rc[:].rearrange("p a b c -> p (a b c)"), 516, [RC])
                    dump(2, kin[:, 0, 0, :], 4096, KIN[0])
                    dump(3, kin[:, 1, 1, :], 4096, KIN[1])
                P.barrier()
            P.pop()
            if dbg == "A":
                return

            scB = nc.named_scope("evB%d" % jl)
            scB.__enter__()
            if prefetch:
                prefetch()
            ksP = vt("ksP", [128, 2, S], BF16); KS = [T("ks%d" % t) for t in range(NT)]
            kwP = vt("kwP", [128, 2, S], BF16); KW = [T("kw%d" % t) for t in range(NT)]
            P.op("dve", lambda e: e.memset(ksP[:], 0.0), w=KS)
            P.op("dve", lambda e: e.memset(kwP[:], 0.0), w=KW)
            vtok = vt("vtok", [128, 32, 2, 2, 65], BF16); VT = [T("vt%d" % t) for t in range(NT)]
            P.op("dve", lambda e: e.memset(vtok[:], 1.0), w=VT)
            gsig = vt("gsig", [128, 32, 24]); GS = [T("gs%d" % t) for t in range(NT)]
            expb = vt("expb", [128, S], BF16); tri = vt("tri", [128, 2, 128], BF16)
            shw = vt("shw", [128, 512], BF16); pat = vt("pat", [128, 128], BF16)
            identb = vt("identb", [128, 128], BF16)
            pa = vt("pa", [128, 128]); pbt = vt("pbt", [128, 128])
            AC = T("attconst")
            for t_, nm, q_ in ((expb, "k_expb", "pool"), (tri, "k_tri", "pool"), (shw, "k_shw", "pool"), (pat, "k_pat", "pool"),
                               (identb, "k_ident", "pool"), (pa, "k_pa", "sp"), (pbt, "k_pb", "sp")):
                tq_ = T(nm)
                P.dma(q_, t_[:], dr[nm], tq_, w=[tq_], max_dma_last_dim=4096)
                AC.w.update(tq_.w)
            wv = vt("wv", [128, 8, 280], BF16); WV = T("wv")
            P.dma("pool", wv[:], dr["awv"][jl], WV, w=[WV], max_dma_last_dim=4096)

            def dst_k(buf, TL):
                def dst(ci, tt, ts):
                    for h in range(2):
                        hs_ = slice(64 * h, 64 * h + 64)
                        P.op("dve", lambda e, h=h, hs_=hs_: e.tensor_tensor(out=buf[hs_, h, ts], in0=r1[hs_, :], in1=r2[hs_, :], op=ALU.add),
                             r=[R1, R2], w=[TL[tt]])
                return dst
            inproj([6], rope_post(dst_k(ksP, KS)))
            inproj([7], rope_post(dst_k(kwP, KW)))
            qo = [vt("qo%d" % i, [128, TT], BF16) for i in range(2)]
            QO = [T("qo%d" % i) for i in range(2)]
            qc = {"n": 0}

            def dst_q(ci, tt, ts):
                b = qc["n"] % 2
                qc["n"] += 1
                P.op("dve", lambda e: e.tensor_tensor(out=qo[b][:], in0=r1[:], in1=r2[:], op=ALU.add), r=[R1, R2], w=[QO[b]])
                P.dma("sp", qs_d[ci, :, ts], qo[b][:], QO[b], r=[QO[b]], w=[QS[tt]])
            inproj([0, 1, 2, 3], rope_post(dst_q))
            for tq in range(32):
                pb_ = sbank()
                for kk in range(8):
                    P.op("pe", mm(ps[pb_][:, 0:280], hT[:, kk, tq * 128:(tq + 1) * 128], wv[:, kk, :], kk == 0, kk == 7),
                         r=[WV, HT[kk][tq // 4]], w=[PS[pb_]])
                P.op("act", lambda e, tq=tq, pb_=pb_: e.activation(out=vtok[:, tq, :, :, 0:64],
                                                                  in_=ps[pb_][:, 0:256].rearrange("p (a b c) -> p a b c", a=2, b=2), func=AF.Identity),
                     r=[PS[pb_]], w=[VT[tq // 4]])
                P.op("act", lambda e, tq=tq, pb_=pb_: e.activation(out=gsig[:, tq, :], in_=ps[pb_][:, 256:280], func=AF.Sigmoid),
                     r=[PS[pb_]], w=[GS[tq // 4]])
            acw = vt("acw", [128, 4, 3]); ACW = T("acw")
            P.dma("sp", acw[:], dr["acw"][:, jl], ACW, w=[ACW])
            xg = vt("xg", [128, TT]); XG = T("xg")
            Ub = [vt("ub%d" % i, [128, TT + 2]) for i in range(2)]
            UB = [T("ub%d" % i) for i in range(2)]
            yb = vt("yb", [128, TT]); YB = T("yb")
            obo = [vt("obo%d" % i, [128, TT], BF16) for i in range(2)]
            OBO = [T("obo%d" % i) for i in range(2)]
            cc = {"n": 0}

            def post_conv(cil, tt, banks):
                ts = slice(tt * TT, (tt + 1) * TT)
                i = cil[0] - 8
                pbb, pbc, pbx = banks
                ub = tt % 2
                u = Ub[ub]
                P.op("act", lambda e: e.activation(out=xg[:], in_=ps[pbx][:], func=AF.Identity), r=[PS[pbx]], w=[XG])
                if tt == 0:
                    P.op("pool", lambda e: e.memset(u[:, 0:2], 0.0), w=[UB[ub]])
                else:
                    pu = Ub[1 - ub]
                    P.op("pool", lambda e: e.tensor_copy(out=u[:, 0:2], in_=pu[:, TT:TT + 2]), r=[UB[1 - ub]], w=[UB[ub]])
                P.op("dve", lambda e: e.tensor_tensor(out=u[:, 2:TT + 2], in0=ps[pbc][:], in1=xg[:], op=ALU.mult), r=[PS[pbc], XG], w=[UB[ub]])
                P.op("dve", lambda e: e.tensor_scalar(out=yb[:], in0=u[:, 2:TT + 2], scalar1=acw[:, i, 2:3], scalar2=None, op0=ALU.mult),
                     r=[UB[ub], ACW], w=[YB])
                for j in range(2):
                    P.op("dve", lambda e, j=j: e.scalar_tensor_tensor(out=yb[:], in0=u[:, j:TT + j], scalar=acw[:, i, j:j + 1], in1=yb[:],
                                                                      op0=ALU.mult, op1=ALU.add), r=[UB[ub], ACW], w=[YB])
                b = cc["n"] % 2
                cc["n"] += 1
                P.op("dve", lambda e: e.tensor_tensor(out=obo[b][:], in0=ps[pbb][:], in1=yb[:], op=ALU.mult), r=[PS[pbb], YB], w=[OBO[b]])
                P.dma("sp", os_d[4 + i, :, ts], obo[b][:], OBO[b], r=[OBO[b]], w=[OS[tt]])
            inproj([8, 12, 16, 9, 13, 17, 10, 14, 18, 11, 15, 19], post_conv, group=3)

            if dbg == "B":
                dump(0, ksP[:, 0, :], 4096, KS)
                dump(1, kwP[:, 1, :], 4096, KW)
                dump(2, vtok[:, 0:15].rearrange("p a b c d -> p (a b c d)"), 3900, VT)
                dump(3, gsig[:].rearrange("p a b -> p (a b)"), 768, GS)
                return
            scB.__exit__(None, None, None)
            ph.enter_context(nc.named_scope("evC%d" % jl))
            qt_ = [vt("qt%d" % i, [128, 4, 128], BF16) for i in range(2)]
            QT = [T("qt%d" % i) for i in range(2)]
            et = [vt("et%d" % i, [128, 512], BF16) for i in range(3)]
            ET = [T("et%d" % i) for i in range(3)]
            ec = {"n": 0}
            den = vt("den", [128, 12]); rden = vt("rden", [128, 12]); cf = vt("cf", [128, 12])
            DEN = T("den")
            imp = vt("imp", [128, 64]); IMP = T("imp")
            sc = vt("sc", [128, 64]); SC = T("sc")
            m8 = vt("m8", [128, 8]); M8 = T("m8")
            selq = vt("selq", [128, 128]); SELQ = T("selq")
            P.op("dve", n=64, fn=lambda e: e.memset(selq[:], 0.0), w=[SELQ])
            selT = vt("selT", [128, 128], BF16); SELT = T("selT")
            otok = [vt("otok%d" % i, [128, 512]) for i in range(2)]
            OTOK = [T("otok%d" % i) for i in range(2)]
            obt = [vt("obt%d" % i, [128, 4, 128], BF16) for i in range(2)]
            OBT = [T("obt%d" % i) for i in range(2)]

            def bc4(ap):
                return ap.unsqueeze(1).to_broadcast([128, 4, 128])

            def loadq(qt):
                b = qt % 2
                P.dma("sp", qt_[b][:], qs_d[:, :, qt * 128:(qt + 1) * 128].rearrange("c p s -> p c s"), QT[b], r=[QS[qt // 4]], w=[QT[b]])

            def score_exp(kP, KT_, kc, h, qb, masks):
                sb_ = sbank()
                q512 = qt_[qb][:]
                P.op("pe", mm(ps[sb_][:], kP, q512, True, len(masks) == 0), r=KT_ + [QT[qb]], w=[PS[sb_]])
                for mi, (lh, rh, rl) in enumerate(masks):
                    P.op("pe", mm(ps[sb_][:], lh, rh, False, mi == len(masks) - 1), r=[AC] + rl, w=[PS[sb_]])
                eb = ec["n"] % 3
                ec["n"] += 1
                P.op("act", lambda e: e.activation(out=et[eb][:], in_=ps[sb_][:], func=AF.Exp, scale=0.125), r=[PS[sb_]], w=[ET[eb]])
                return eb

            fifo = []

            def push_pv(fn):
                fifo.append(fn)
                while len(fifo) > 2:
                    fifo.pop(0)()

            def flush():
                while fifo:
                    fifo.pop(0)()

            QN = 2 if (dbg and dbg.startswith("C")) else 32

            def combine(qt, h, bank, br):
                c0 = 4 * br
                tq4 = qt // 4
                ot_ = otok[qt % 2]
                OT_ = OTOK[qt % 2]
                if dbg in ("C0", "C1", "C2") and dbg != "C%d" % br:
                    return
                P.op("dve", n=64, fn=lambda e: e.tensor_scalar(out=den[:, c0:c0 + 4], in0=ps[bank][:, 0:260].rearrange("p (a b) -> p a b", a=4)[:, :, 64],
                                                      scalar1=1e-30, scalar2=None, op0=ALU.max), r=[PS[bank]], w=[DEN])
                P.op("dve", n=64, fn=lambda e: e.reciprocal(out=rden[:, c0:c0 + 4], in_=den[:, c0:c0 + 4]), r=[DEN], w=[DEN])
                P.op("dve", n=64, fn=lambda e: e.tensor_tensor(out=cf[:, c0:c0 + 4], in0=rden[:, c0:c0 + 4], in1=gsig[:, qt, h * 12 + br:h * 12 + 12:3], op=ALU.mult),
                     r=[DEN, GS[tq4]], w=[DEN])
                for g in range(4):
                    oc = h * 256 + g * 64
                    P.op("dve", n=64, fn=lambda e, g=g, oc=oc: e.scalar_tensor_tensor(out=ot_[:, oc:oc + 64], in0=ps[bank][:, g * 65:g * 65 + 64],
                                                                            scalar=cf[:, c0 + g:c0 + g + 1], in1=ot_[:, oc:oc + 64],
                                                                            op0=ALU.mult, op1=ALU.add), r=[PS[bank], DEN], w=[OT_])

            def part1(qt, h):
                qb = qt % 2
                tq4 = qt // 4
                ot_ = otok[qt % 2]
                OT_ = OTOK[qt % 2]
                jcs = [0, 1] if qt >= 16 else [0]
                for jc in jcs:
                    st_ = 128 * jc - 8 * qt + 256
                    eb = score_exp(kcP[:, h, jc * 128:(jc + 1) * 128], [KCP], jc, h, qb,
                                   [(shw[:, st_:st_ + 128], bc4(pat[:]), [])])

                    def pv_cmp(eb=eb, jc=jc, h=h, last=(jc == jcs[-1])):
                        for g in range(4):
                            bk = 6 + g // 2
                            reg = slice((g % 2) * 129, (g % 2) * 129 + 129)
                            P.op("pe", mm(ps[bk][:, reg], et[eb][:, g * 128:(g + 1) * 128], rc[:, h, jc, :],
                                          jc == 0 and g % 2 == 0, last), r=[ET[eb], RC], w=[PS[bk]])
                    push_pv(pv_cmp)
                kcs = list(range(max(0, qt - 4), qt + 1))
                for kc in kcs:
                    masks = []
                    if kc == qt - 4:
                        masks.append((identb[:], bc4(tri[:, 1, :]), []))
                    if kc == qt:
                        masks.append((identb[:], bc4(tri[:, 0, :]), []))
                    eb = score_exp(kwP[:, h, kc * 128:(kc + 1) * 128], [KW[kc // 4]], kc, h, qb, masks)

                    def pv_win(eb=eb, kc=kc, h=h, first=(kc == kcs[0]), last=(kc == qt)):
                        for g in range(4):
                            P.op("pe", mm(ps[5][:, g * 65:(g + 1) * 65], et[eb][:, g * 128:(g + 1) * 128], vtok[:, kc, 1, h, :],
                                          first and g == 0, last), r=[ET[eb], VT[kc // 4]], w=[PS[5]])
                    push_pv(pv_win)
                flush()
                for bk in range(2):
                    P.op("dve", n=64, fn=lambda e, bk=bk: e.tensor_scalar(out=den[:, 2 * bk:2 * bk + 2],
                                                                in0=ps[6 + bk][:, 0:258].rearrange("p (a b) -> p a b", a=2)[:, :, 128],
                                                                scalar1=1e-30, scalar2=None, op0=ALU.max), r=[PS[6 + bk]], w=[DEN])
                P.op("dve", n=64, fn=lambda e: e.reciprocal(out=rden[:, 0:4], in_=den[:, 0:4]), r=[DEN], w=[DEN])
                for g in range(4):
                    bk = 6 + g // 2
                    o0 = (g % 2) * 129
                    if g == 0:
                        P.op("dve", n=64, fn=lambda e: e.tensor_scalar(out=imp[:], in0=ps[6][:, 64:128], scalar1=rden[:, 0:1], scalar2=None, op0=ALU.mult),
                             r=[PS[6], DEN], w=[IMP])
                    else:
                        P.op("dve", n=64, fn=lambda e, bk=bk, o0=o0, g=g: e.scalar_tensor_tensor(out=imp[:], in0=ps[bk][:, o0 + 64:o0 + 128], scalar=rden[:, g:g + 1],
                                                                                       in1=imp[:], op0=ALU.mult, op1=ALU.add), r=[PS[bk], DEN], w=[IMP])
                w0 = 63 - 2 * qt
                P.op("dve", n=64, fn=lambda e: e.tensor_tensor(out=sc[:], in0=imp[:], in1=pa[:, w0:w0 + 64], op=ALU.mult), r=[IMP, AC], w=[SC])
                P.op("dve", n=64, fn=lambda e: e.tensor_tensor(out=sc[:], in0=sc[:], in1=pbt[:, w0:w0 + 64], op=ALU.add), r=[SC, AC], w=[SC])
                P.op("dve", n=64, fn=lambda e: e.memset(sc[:, 0:1], 1e4), w=[SC])
                P.op("dve", n=64, fn=lambda e: e.max(out=m8[:], in_=sc[:]), r=[SC], w=[M8])
                P.op("dve", n=64, fn=lambda e: e.tensor_scalar(out=selq[:, 0:64], in0=sc[:], scalar1=m8[:, 7:8], scalar2=1.0, op0=ALU.is_ge, op1=ALU.subtract),
                     r=[SC, M8], w=[SELQ])
                P.op("dve", n=64, fn=lambda e: e.tensor_tensor(out=cf[:, 0:4], in0=rden[:, 0:4], in1=gsig[:, qt, h * 12 + 0:h * 12 + 12:3], op=ALU.mult),
                     r=[DEN, GS[tq4]], w=[DEN])
                for g in range(4):
                    bk = 6 + g // 2
                    o0 = (g % 2) * 129
                    oc = h * 256 + g * 64
                    P.op("dve", n=64, fn=lambda e, bk=bk, o0=o0, g=g, oc=oc: e.tensor_scalar(out=ot_[:, oc:oc + 64], in0=ps[bk][:, o0:o0 + 64],
                                                                                   scalar1=cf[:, g:g + 1], scalar2=None, op0=ALU.mult),
                         r=[PS[bk], DEN], w=[OT_])
                if dbg in ("C1", "C2"):
                    P.op("dve", n=64, fn=lambda e: e.memset(ot_[:, h * 256:h * 256 + 256], 0.0), w=[OT_])
                combine(qt, h, 5, 2)

            def part2(qt, h):
                P.op("pe", n=128, fn=lambda e: e.transpose(out=ps[3][:, 0:128], in_=selq[:], identity=ident_f[:]), r=[SELQ, CONST], w=[PS[3]])
                P.op("dve", n=64, fn=lambda e: e.tensor_copy(out=selT[:], in_=ps[3][:, 0:128]), r=[PS[3]], w=[SELT])

            def part3a(qt, h):
                qb = qt % 2
                for kc in range(qt + 1):
                    masks = [(expb[:, kc * 128:(kc + 1) * 128], bc4(selT[:]), [SELT])]
                    if kc == qt:
                        masks.append((identb[:], bc4(tri[:, 0, :]), []))
                    eb = score_exp(ksP[:, h, kc * 128:(kc + 1) * 128], [KS[kc // 4]], kc, h, qb, masks)

                    def pv_slc(eb=eb, kc=kc, h=h, last=(kc == qt)):
                        for g in range(4):
                            P.op("pe", mm(ps[4][:, g * 65:(g + 1) * 65], et[eb][:, g * 128:(g + 1) * 128], vtok[:, kc, 0, h, :],
                                          kc == 0 and g == 0, last), r=[ET[eb], VT[kc // 4]], w=[PS[4]])
                    push_pv(pv_slc)
                flush()

            def part3b(qt, h):
                combine(qt, h, 4, 1)
                if h == 1:
                    ob_ = qt % 2
                    for c4 in range(4):
                        P.op("pe", lambda e, c4=c4: e.transpose(out=ps[3][:, c4 * 128:(c4 + 1) * 128], in_=otok[ob_][:, c4 * 128:(c4 + 1) * 128], identity=ident_f[:]),
                             r=[OTOK[ob_], CONST], w=[PS[3]])
                    P.op("dve", n=64, fn=lambda e: e.tensor_copy(out=obt[ob_][:], in_=ps[3][:].rearrange("p (a b) -> p a b", a=4)), r=[PS[3]], w=[OBT[ob_]])
                    P.dma("sp", os_d[0:4, :, qt * 128:(qt + 1) * 128].rearrange("c p s -> p c s"), obt[ob_][:], OBT[ob_], r=[OBT[ob_]], w=[OS[qt // 4]])

            iters = [(qt, h) for qt in range(QN) for h in range(2)]
            loadq(0)
            if QN > 1:
                loadq(1)
            part1(*iters[0])
            part2(*iters[0])
            for i, (qt, h) in enumerate(iters):
                nxt = iters[i + 1] if i + 1 < len(iters) else None
                if nxt:
                    part1(*nxt)
                part3a(qt, h)
                if nxt:
                    part2(*nxt)
                part3b(qt, h)
                if h == 1 and qt + 2 < QN:
                    loadq(qt + 2)
            if dbg and dbg.startswith("C"):
                dump(0, otok[0][:], 512, [OTOK[0]])
                dump(1, otok[1][:], 512, [OTOK[1]])
                dump(2, imp[:], 64, [IMP])
                dump(3, sc[:], 64, [SC])
                dump(4, selq[:], 128, [SELQ])
                dump(5, den[:], 12, [DEN])
                dump(6, rden[:], 12, [DEN])
                dump(7, cf[:], 12, [DEN])
            P.barrier()

    def emit_x_out():
        P.push()
        with contextlib.ExitStack() as ph:
            xt = ph.enter_context(sb("xdbg", [128, 8, TT]))
            XT = T("xdbg")
            for tt in range(NT):
                ts = slice(tt * TT, (tt + 1) * TT)
                P.dma("sp", xt[:], xr_d[:, :, ts].rearrange("c p s -> p c s"), XT, r=[XR[tt]], w=[XT])
                P.dma("sp", out_d[:, :, ts].rearrange("c p s -> p c s"), xt[:], XT, r=[XT], w=[OUT])
            P.barrier()
        P.pop()

    def gain(step):
        kind, l = step
        return gffn[:, l, :] if kind == "ffn" else gmix[:, l, :]

    cbos = []
    for jl_ in range(2):
        cbo_ = es.enter_context(sb("cbo%d" % jl_, [128, 8]))
        tb = T("cbo")
        P.dma("sp", cbo_[:], dr["cbout"][:, jl_], tb, w=[tb], perm=True)
        CONST.w.update(tb.w)
        cbos.append(cbo_)
    phase_resid("init", None, 0, None, None, None, gain(plan[0]), False)
    for si, (kind, l) in enumerate(plan):
        jl = l // 2
        is_last = si == len(plan) - 1
        last = is_last and final
        gn = gfin[:, :] if last else (gain(plan[si + 1]) if not is_last else gfin[:, :])
        kc_ = NPAIR if kind == "ffn" else 8
        w_dram = {"even": dr["awout"], "odd": dr["cwout"], "ffn": dr["wdn"]}[kind][jl if kind != "ffn" else l]
        with contextlib.ExitStack() as lay:
            P.push()
            wt = lay.enter_context(sb("wres", [128, kc_, D], BF16))
            WT = T("wres")

            def prefetch(wt=wt, WT=WT, w_dram=w_dram, kc_=kc_):
                half = kc_ // 2
                P.dma("pool", wt[:, :half, :], w_dram[:, :half, :], WT, w=[WT], max_dma_last_dim=4096)
                P.dma("pool", wt[:, half:, :], w_dram[:, half:, :], WT, w=[WT], max_dma_last_dim=4096)
            if kind == "even":
                phase_even(jl, dbg, prefetch)
                if dbg:
                    P.pop()
                    break
                with nc.named_scope("resmix%d" % l):
                    phase_resid("mix", None, 8, os_d, OS, None, gn, last, (wt, WT))
            elif kind == "odd":
                cbo = cbos[jl]
                with nc.named_scope("odd%d" % l):
                    phase_odd(jl, prefetch)
                with nc.named_scope("resmix%d" % l):
                    phase_resid("mix", None, 8, os_d, OS, cbo, gn, last, (wt, WT))
            else:
                with nc.named_scope("ffn%d" % l):
                    phase_ffn1(l, prefetch)
                with nc.named_scope("resffn%d" % l):
                    phase_resid("ffn", None, NPAIR, as_d, AS, None, gn, last, (wt, WT))
            P.pop()
    if not final and not dbg:
        emit_x_out()
    P.barrier()
    es.close()
    return nc


SHAPES = None


def _shapes(shared):
    s = {k: v.shape for k, v in shared.items()}
    s["xT"] = (D, S)
    return s


def kernel(**inputs):
    shared = prep_shared(inputs)
    x = np.asarray(inputs["x"], dtype=np.float32)
    nb = x.shape[0]
    nc = build(_shapes(shared))
    in_maps = []
    for b in range(nb):
        m = dict(shared)
        m["xT"] = _c(x[b].T)
        in_maps.append(m)
    res = run_bass_kernel_spmd(nc, in_maps, core_ids=list(range(nb)))
    out = np.stack([np.asarray(r["out"]).reshape(D, S).T for r in res.results], axis=0)
    return np.ascontiguousarray(out.astype(np.float32))
```

```python
import contextlib
import os
import numpy as np
import concourse.bass as bass
import concourse.mybir as mybir
from concourse.bass_utils import run_bass_kernel_spmd

F32 = mybir.dt.float32
BF16 = mybir.dt.bfloat16
ALU = mybir.AluOpType
AF = mybir.ActivationFunctionType

S = 4096
D = 1024
NT = 8
TT = 512
DFF = 2816
NPAIR = 22
BIG = 30000.0
EPS = 1e-6


class T:
    __slots__ = ("name", "w", "r", "ds", "dram")

    def __init__(self, name="", dram=False):
        self.name = name
        self.w = {}
        self.r = {}
        self.ds = None
        self.dram = dram
        if not dram:
            _SCOPES[-1].append(self)


_SCOPES = [[]]


class Prog:
    def __init__(self, nc, es, n_dsem=72):
        self.nc = nc
        self.E = {"pe": nc.tensor, "dve": nc.vector, "act": nc.scalar, "pool": nc.gpsimd, "sp": nc.sync}
        self.sem = {k: es.enter_context(nc.semaphore("e_" + k)) for k in self.E}
        self.cnt = {k: 0 for k in self.E}
        self.seen = {k: {} for k in self.E}
        self.dpool = [[es.enter_context(nc.semaphore("d%d" % i)), 0] for i in range(n_dsem)]
        self.dperm = set()
        self.dfree = list(range(n_dsem))

    def dsem(self, t, perm=False):
        if t.ds is None:
            t.ds = self.dfree.pop()
            if perm:
                self.dperm.add(t.ds)
        return self.dpool[t.ds]

    def _waits(self, eng, r, w):
        need = {}
        seen = self.seen[eng]

        def add(evs):
            for key, (sem, val, e, small) in evs.items():
                if e == eng and not (small and self.cnt[eng] - val < 4):
                    continue
                if seen.get(key, 0) >= val:
                    continue
                if key not in need or need[key][1] < val:
                    need[key] = (sem, val)

        for t in r:
            add(t.w)
        for t in w:
            if not t.dram:
                add(t.w)
            add(t.r)
        for key, (sem, val) in need.items():
            self.E[eng].wait_ge(sem, val)
            seen[key] = val

    def op(self, eng, fn, r=(), w=(), n=512):
        self._waits(eng, r, w)
        ins = fn(self.E[eng])
        self.cnt[eng] += 1
        ins.then_inc(self.sem[eng], 1)
        key = "e_" + eng
        ev = (self.sem[eng], self.cnt[eng], eng, n < 256)
        for t in w:
            t.w = {key: ev}
            t.r = {}
        for t in r:
            t.r[key] = ev

    def dma(self, q, out, in_, sb, r=(), w=(), perm=False, **kw):
        self._waits(q, r, w)
        d = self.dsem(sb, perm)
        d[1] += 16
        self.E[q].dma_start(out=out, in_=in_, **kw).then_inc(d[0], 16)
        key = "d%d" % sb.ds
        ev = (d[0], d[1], "dma", False)
        for t in w:
            if t.dram:
                if t.r:
                    t.w = {}
                    t.r = {}
                t.w[key] = ev
            else:
                t.w = {key: ev}
                t.r = {}
        for t in r:
            t.r[key] = ev

    def barrier(self):
        sp = self.E["sp"]
        seen = self.seen["sp"]
        for k in self.E:
            if k != "sp" and self.cnt[k] > seen.get("e_" + k, 0):
                sp.wait_ge(self.sem[k], self.cnt[k])
        for i, (s, c) in enumerate(self.dpool):
            if c > seen.get("d%d" % i, 0):
                sp.wait_ge(s, c)
        self.cnt["sp"] += 1
        sp.sem_inc(self.sem["sp"], 1)
        for k in self.E:
            if k != "sp":
                self.E[k].wait_ge(self.sem["sp"], self.cnt["sp"])
            sn = self.seen[k]
            for k2 in self.E:
                sn["e_" + k2] = self.cnt[k2]
            for i, (s, c) in enumerate(self.dpool):
                sn["d%d" % i] = c

    def push(self):
        _SCOPES.append([])

    def pop(self):
        for t in _SCOPES.pop():
            if t.ds is not None and t.ds not in self.dperm:
                self.dfree.append(t.ds)
                t.ds = None


def _c(a):
    return np.ascontiguousarray(a, dtype=np.float32)


def _vec(a):
    a = np.asarray(a, dtype=np.float32)
    lead = a.shape[:-1]
    c = a.shape[-1] // 128
    a = a.reshape(lead + (c, 128))
    return _c(np.moveaxis(a, -1, 0))


def _wchunks(w, cols):
    k = w.shape[0]
    out = np.empty((len(cols), 128, k // 128, 128), np.float32)
    for i, ci in enumerate(cols):
        out[i] = w[:, ci].reshape(k // 128, 128, 128).transpose(1, 0, 2)
    return out


def _rows(w):
    k, n = w.shape
    return _c(w.reshape(k // 128, 128, n).transpose(1, 0, 2))


def make_consts():
    c = {}
    inv = 1.0 / (10000.0 ** (np.arange(0, 64, 2, dtype=np.float32) / 64.0))
    ang = np.arange(S, dtype=np.float32)[:, None] * inv[None, :]
    ang = np.concatenate([ang, ang], axis=-1).astype(np.float32)
    cos = np.cos(ang).astype(np.float32).T
    sin = np.sin(ang).astype(np.float32).T
    sgn = np.where(np.arange(64) < 32, -1.0, 1.0).astype(np.float32)[:, None]
    c["k_cos"] = _c(np.concatenate([cos, cos], 0))
    c["k_sin"] = _c(np.concatenate([sin * sgn, sin * sgn], 0))
    psw = np.zeros((128, 128), np.float32)
    for cp in range(128):
        base = (cp // 64) * 64
        psw[base + ((cp % 64) + 32) % 64, cp] = 1.0
    c["k_psw"] = psw
    c["k_ident"] = np.eye(128, dtype=np.float32)
    c["k_ones"] = np.ones((128, 128), np.float32)
    expb = np.zeros((128, S), np.float32)
    kk = np.arange(S)
    expb[kk // 64, kk] = BIG
    c["k_expb"] = expb
    j = np.arange(128)[:, None]
    i = np.arange(128)[None, :]
    tri = np.zeros((128, 2, 128), np.float32)
    tri[:, 0, :] = np.where(j > i, -BIG, 0.0)
    tri[:, 1, :] = np.where(j <= i, -BIG, 0.0)
    c["k_tri"] = tri
    shw = np.zeros((128, 512), np.float32)
    r = np.arange(512) - 256
    shw[0, r <= -2] = 1.0
    for jj in range(1, 9):
        shw[jj, r == jj - 2] = 1.0
    shw[9, r >= 7] = 1.0
    c["k_shw"] = shw
    pat = np.zeros((128, 128), np.float32)
    ii = np.arange(128)
    for jj in range(1, 9):
        pat[jj] = np.where(ii >= 31 + 16 * (jj - 2), 0.0, -BIG)
    pat[9] = -BIG
    c["k_pat"] = pat
    n = np.arange(256)[:, None]
    b = np.arange(64)[None, :]
    ov = ((16 * n <= 64 * b + 63) & (16 * n + 31 >= 64 * b)).astype(np.float32)
    ovc = np.zeros((128, 2, 65), np.float32)
    for jc in range(2):
        ovc[:, jc, :64] = ov[jc * 128:(jc + 1) * 128]
        ovc[:, jc, 64] = 1.0
    c["k_ovc"] = ovc
    pa = np.zeros((128, 128), np.float32)
    pb = np.zeros((128, 128), np.float32)
    for q in range(128):
        cr = q // 64
        for cc in range(127):
            rr = cc - 63
            if rr <= cr - 2:
                pa[q, cc] = 1.0
            elif rr <= cr:
                pb[q, cc] = 1e4
            else:
                pb[q, cc] = -1e4
    c["k_pa"] = pa
    c["k_pb"] = pb
    return c


def prep_shared(inp):
    g = {}
    f = lambda k: np.asarray(inp[k], dtype=np.float32)
    g["gmix"] = _vec(f("norm_mix"))
    g["gffn"] = _vec(f("norm_ffn"))
    g["gfin"] = _vec(f("norm_final"))
    wup = f("f_w_up")
    cols = [np.arange(j * 128, (j + 1) * 128) for j in range(44)]
    g["wup"] = np.stack([_wchunks(wup[l], cols) for l in range(4)])
    g["fcw"] = _c(np.moveaxis(f("f_conv_w").reshape(4, 3, 44, 128), 3, 0).transpose(0, 1, 3, 2))
    g["fcb"] = _vec(f("f_conv_b"))
    g["wdn"] = np.stack([_rows(f("f_w_down")[l]) for l in range(4)])
    cw = f("c_w_in")
    ccols = [np.concatenate([np.arange(n * 128, (n + 1) * 128), 1024 + np.arange(n * 128, (n + 1) * 128)]) for n in range(8)]
    cwin = np.empty((2, 8, 128, 8, 256), np.float32)
    for l in range(2):
        for n in range(8):
            cwin[l, n] = cw[l][:, ccols[n]].reshape(8, 128, 256).transpose(1, 0, 2)
    g["cwin"] = cwin
    g["cbin"] = _vec(f("c_b_in"))
    g["ccw"] = _c(np.moveaxis(f("c_conv_w").reshape(2, 4, 8, 128), 3, 0).transpose(0, 1, 3, 2))
    g["ccb"] = _vec(f("c_conv_b"))
    g["cwa"] = _c(f("c_w_a"))
    g["cwi"] = _c(f("c_w_i"))
    g["cba"] = _vec(f("c_b_a"))
    g["cbi"] = _vec(f("c_b_i"))
    g["clam"] = _vec(f("c_lambda"))
    g["cwout"] = np.stack([_rows(f("c_w_out")[l]) for l in range(2)])
    g["cbout"] = _vec(f("c_b_out"))
    aw = f("a_w_in")
    acols = []
    for i in range(4):
        acols.append(np.concatenate([np.arange(i * 64, (i + 1) * 64), np.arange((4 + i) * 64, (5 + i) * 64)]))
    for t in (0, 1, 2, 4):
        acols.append(512 + t * 128 + np.arange(128))
    for t in range(12):
        acols.append(1304 + t * 128 + np.arange(128))
    g["awin"] = np.stack([_wchunks(aw[l], acols) for l in range(2)])
    vcols = np.concatenate([512 + 3 * 128 + np.arange(128), 512 + 5 * 128 + np.arange(128), 1280 + np.arange(24)])
    g["awv"] = np.stack([_rows(aw[l][:, vcols]) for l in range(2)])
    g["acw"] = _c(np.moveaxis(f("a_conv_w").reshape(2, 3, 4, 128), 3, 0).transpose(0, 1, 3, 2))
    w1 = f("a_cmp_w1").reshape(2, 2, 32, 64, 128).transpose(0, 1, 3, 2, 4)
    g["aw1"] = _c(np.concatenate([w1, w1], axis=2))
    pe = f("a_cmp_pe").transpose(0, 1, 3, 2)
    g["ape"] = _c(np.concatenate([pe, np.zeros_like(pe)], axis=2))
    g["aw2"] = _c(f("a_cmp_w2"))
    g["awout"] = np.stack([_rows(f("a_w_out")[l]) for l in range(2)])
    g.update(make_consts())
    return {k: _c(v) for k, v in g.items()}


FULL_PLAN = [("even", 0), ("ffn", 0), ("odd", 1), ("ffn", 1), ("even", 2), ("ffn", 2), ("odd", 3), ("ffn", 3)]


def build(shapes, plan=None, final=True, dbg=None):
    plan = plan or FULL_PLAN
    nc = bass.Bass("TRN2", target_bir_lowering=False)
    es = contextlib.ExitStack()
    P = Prog(nc, es)
    dr = {}
    for name, shp in shapes.items():
        dr[name] = nc.dram_tensor(name, list(shp), F32, kind="ExternalInput").ap()
    out_d = nc.dram_tensor("out", [NT, 128, 8, TT], F32, kind="ExternalOutput").ap()
    xr_d = nc.dram_tensor("xr", [NT, 128, 8, TT], F32, kind="Internal").ap()
    os_d = nc.dram_tensor("osc", [NT, 128, 8, TT], BF16, kind="Internal").ap()
    as_d = nc.dram_tensor("asc", [NT, 128, NPAIR, TT], BF16, kind="Internal").ap()
    qs_d = nc.dram_tensor("qsc", [4, 128, S], BF16, kind="Internal").ap()
    x_d = dr["xT"]
    XR = [T("xr%d" % t, dram=True) for t in range(NT)]
    OS = [T("os%d" % t, dram=True) for t in range(NT)]
    AS = [T("as%d" % t, dram=True) for t in range(NT)]
    QS = [T("qs%d" % t, dram=True) for t in range(NT)]
    XIN = T("xin", dram=True)
    OUT = T("out", dram=True)

    ps = [es.enter_context(nc.psum_tensor("ps%d" % i, [128, 512], F32)) for i in range(8)]
    PS = [T("ps%d" % i) for i in range(8)]

    uid = [0]

    def sb(name, shape, dt=F32):
        uid[0] += 1
        return nc.sbuf_tensor("s%d_%s" % (uid[0], name), list(shape), dt)

    hT = es.enter_context(sb("hT", [128, 8, S], BF16))
    HT = [[T("h%d_%d" % (c, t)) for t in range(NT)] for c in range(8)]
    ones_f = es.enter_context(sb("ones_f", [128, 128]))
    ident_f = es.enter_context(sb("ident_f", [128, 128]))
    gmix = es.enter_context(sb("gmix", [128, 4, 8]))
    gffn = es.enter_context(sb("gffn", [128, 4, 8]))
    gfin = es.enter_context(sb("gfin", [128, 8]))
    CONST = T("const")
    for t_, nm in ((ones_f, "k_ones"), (ident_f, "k_ident"), (gmix, "gmix"), (gffn, "gffn"), (gfin, "gfin")):
        tt_ = T(nm)
        P.dma("sp", t_[:], dr[nm], tt_, w=[tt_], perm=True)
        CONST.w.update(tt_.w)

    def mm(out, lhsT, rhs, start, stop):
        return lambda e: e.matmul(out, lhsT, rhs, start=start, stop=stop, skip_group_check=True)

    def phase_resid(mode, w_dram, kc, o_dram, OT, bias_ap, g_ap, final, w_pre=None):
        P.push()
        with contextlib.ExitStack() as ph:
            xt = [ph.enter_context(sb("xt%d" % i, [128, 8, TT])) for i in range(2)]
            XT = [[T("xt%d_%d" % (i, m)) for m in range(8)] for i in range(2)]
            sq = [ph.enter_context(sb("sq%d" % i, [128, TT], BF16)) for i in range(2)]
            SQ = [T("sq%d" % i) for i in range(2)]
            rt = ph.enter_context(sb("rt", [128, TT]))
            RT = T("rt")
            rstd = [ph.enter_context(sb("rstd%d" % i, [128, TT])) for i in range(2)]
            RSTD = [T("rstd%d" % i) for i in range(2)]
            pend_tiles = []
            if mode != "init":
                wt, WT = w_pre
                nob = 2
                ot = [ph.enter_context(sb("ot%d" % i, [128, kc, TT], BF16)) for i in range(nob)]
                OTL = [T("ot%d" % i) for i in range(nob)]
            for tt in range(NT):
                ts = slice(tt * TT, (tt + 1) * TT)
                b = tt % 2
                if mode == "init":
                    P.dma("sp", xt[b][:], x_d[tt], XT[b][0], r=[XIN], w=XT[b])
                else:
                    ob = tt % nob
                    if not (os.environ.get("NOOT") and tt > 1):
                        P.dma("sp", ot[ob][:], o_dram[tt], OTL[ob], r=[OT[tt]], w=[OTL[ob]])
                    P.dma("sp", xt[b][:], xr_d[tt], XT[b][0], r=[XR[tt]], w=XT[b])
                def stat(m):
                    sb_ = m % 2
                    P.op("pe", mm(ps[4][:], ones_b[:], sq[sb_][:], m == 0, m == 7), r=[SQ[sb_], CONST], w=[PS[4]])

                for m in range(8):
                    if mode != "init":
                        pb_ = m % 4
                        for k in range(kc):
                            P.op("pe", mm(ps[pb_][:], wt[:, k, m * 128:(m + 1) * 128], ot[ob][:, k, :], k == 0, k == kc - 1),
                                 r=[WT, OTL[ob]], w=[PS[pb_]])
                        bsc = bias_ap[:, m:m + 1] if bias_ap is not None else 0.0
                        P.op("dve", lambda e, m=m, pb_=pb_, bsc=bsc: e.scalar_tensor_tensor(
                            out=xt[b][:, m, :], in0=ps[pb_][:], scalar=bsc, in1=xt[b][:, m, :], op0=ALU.add, op1=ALU.add),
                            r=[PS[pb_], CONST], w=[XT[b][m]])
                    sb_ = m % 2
                    P.op("act", lambda e, m=m, sb_=sb_: e.activation(out=sq[sb_][:], in_=xt[b][:, m, :], func=AF.Square),
                         r=[XT[b][m]], w=[SQ[sb_]])
                    if pend_tiles:
                        pend_tiles[-1](m)
                    if m >= 1:
                        stat(m - 1)
                stat(7)
                P.op("act", lambda e: e.activation(out=rt[:], in_=ps[4][:], func=AF.Sqrt, scale=1.0 / D, bias=eps_t[:, 0:1]),
                     r=[PS[4], CONST], w=[RT])
                P.op("dve", lambda e, b=b: e.reciprocal(out=rstd[b][:], in_=rt[:]), r=[RT], w=[RSTD[b]])
                if not final:
                    P.dma("pool", xr_d[tt], xt[b][:], XT[b][0], r=XT[b], w=[XR[tt]])

                def hop(m, b=b, ts=ts, tt=tt):
                    if not final:
                        P.op("dve", lambda e: e.scalar_tensor_tensor(
                            out=hT[:, m, ts], in0=xt[b][:, m, :], scalar=g_ap[:, m:m + 1], in1=rstd[b][:], op0=ALU.mult, op1=ALU.mult),
                            r=[XT[b][m], RSTD[b], CONST], w=[HT[m][tt]])
                    else:
                        P.op("dve", lambda e: e.scalar_tensor_tensor(
                            out=xt[b][:, m, :], in0=xt[b][:, m, :], scalar=g_ap[:, m:m + 1], in1=rstd[b][:], op0=ALU.mult, op1=ALU.mult),
                            r=[XT[b][m], RSTD[b], CONST], w=[XT[b][m]])
                        if m == 7:
                            P.dma("pool", out_d[tt], xt[b][:], XT[b][0], r=XT[b], w=[OUT])
                pend_tiles.append(hop)
            for m in range(8):
                pend_tiles[-1](m)
            P.barrier()
        P.pop()

    ones_b = es.enter_context(sb("ones_b", [128, 128], BF16))
    P.op("dve", lambda e: e.memset(ones_b[:], 1.0), w=[CONST])
    eps_t = es.enter_context(sb("eps_t", [128, 1]))
    one_t = es.enter_context(sb("one_t", [128, 1]))
    P.op("dve", lambda e: e.memset(eps_t[:], EPS), w=[CONST])
    P.op("dve", lambda e: e.memset(one_t[:], 1.0), w=[CONST])

    def phase_ffn1(l, prefetch=None):
        P.push()
        with contextlib.ExitStack() as ph:
            cw = ph.enter_context(sb("fcw", [128, 44, 3]))
            cb = ph.enter_context(sb("fcb", [128, 44]))
            CW = T("fcw")
            P.dma("sp", cw[:], dr["fcw"][:, l], CW, w=[CW])
            P.dma("sp", cb[:], dr["fcb"][:, l], CW, w=[CW])
            wg = [ph.enter_context(sb("wg%d" % i, [128, 2, 8, 128], BF16)) for i in range(2)]
            WG = [T("wg%d" % i) for i in range(2)]
            U = [[ph.enter_context(sb("u%d_%d" % (k, i), [128, TT + 2])) for i in range(2)] for k in range(2)]
            UT = [[T("u%d_%d" % (k, i)) for i in range(2)] for k in range(2)]
            t1 = [ph.enter_context(sb("t1_%d" % k, [128, TT])) for k in range(2)]
            T1 = [T("t1_%d" % k) for k in range(2)]
            y = [ph.enter_context(sb("y_%d" % k, [128, TT])) for k in range(2)]
            Y = [T("y_%d" % k) for k in range(2)]
            sg = ph.enter_context(sb("sg", [128, TT]))
            SG = T("sg")
            ao = [ph.enter_context(sb("ao%d" % i, [128, TT], BF16)) for i in range(3)]
            AO = [T("ao%d" % i) for i in range(3)]

            def loadw(c):
                b = c % 2
                P.dma("pool", wg[b][:, 0], dr["wup"][l, c], WG[b], w=[WG[b]])
                P.dma("pool", wg[b][:, 1], dr["wup"][l, NPAIR + c], WG[b], w=[WG[b]])

            loadw(0)
            it = 0
            for c in range(NPAIR):
                if c + 1 < NPAIR:
                    loadw(c + 1)
                if c == 0 and prefetch:
                    prefetch()
                wb = c % 2
                for tt in range(NT):
                    ts = slice(tt * TT, (tt + 1) * TT)
                    ub = tt % 2
                    for k in range(2):
                        pb_ = (2 * it + k) % 4
                        j = c + k * NPAIR
                        for kk in range(8):
                            P.op("pe", mm(ps[pb_][:], wg[wb][:, k, kk, :], hT[:, kk, ts], kk == 0, kk == 7),
                                 r=[WG[wb], HT[kk][tt]], w=[PS[pb_]])
                        u = U[k][ub]
                        if tt == 0:
                            P.op("pool", lambda e, u=u: e.memset(u[:, 0:2], 0.0), w=[UT[k][ub]])
                        else:
                            pu = U[k][1 - ub]
                            P.op("pool", lambda e, u=u, pu=pu: e.tensor_copy(out=u[:, 0:2], in_=pu[:, TT:TT + 2]),
                                 r=[UT[k][1 - ub]], w=[UT[k][ub]])
                        P.op("act", lambda e, u=u, pb_=pb_: e.activation(out=u[:, 2:TT + 2], in_=ps[pb_][:], func=AF.Identity),
                             r=[PS[pb_]], w=[UT[k][ub]])
                        P.op("act", lambda e, pb_=pb_, j=j, k=k: e.activation(out=t1[k][:], in_=ps[pb_][:], func=AF.Identity,
                                                                            scale=cw[:, j, 2:3], bias=cb[:, j:j + 1]),
                             r=[PS[pb_], CW], w=[T1[k]])
                        P.op("dve", lambda e, u=u, j=j, k=k: e.scalar_tensor_tensor(
                            out=t1[k][:], in0=u[:, 1:TT + 1], scalar=cw[:, j, 1:2], in1=t1[k][:], op0=ALU.mult, op1=ALU.add),
                            r=[UT[k][ub], CW], w=[T1[k]])
                        P.op("dve", lambda e, u=u, j=j, k=k: e.scalar_tensor_tensor(
                            out=y[k][:], in0=u[:, 0:TT], scalar=cw[:, j, 0:1], in1=t1[k][:], op0=ALU.mult, op1=ALU.add),
                            r=[UT[k][ub], T1[k], CW], w=[Y[k]])
                    P.op("act", lambda e: e.activation(out=sg[:], in_=y[0][:], func=AF.Silu), r=[Y[0]], w=[SG])
                    ab = it % 3
                    P.op("dve", lambda e, ab=ab: e.tensor_tensor(out=ao[ab][:], in0=sg[:], in1=y[1][:], op=ALU.mult),
                         r=[SG, Y[1]], w=[AO[ab]])
                    P.dma("sp", as_d[tt, :, c, :], ao[ab][:], AO[ab], r=[AO[ab]], w=[AS[tt]])
                    it += 1
            P.barrier()
        P.pop()

    def phase_odd(jl, prefetch=None):
        P.push()
        with contextlib.ExitStack() as ph:
            def vt(name, shape):
                return ph.enter_context(sb(name, shape))
            bin_ = vt("cbin", [128, 16]); ccw = vt("ccw", [128, 8, 4]); ccb = vt("ccb", [128, 8])
            cba = vt("cba", [128, 8]); cbi = vt("cbi", [128, 8]); lam = vt("clam", [128, 8]); cl = vt("cl", [128, 8]); bhy = vt("bhy", [128, 8])
            CV = T("cvec")
            for t_, nm in ((bin_, "cbin"), (ccw, "ccw"), (ccb, "ccb"), (cba, "cba"), (cbi, "cbi"), (lam, "clam")):
                P.dma("sp", t_[:], dr[nm][:, jl], CV, w=[CV])
            P.op("act", lambda e: e.activation(out=cl[:], in_=lam[:], func=AF.Exp, scale=-1.0), r=[CV], w=[CV], n=8)
            P.op("act", lambda e: e.activation(out=cl[:], in_=cl[:], func=AF.Ln, bias=one_t[:, 0:1]), r=[CV], w=[CV], n=8)
            P.op("dve", lambda e: e.tensor_scalar(out=cl[:], in0=cl[:], scalar1=-4.0, scalar2=None, op0=ALU.mult), r=[CV], w=[CV], n=8)
            P.op("dve", lambda e: e.tensor_scalar(out=cba[:], in0=cba[:], scalar1=0.5, scalar2=None, op0=ALU.mult), r=[CV], w=[CV], n=8)
            P.op("dve", lambda e: e.tensor_scalar(out=cbi[:], in0=cbi[:], scalar1=0.5, scalar2=None, op0=ALU.mult), r=[CV], w=[CV], n=8)
            P.op("dve", lambda e: e.tensor_scalar(out=bhy[:], in0=bin_[:, 0:8], scalar1=0.5, scalar2=None, op0=ALU.mult), r=[CV], w=[CV], n=8)
            win = [ph.enter_context(sb("cwin%d" % i, [128, 8, 256], BF16)) for i in range(2)]
            WIN = [T("cwin%d" % i) for i in range(2)]
            wai = [ph.enter_context(sb("cwai%d" % i, [128, 2, 128])) for i in range(2)]
            WAI = [T("cwai%d" % i) for i in range(2)]
            U = [vt("cu%d" % i, [128, TT + 3]) for i in range(2)]
            UT = [T("cu%d" % i) for i in range(2)]
            xs = [vt("xs%d" % i, [128, TT]) for i in range(3)]; XS = [T("xs%d" % i) for i in range(3)]
            x2l = [vt("x2_%d" % i, [128, TT]) for i in range(2)]; X2L = [T("x2_%d" % i) for i in range(2)]
            sgm = [vt("sgm%d" % i, [128, TT]) for i in range(2)]; SGM = [T("sgm%d" % i) for i in range(2)]
            uu = [vt("uu%d" % i, [128, TT]) for i in range(2)]; UU = [T("uu%d" % i) for i in range(2)]
            rr = vt("rr", [128, TT]); RR = T("rr")
            ig = vt("ig", [128, TT]); IG = T("ig")
            aa = vt("aa", [128, TT]); AA = T("aa")
            mmul = vt("mmul", [128, TT]); MM = T("mmul")
            hs = [vt("hs%d" % i, [128, TT]) for i in range(2)]
            HS = [T("hs%d" % i) for i in range(2)]
            oo = [ph.enter_context(sb("oo%d" % i, [128, TT], BF16)) for i in range(3)]
            OO = [T("oo%d" % i) for i in range(3)]

            def loadw(n):
                b = n % 2
                P.dma("pool", win[b][:], dr["cwin"][jl, n], WIN[b], w=[WIN[b]])
                P.dma("sp", wai[b][:, 0, :], dr["cwa"][jl, n], WAI[b], w=[WAI[b]])
                P.dma("sp", wai[b][:, 1, :], dr["cwi"][jl, n], WAI[b], w=[WAI[b]])

            def stage1(it, n, tt):
                ts = slice(tt * TT, (tt + 1) * TT)
                wb = n % 2
                ub = tt % 2
                db = it % 2
                xb = it % 3
                x2 = x2l[db]
                X2 = X2L[db]
                py, pu_ = (2 * it) % 4, (2 * it + 1) % 4
                for k, pb_ in ((0, py), (1, pu_)):
                    for kk in range(8):
                        P.op("pe", mm(ps[pb_][:], win[wb][:, kk, k * 128:(k + 1) * 128], hT[:, kk, ts], kk == 0, kk == 7),
                             r=[WIN[wb], HT[kk][tt]], w=[PS[pb_]])
                P.op("act", lambda e: e.activation(out=xs[xb][:], in_=ps[py][:], func=AF.Identity, scale=0.5, bias=bhy[:, n:n + 1]),
                     r=[PS[py], CV], w=[XS[xb]])
                P.op("act", lambda e: e.activation(out=x2[:], in_=ps[py][:], func=AF.Square, bias=bin_[:, n:n + 1]),
                     r=[PS[py], CV], w=[X2])
                u = U[ub]
                if tt == 0:
                    P.op("pool", lambda e: e.memset(u[:, 0:3], 0.0), w=[UT[ub]])
                else:
                    pu = U[1 - ub]
                    P.op("pool", lambda e: e.tensor_copy(out=u[:, 0:3], in_=pu[:, TT:TT + 3]), r=[UT[1 - ub]], w=[UT[ub]])
                P.op("act", lambda e: e.activation(out=u[:, 3:TT + 3], in_=ps[pu_][:], func=AF.Identity, bias=bin_[:, 8 + n:9 + n]),
                     r=[PS[pu_], CV], w=[UT[ub]])

            def stage2(it, n, tt):
                ts = slice(tt * TT, (tt + 1) * TT)
                wb = n % 2
                ub = tt % 2
                db = it % 2
                xb = it % 3
                x2 = x2l[db]
                X2 = X2L[db]
                u = U[ub]
                P.op("dve", lambda e: e.tensor_scalar(out=x2[:], in0=x2[:], scalar1=0.044715, scalar2=1.0, op0=ALU.mult, op1=ALU.add),
                     r=[X2], w=[X2])
                P.op("dve", lambda e: e.tensor_tensor(out=x2[:], in0=x2[:], in1=xs[xb][:], op=ALU.mult), r=[X2, XS[xb]], w=[X2])
                P.op("dve", lambda e: e.tensor_scalar(out=uu[db][:], in0=u[:, 3:TT + 3], scalar1=ccw[:, n, 3:4], scalar2=ccb[:, n:n + 1],
                                                      op0=ALU.mult, op1=ALU.add), r=[UT[ub], CV], w=[UU[db]])
                for j in range(3):
                    P.op("dve", lambda e, j=j: e.scalar_tensor_tensor(out=uu[db][:], in0=u[:, j:TT + j], scalar=ccw[:, n, j:j + 1], in1=uu[db][:],
                                                                      op0=ALU.mult, op1=ALU.add), r=[UT[ub], CV], w=[UU[db]])

                P.op("act", lambda e: e.activation(out=sgm[db][:], in_=x2[:], func=AF.Tanh, scale=1.5957691216), r=[X2], w=[SGM[db]])
                pr, pi_ = 4 + (2 * it) % 4, 4 + (2 * it + 1) % 4
                P.op("pe", mm(ps[pr][:], wai[wb][:, 0, :], uu[db][:], True, True), r=[WAI[wb], UU[db]], w=[PS[pr]])
                P.op("pe", mm(ps[pi_][:], wai[wb][:, 1, :], uu[db][:], True, True), r=[WAI[wb], UU[db]], w=[PS[pi_]])

            def stage3(it, n, tt):
                ts = slice(tt * TT, (tt + 1) * TT)
                wb = n % 2
                db = it % 2
                xb = it % 3
                pr, pi_ = 4 + (2 * it) % 4, 4 + (2 * it + 1) % 4
                P.op("act", lambda e: e.activation(out=rr[:], in_=ps[pr][:], func=AF.Tanh, scale=0.5, bias=cba[:, n:n + 1]), r=[PS[pr], CV], w=[RR])
                P.op("act", lambda e: e.activation(out=ig[:], in_=ps[pi_][:], func=AF.Tanh, scale=0.5, bias=cbi[:, n:n + 1]), r=[PS[pi_], CV], w=[IG])
                P.op("act", lambda e: e.activation(out=aa[:], in_=rr[:], func=AF.Exp, scale=cl[:, n:n + 1], bias=cl[:, n:n + 1]), r=[RR, CV], w=[AA])
                P.op("act", lambda e: e.activation(out=mmul[:], in_=aa[:], func=AF.Square), r=[AA], w=[MM])
                P.op("act", lambda e: e.activation(out=mmul[:], in_=mmul[:], func=AF.Sqrt, scale=-1.0, bias=one_t[:, 0:1]), r=[MM], w=[MM])
                P.op("dve", lambda e: e.scalar_tensor_tensor(out=ig[:], in0=ig[:], scalar=1.0, in1=uu[db][:], op0=ALU.add, op1=ALU.mult), r=[IG, UU[db]], w=[IG])
                P.op("dve", lambda e: e.scalar_tensor_tensor(out=ig[:], in0=ig[:], scalar=0.5, in1=mmul[:], op0=ALU.mult, op1=ALU.mult), r=[IG, MM], w=[IG])
                hb = tt % 2
                init = 0.0 if tt == 0 else hs[1 - hb][:, TT - 1:TT]
                rl = [AA, IG] + ([HS[1 - hb]] if tt else [])
                P.op("dve", lambda e: e.tensor_tensor_scan(out=hs[hb][:], data0=aa[:], data1=ig[:], initial=init,
                                                           op0=ALU.mult, op1=ALU.add), r=rl, w=[HS[hb]])
                P.op("dve", lambda e: e.tensor_tensor(out=xs[xb][:], in0=xs[xb][:], in1=hs[hb][:], op=ALU.mult), r=[XS[xb], HS[hb]], w=[XS[xb]])
                ob = it % 3
                P.op("dve", lambda e: e.scalar_tensor_tensor(out=oo[ob][:], in0=sgm[db][:], scalar=1.0, in1=xs[xb][:], op0=ALU.add, op1=ALU.mult),
                     r=[XS[xb], SGM[db]], w=[OO[ob]])
                P.dma("sp", os_d[tt, :, n, :], oo[ob][:], OO[ob], r=[OO[ob]], w=[OS[tt]])

            loadw(0)
            if prefetch:
                prefetch()
            its = [(n, tt) for n in range(8) for tt in range(NT)]
            NI = len(its)
            for k in range(NI + 2):
                if k < NI:
                    stage1(k, *its[k])
                if 1 <= k <= NI:
                    stage2(k - 1, *its[k - 1])
                if 2 <= k:
                    stage3(k - 2, *its[k - 2])
                if k >= 2 and k - 2 < NI and its[k - 2][1] == NT - 1 and its[k - 2][0] + 2 < 8:
                    loadw(its[k - 2][0] + 2)
                if k == 0:
                    loadw(1)
            P.barrier()
        P.pop()

    def dump(idx, ap, n, TL):
        P.push()
        with contextlib.ExitStack() as dd:
            stg = dd.enter_context(sb("stg", [128, n]))
            STG = T("stg")
            P.op("dve", lambda e: e.tensor_copy(out=stg[:, 0:n], in_=ap), r=TL, w=[STG])
            P.dma("sp", out_d[idx].rearrange("p c s -> p (c s)")[:, 0:n], stg[:, 0:n], STG, r=[STG], w=[OUT])
            P.barrier()
        P.pop()

    def phase_even(jl, dbg=None, prefetch=None):
        P.push()
        try:
            _phase_even(jl, dbg, prefetch)
        finally:
            P.pop()

    def _phase_even(jl, dbg=None, prefetch=None):
        rot = {"s": 0}

        def sbank():
            rot["s"] = (rot["s"] + 1) % 3
            return rot["s"]

        with contextlib.ExitStack() as ph:
            def vt(name, shape, dt=F32):
                return ph.enter_context(sb(name, shape, dt))
            psw = vt("psw", [128, 128]); PSW = T("psw")
            P.dma("sp", psw[:], dr["k_psw"], PSW, w=[PSW])
            kcP = vt("kcP", [128, 2, 256], BF16); KCP = T("kcP")
            rc = vt("rc", [128, 2, 2, 129], BF16); RC = T("rc")
            P.op("dve", lambda e: e.memset(kcP[:], 0.0), w=[KCP])
            P.op("dve", lambda e: e.memset(rc[:], 0.0), w=[RC])
            for h in range(2):
                P.dma("pool", rc[:, h, :, 64:129], dr["k_ovc"], RC, r=[], w=[RC])
            cs = [vt("cs%d" % i, [128, 2, TT]) for i in range(2)]
            CS = [T("cs%d" % i) for i in range(2)]
            wq = [vt("wq%d" % i, [128, 8, 128], BF16) for i in range(3)]
            WQ = [T("wq%d" % i) for i in range(3)]
            zt = [vt("zt%d" % i, [128, TT]) for i in range(2)]
            ZT = [T("zt%d" % i) for i in range(2)]
            r1 = vt("r1", [128, TT]); R1 = T("r1")
            r2 = vt("r2", [128, TT]); R2 = T("r2")
            wcnt = {"n": 0, "it": 0}

            def inproj(chunk_ids, post, group=1):
                ng = len(chunk_ids) // group

                def loadw(gi):
                    bl = []
                    for k in range(group):
                        b = wcnt["n"] % 3
                        wcnt["n"] += 1
                        P.dma("pool", wq[b][:], dr["awin"][jl, chunk_ids[gi * group + k]], WQ[b], w=[WQ[b]])
                        bl.append(b)
                    return bl
                cur = loadw(0)
                for gi in range(ng):
                    nxt = loadw(gi + 1) if (gi + 1 < ng and group == 1) else None
                    for tt in range(NT):
                        ts = slice(tt * TT, (tt + 1) * TT)
                        banks = []
                        for k in range(group):
                            pb_ = (wcnt["it"]) % 6 if group > 1 else sbank()
                            wcnt["it"] += 1
                            for kk in range(8):
                                P.op("pe", mm(ps[pb_][:], wq[cur[k]][:, kk, :], hT[:, kk, ts], kk == 0, kk == 7),
                                     r=[WQ[cur[k]], HT[kk][tt]], w=[PS[pb_]])
                            banks.append(pb_)
                        post(chunk_ids[gi * group:(gi + 1) * group], tt, banks)
                    if group > 1 and gi + 1 < ng:
                        nxt = loadw(gi + 1)
                    cur = nxt

            ropec = {"n": 0}

            def rope_post(dst):
                def post(cil, tt, banks):
                    ts = slice(tt * TT, (tt + 1) * TT)
                    pb_ = banks[0]
                    n = ropec["n"]
                    ropec["n"] += 1
                    cb_ = n % 2
                    P.dma("sp", cs[cb_][:, 0, :], dr["k_cos"][:, ts], CS[cb_], w=[CS[cb_]])
                    P.dma("sp", cs[cb_][:, 1, :], dr["k_sin"][:, ts], CS[cb_], w=[CS[cb_]])
                    z = zt[cb_]
                    P.op("act", lambda e: e.activation(out=z[:], in_=ps[pb_][:], func=AF.Identity), r=[PS[pb_]], w=[ZT[cb_]])
                    sw = 6 + cb_
                    P.op("pe", mm(ps[sw][:], psw[:], z[:], True, True), r=[PSW, ZT[cb_]], w=[PS[sw]])
                    P.op("pool", lambda e: e.tensor_tensor(out=r1[:], in0=z[:], in1=cs[cb_][:, 0, :], op=ALU.mult), r=[ZT[cb_], CS[cb_]], w=[R1])
                    P.op("dve", lambda e: e.tensor_tensor(out=r2[:], in0=ps[sw][:], in1=cs[cb_][:, 1, :], op=ALU.mult), r=[PS[sw], CS[cb_]], w=[R2])
                    dst(cil[0], tt, ts)
                return post

            P.push()
            with contextlib.ExitStack() as st:
                st.enter_context(nc.named_scope("evA%d" % jl))
                def va(name, shape, dt=F32):
                    return st.enter_context(sb(name, shape, dt))
                kin = va("kin", [128, 2, 2, S], BF16)
                KIN = [[T("kin%d_%d" % (kv, t)) for t in range(NT)] for kv in range(2)]
                for kv in range(2):
                    P.op("dve", lambda e, kv=kv: e.memset(kin[:, kv], 0.0), w=KIN[kv])
                w1t = [va("w1t%d" % kv, [128, 32, 128], BF16) for kv in range(2)]
                W1T = [T("w1t%d" % kv) for kv in range(2)]
                pet = va("pet", [128, 2, 32], BF16); PET = T("pet")
                w2p = va("w2p", [128, 2, 128], BF16); W2P = T("w2p")
                w2v = va("w2v", [128, 64], BF16); W2V = T("w2v")
                P.op("dve", lambda e: e.memset(w2p[:], 0.0), w=[W2P])
                for kv in range(2):
                    P.dma("pool", w1t[kv][:], dr["aw1"][jl, kv], W1T[kv], w=[W1T[kv]], max_dma_last_dim=4096)
                    P.dma("pool", pet[:, kv, :], dr["ape"][jl, kv], PET, w=[PET])
                for h in range(2):
                    P.dma("pool", w2p[:, h, 64 * h:64 * h + 64], dr["aw2"][jl, 0], W2P, w=[W2P])
                P.dma("pool", w2v[:], dr["aw2"][jl, 1], W2V, w=[W2V])

                def dst_kc(ci, tt, ts):
                    for h in range(2):
                        hs_ = slice(64 * h, 64 * h + 64)
                        P.op("dve", lambda e, h=h, hs_=hs_: e.tensor_tensor(out=kin[hs_, 0, h, ts], in0=r1[hs_, :], in1=r2[hs_, :], op=ALU.add),
                             r=[R1, R2], w=[KIN[0][tt]])
                inproj([4], rope_post(dst_kc))

                def post_vc(cil, tt, banks):
                    ts = slice(tt * TT, (tt + 1) * TT)
                    for h in range(2):
                        hs_ = slice(64 * h, 64 * h + 64)
                        P.op("act", lambda e, h=h, hs_=hs_: e.activation(out=kin[hs_, 1, h, ts], in_=ps[banks[0]][hs_, :], func=AF.Identity),
                             r=[PS[banks[0]]], w=[KIN[1][tt]])
                inproj([5], post_vc)

                hx = va("hx", [128, 256]); HX = T("hx")
                h2 = va("h2", [128, 256]); H2 = T("h2")
                hg = va("hg", [128, 256]); HG = T("hg")
                hidb = va("hidb", [128, 256], BF16); HB = T("hidb")
                P.op("dve", lambda e: e.memset(hidb[:], 0.0), w=[HB])
                for kv in range(2):
                    for h in range(2):
                        pb_ = sbank()
                        for l in range(32):
                            P.op("pe", mm(ps[pb_][:, 0:255], w1t[kv][:, l, :], kin[:, kv, h, l:l + 4065:16], l == 0, False),
                                 r=[W1T[kv]] + KIN[kv], w=[PS[pb_]])
                        for l in range(32):
                            P.op("pe", mm(ps[pb_][:, 0:255], w1t[kv][:, l, :], pet[:, kv, l:l + 1].to_broadcast([128, 255]), False, l == 31),
                                 r=[W1T[kv], PET], w=[PS[pb_]])
                        P.op("act", lambda e: e.activation(out=hx[:, 0:255], in_=ps[pb_][:, 0:255], func=AF.Identity), r=[PS[pb_]], w=[HX])
                        P.op("act", lambda e: e.activation(out=h2[:, 0:255], in_=ps[pb_][:, 0:255], func=AF.Square), r=[PS[pb_]], w=[H2])
                        P.op("dve", lambda e: e.tensor_scalar(out=h2[:, 0:255], in0=h2[:, 0:255], scalar1=0.044715, scalar2=1.0, op0=ALU.mult, op1=ALU.add),
                             r=[H2], w=[H2])
                        P.op("dve", lambda e: e.tensor_tensor(out=h2[:, 0:255], in0=h2[:, 0:255], in1=hx[:, 0:255], op=ALU.mult), r=[H2, HX], w=[H2])
                        P.op("act", lambda e: e.activation(out=hg[:, 0:255], in_=h2[:, 0:255], func=AF.Sigmoid, scale=1.5957691216), r=[H2], w=[HG])
                        P.op("dve", lambda e: e.tensor_tensor(out=hidb[:, 0:255], in0=hx[:, 0:255], in1=hg[:, 0:255], op=ALU.mult), r=[HX, HG], w=[HB])
                        if kv == 0:
                            P.op("pe", mm(ps[3][:, 0:255], w2p[:, h, :], hidb[:, 0:255], True, True), r=[W2P, HB], w=[PS[3]])
                            P.op("act", lambda e, h=h: e.activation(out=kcP[:, h, 0:255], in_=ps[3][:, 0:255], func=AF.Identity), r=[PS[3]], w=[KCP])
                        else:
                            for jc in range(2):
                                P.op("pe", mm(ps[3][:, jc * 64:(jc + 1) * 64], hidb[:, jc * 128:(jc + 1) * 128], w2v[:], True, True),
                                     r=[W2V, HB], w=[PS[3]])
                            P.op("act", lambda e, h=h: e.activation(out=rc[:, h, :, 0:64], in_=ps[3][:, 0:128].rearrange("p (a b) -> p a b", a=2),
                                                                    func=AF.Identity), r=[PS[3]], w=[RC])
                if dbg == "A":
                    dump(0, kcP[:].rearrange("p a b -> p (a b)"), 512, [KCP])
                    dump(1, rc[:].rearrange("p a b c -> p (a b c)"), 516, [RC])
                    dump(2, kin[:, 0, 0, :], 4096, KIN[0])
                    dump(3, kin[:, 1, 1, :], 4096, KIN[1])
                P.barrier()
            P.pop()
            if dbg == "A":
                return

            scB = nc.named_scope("evB%d" % jl)
            scB.__enter__()
            if prefetch:
                prefetch()
            ksP = vt("ksP", [128, 2, S], BF16); KS = [T("ks%d" % t) for t in range(NT)]
            kwP = vt("kwP", [128, 2, S], BF16); KW = [T("kw%d" % t) for t in range(NT)]
            P.op("dve", lambda e: e.memset(ksP[:], 0.0), w=KS)
            P.op("dve", lambda e: e.memset(kwP[:], 0.0), w=KW)
            vtok = vt("vtok", [128, 32, 2, 2, 65], BF16); VT = [T("vt%d" % t) for t in range(NT)]
            P.op("dve", lambda e: e.memset(vtok[:], 1.0), w=VT)
            gsig = vt("gsig", [128, 32, 24]); GS = [T("gs%d" % t) for t in range(NT)]
            expb = vt("expb", [128, S], BF16); tri = vt("tri", [128, 2, 128], BF16)
            shw = vt("shw", [128, 512], BF16); pat = vt("pat", [128, 128], BF16)
            identb = vt("identb", [128, 128], BF16)
            pa = vt("pa", [128, 128]); pbt = vt("pbt", [128, 128])
            AC = T("attconst")
            for t_, nm, q_ in ((expb, "k_expb", "pool"), (tri, "k_tri", "pool"), (shw, "k_shw", "pool"), (pat, "k_pat", "pool"),
                               (identb, "k_ident", "pool"), (pa, "k_pa", "sp"), (pbt, "k_pb", "sp")):
                tq_ = T(nm)
                P.dma(q_, t_[:], dr[nm], tq_, w=[tq_], max_dma_last_dim=4096)
                AC.w.update(tq_.w)
            wv = vt("wv", [128, 8, 280], BF16); WV = T("wv")
            P.dma("pool", wv[:], dr["awv"][jl], WV, w=[WV], max_dma_last_dim=4096)

            def dst_k(buf, TL):
                def dst(ci, tt, ts):
                    for h in range(2):
                        hs_ = slice(64 * h, 64 * h + 64)
                        P.op("dve", lambda e, h=h, hs_=hs_: e.tensor_tensor(out=buf[hs_, h, ts], in0=r1[hs_, :], in1=r2[hs_, :], op=ALU.add),
                             r=[R1, R2], w=[TL[tt]])
                return dst
            inproj([6], rope_post(dst_k(ksP, KS)))
            inproj([7], rope_post(dst_k(kwP, KW)))
            qo = [vt("qo%d" % i, [128, TT], BF16) for i in range(2)]
            QO = [T("qo%d" % i) for i in range(2)]
            qc = {"n": 0}

            def dst_q(ci, tt, ts):
                b = qc["n"] % 2
                qc["n"] += 1
                P.op("dve", lambda e: e.tensor_tensor(out=qo[b][:], in0=r1[:], in1=r2[:], op=ALU.add), r=[R1, R2], w=[QO[b]])
                P.dma("sp", qs_d[ci, :, ts], qo[b][:], QO[b], r=[QO[b]], w=[QS[tt]])
            inproj([0, 1, 2, 3], rope_post(dst_q))
            for tq in range(32):
                pb_ = sbank()
                for kk in range(8):
                    P.op("pe", mm(ps[pb_][:, 0:280], hT[:, kk, tq * 128:(tq + 1) * 128], wv[:, kk, :], kk == 0, kk == 7),
                         r=[WV, HT[kk][tq // 4]], w=[PS[pb_]])
                P.op("act", lambda e, tq=tq, pb_=pb_: e.activation(out=vtok[:, tq, :, :, 0:64],
                                                                  in_=ps[pb_][:, 0:256].rearrange("p (a b c) -> p a b c", a=2, b=2), func=AF.Identity),
                     r=[PS[pb_]], w=[VT[tq // 4]])
                P.op("act", lambda e, tq=tq, pb_=pb_: e.activation(out=gsig[:, tq, :], in_=ps[pb_][:, 256:280], func=AF.Sigmoid),
                     r=[PS[pb_]], w=[GS[tq // 4]])
            acw = vt("acw", [128, 4, 3]); ACW = T("acw")
            P.dma("sp", acw[:], dr["acw"][:, jl], ACW, w=[ACW])
            xg = vt("xg", [128, TT]); XG = T("xg")
            Ub = [vt("ub%d" % i, [128, TT + 2]) for i in range(2)]
            UB = [T("ub%d" % i) for i in range(2)]
            yb = vt("yb", [128, TT]); YB = T("yb")
            obo = [vt("obo%d" % i, [128, TT], BF16) for i in range(2)]
            OBO = [T("obo%d" % i) for i in range(2)]
            cc = {"n": 0}

            def post_conv(cil, tt, banks):
                ts = slice(tt * TT, (tt + 1) * TT)
                i = cil[0] - 8
                pbb, pbc, pbx = banks
                ub = tt % 2
                u = Ub[ub]
                P.op("act", lambda e: e.activation(out=xg[:], in_=ps[pbx][:], func=AF.Identity), r=[PS[pbx]], w=[XG])
                if tt == 0:
                    P.op("pool", lambda e: e.memset(u[:, 0:2], 0.0), w=[UB[ub]])
                else:
                    pu = Ub[1 - ub]
                    P.op("pool", lambda e: e.tensor_copy(out=u[:, 0:2], in_=pu[:, TT:TT + 2]), r=[UB[1 - ub]], w=[UB[ub]])
                P.op("dve", lambda e: e.tensor_tensor(out=u[:, 2:TT + 2], in0=ps[pbc][:], in1=xg[:], op=ALU.mult), r=[PS[pbc], XG], w=[UB[ub]])
                P.op("dve", lambda e: e.tensor_scalar(out=yb[:], in0=u[:, 2:TT + 2], scalar1=acw[:, i, 2:3], scalar2=None, op0=ALU.mult),
                     r=[UB[ub], ACW], w=[YB])
                for j in range(2):
                    P.op("dve", lambda e, j=j: e.scalar_tensor_tensor(out=yb[:], in0=u[:, j:TT + j], scalar=acw[:, i, j:j + 1], in1=yb[:],
                                                                      op0=ALU.mult, op1=ALU.add), r=[UB[ub], ACW], w=[YB])
                b = cc["n"] % 2
                cc["n"] += 1
                P.op("dve", lambda e: e.tensor_tensor(out=obo[b][:], in0=ps[pbb][:], in1=yb[:], op=ALU.mult), r=[PS[pbb], YB], w=[OBO[b]])
                P.dma("sp", os_d[tt, :, 4 + i, :], obo[b][:], OBO[b], r=[OBO[b]], w=[OS[tt]])
            inproj([8, 12, 16, 9, 13, 17, 10, 14, 18, 11, 15, 19], post_conv, group=3)

            if dbg == "B":
                dump(0, ksP[:, 0, :], 4096, KS)
                dump(1, kwP[:, 1, :], 4096, KW)
                dump(2, vtok[:, 0:15].rearrange("p a b c d -> p (a b c d)"), 3900, VT)
                dump(3, gsig[:].rearrange("p a b -> p (a b)"), 768, GS)
                return
            scB.__exit__(None, None, None)
            ph.enter_context(nc.named_scope("evC%d" % jl))
            qt_ = [vt("qt%d" % i, [128, 4, 128], BF16) for i in range(2)]
            QT = [T("qt%d" % i) for i in range(2)]
            et = [vt("et%d" % i, [128, 512], BF16) for i in range(3)]
            ET = [T("et%d" % i) for i in range(3)]
            ec = {"n": 0}
            den = vt("den", [128, 12]); rden = vt("rden", [128, 12]); cf = vt("cf", [128, 12])
            DEN = T("den")
            imp = vt("imp", [128, 64]); IMP = T("imp")
            sc = vt("sc", [128, 64]); SC = T("sc")
            m8 = vt("m8", [128, 8]); M8 = T("m8")
            selq = vt("selq", [128, 128]); SELQ = T("selq")
            P.op("dve", n=64, fn=lambda e: e.memset(selq[:], 0.0), w=[SELQ])
            selT = vt("selT", [128, 128], BF16); SELT = T("selT")
            otok = [vt("otok%d" % i, [128, 512]) for i in range(2)]
            OTOK = [T("otok%d" % i) for i in range(2)]
            obt = [vt("obt%d" % i, [128, 4, 128], BF16) for i in range(2)]
            OBT = [T("obt%d" % i) for i in range(2)]

            def bc4(ap):
                return ap.unsqueeze(1).to_broadcast([128, 4, 128])

            def loadq(qt):
                b = qt % 2
                P.dma("sp", qt_[b][:], qs_d[:, :, qt * 128:(qt + 1) * 128].rearrange("c p s -> p c s"), QT[b], r=[QS[qt // 4]], w=[QT[b]])

            def score_exp(kP, KT_, kc, h, qb, masks):
                sb_ = sbank()
                q512 = qt_[qb][:]
                P.op("pe", mm(ps[sb_][:], kP, q512, True, len(masks) == 0), r=KT_ + [QT[qb]], w=[PS[sb_]])
                for mi, (lh, rh, rl) in enumerate(masks):
                    P.op("pe", mm(ps[sb_][:], lh, rh, False, mi == len(masks) - 1), r=[AC] + rl, w=[PS[sb_]])
                eb = ec["n"] % 3
                ec["n"] += 1
                P.op("act", lambda e: e.activation(out=et[eb][:], in_=ps[sb_][:], func=AF.Exp, scale=0.125), r=[PS[sb_]], w=[ET[eb]])
                return eb

            fifo = []

            def push_pv(fn):
                fifo.append(fn)
                while len(fifo) > 2:
                    fifo.pop(0)()

            def flush():
                while fifo:
                    fifo.pop(0)()

            QN = 2 if (dbg and dbg.startswith("C")) else 32

            def combine(qt, h, bank, br):
                c0 = 4 * br
                tq4 = qt // 4
                ot_ = otok[qt % 2]
                OT_ = OTOK[qt % 2]
                if dbg in ("C0", "C1", "C2") and dbg != "C%d" % br:
                    return
                P.op("dve", n=64, fn=lambda e: e.tensor_scalar(out=den[:, c0:c0 + 4], in0=ps[bank][:, 0:260].rearrange("p (a b) -> p a b", a=4)[:, :, 64],
                                                      scalar1=1e-30, scalar2=None, op0=ALU.max), r=[PS[bank]], w=[DEN])
                P.op("dve", n=64, fn=lambda e: e.reciprocal(out=rden[:, c0:c0 + 4], in_=den[:, c0:c0 + 4]), r=[DEN], w=[DEN])
                P.op("dve", n=64, fn=lambda e: e.tensor_tensor(out=cf[:, c0:c0 + 4], in0=rden[:, c0:c0 + 4], in1=gsig[:, qt, h * 12 + br:h * 12 + 12:3], op=ALU.mult),
                     r=[DEN, GS[tq4]], w=[DEN])
                for g in range(4):
                    oc = h * 256 + g * 64
                    P.op("dve", n=64, fn=lambda e, g=g, oc=oc: e.scalar_tensor_tensor(out=ot_[:, oc:oc + 64], in0=ps[bank][:, g * 65:g * 65 + 64],
                                                                            scalar=cf[:, c0 + g:c0 + g + 1], in1=ot_[:, oc:oc + 64],
                                                                            op0=ALU.mult, op1=ALU.add), r=[PS[bank], DEN], w=[OT_])

            def part1(qt, h):
                qb = qt % 2
                tq4 = qt // 4
                ot_ = otok[qt % 2]
                OT_ = OTOK[qt % 2]
                jcs = [0, 1] if qt >= 16 else [0]
                for jc in jcs:
                    st_ = 128 * jc - 8 * qt + 256
                    eb = score_exp(kcP[:, h, jc * 128:(jc + 1) * 128], [KCP], jc, h, qb,
                                   [(shw[:, st_:st_ + 128], bc4(pat[:]), [])])

                    def pv_cmp(eb=eb, jc=jc, h=h, last=(jc == jcs[-1])):
                        for g in range(4):
                            bk = 6 + g // 2
                            reg = slice((g % 2) * 129, (g % 2) * 129 + 129)
                            P.op("pe", mm(ps[bk][:, reg], et[eb][:, g * 128:(g + 1) * 128], rc[:, h, jc, :],
                                          jc == 0 and g % 2 == 0, last), r=[ET[eb], RC], w=[PS[bk]])
                    push_pv(pv_cmp)
                kcs = list(range(max(0, qt - 4), qt + 1))
                for kc in kcs:
                    masks = []
                    if kc == qt - 4:
                        masks.append((identb[:], bc4(tri[:, 1, :]), []))
                    if kc == qt:
                        masks.append((identb[:], bc4(tri[:, 0, :]), []))
                    eb = score_exp(kwP[:, h, kc * 128:(kc + 1) * 128], [KW[kc // 4]], kc, h, qb, masks)

                    def pv_win(eb=eb, kc=kc, h=h, first=(kc == kcs[0]), last=(kc == qt)):
                        for g in range(4):
                            P.op("pe", mm(ps[5][:, g * 65:(g + 1) * 65], et[eb][:, g * 128:(g + 1) * 128], vtok[:, kc, 1, h, :],
                                          first and g == 0, last), r=[ET[eb], VT[kc // 4]], w=[PS[5]])
                    push_pv(pv_win)
                flush()
                for bk in range(2):
                    P.op("dve", n=64, fn=lambda e, bk=bk: e.tensor_scalar(out=den[:, 2 * bk:2 * bk + 2],
                                                                in0=ps[6 + bk][:, 0:258].rearrange("p (a b) -> p a b", a=2)[:, :, 128],
                                                                scalar1=1e-30, scalar2=None, op0=ALU.max), r=[PS[6 + bk]], w=[DEN])
                P.op("dve", n=64, fn=lambda e: e.reciprocal(out=rden[:, 0:4], in_=den[:, 0:4]), r=[DEN], w=[DEN])
                for g in range(4):
                    bk = 6 + g // 2
                    o0 = (g % 2) * 129
                    if g == 0:
                        P.op("dve", n=64, fn=lambda e: e.tensor_scalar(out=imp[:], in0=ps[6][:, 64:128], scalar1=rden[:, 0:1], scalar2=None, op0=ALU.mult),
                             r=[PS[6], DEN], w=[IMP])
                    else:
                        P.op("dve", n=64, fn=lambda e, bk=bk, o0=o0, g=g: e.scalar_tensor_tensor(out=imp[:], in0=ps[bk][:, o0 + 64:o0 + 128], scalar=rden[:, g:g + 1],
                                                                                       in1=imp[:], op0=ALU.mult, op1=ALU.add), r=[PS[bk], DEN], w=[IMP])
                w0 = 63 - 2 * qt
                P.op("dve", n=64, fn=lambda e: e.tensor_tensor(out=sc[:], in0=imp[:], in1=pa[:, w0:w0 + 64], op=ALU.mult), r=[IMP, AC], w=[SC])
                P.op("dve", n=64, fn=lambda e: e.tensor_tensor(out=sc[:], in0=sc[:], in1=pbt[:, w0:w0 + 64], op=ALU.add), r=[SC, AC], w=[SC])
                P.op("dve", n=64, fn=lambda e: e.memset(sc[:, 0:1], 1e4), w=[SC])
                P.op("dve", n=64, fn=lambda e: e.max(out=m8[:], in_=sc[:]), r=[SC], w=[M8])
                P.op("dve", n=64, fn=lambda e: e.tensor_scalar(out=selq[:, 0:64], in0=sc[:], scalar1=m8[:, 7:8], scalar2=1.0, op0=ALU.is_ge, op1=ALU.subtract),
                     r=[SC, M8], w=[SELQ])
                P.op("dve", n=64, fn=lambda e: e.tensor_tensor(out=cf[:, 0:4], in0=rden[:, 0:4], in1=gsig[:, qt, h * 12 + 0:h * 12 + 12:3], op=ALU.mult),
                     r=[DEN, GS[tq4]], w=[DEN])
                for g in range(4):
                    bk = 6 + g // 2
                    o0 = (g % 2) * 129
                    oc = h * 256 + g * 64
                    P.op("dve", n=64, fn=lambda e, bk=bk, o0=o0, g=g, oc=oc: e.tensor_scalar(out=ot_[:, oc:oc + 64], in0=ps[bk][:, o0:o0 + 64],
                                                                                   scalar1=cf[:, g:g + 1], scalar2=None, op0=ALU.mult),
                         r=[PS[bk], DEN], w=[OT_])
                if dbg in ("C1", "C2"):
                    P.op("dve", n=64, fn=lambda e: e.memset(ot_[:, h * 256:h * 256 + 256], 0.0), w=[OT_])
                combine(qt, h, 5, 2)

            def part2(qt, h):
                P.op("pe", n=128, fn=lambda e: e.transpose(out=ps[3][:, 0:128], in_=selq[:], identity=ident_f[:]), r=[SELQ, CONST], w=[PS[3]])
                P.op("dve", n=64, fn=lambda e: e.tensor_copy(out=selT[:], in_=ps[3][:, 0:128]), r=[PS[3]], w=[SELT])

            def part3a(qt, h):
                qb = qt % 2
                for kc in range(qt + 1):
                    masks = [(expb[:, kc * 128:(kc + 1) * 128], bc4(selT[:]), [SELT])]
                    if kc == qt:
                        masks.append((identb[:], bc4(tri[:, 0, :]), []))
                    eb = score_exp(ksP[:, h, kc * 128:(kc + 1) * 128], [KS[kc // 4]], kc, h, qb, masks)

                    def pv_slc(eb=eb, kc=kc, h=h, last=(kc == qt)):
                        for g in range(4):
                            P.op("pe", mm(ps[4][:, g * 65:(g + 1) * 65], et[eb][:, g * 128:(g + 1) * 128], vtok[:, kc, 0, h, :],
                                          kc == 0 and g == 0, last), r=[ET[eb], VT[kc // 4]], w=[PS[4]])
                    push_pv(pv_slc)
                flush()

            def part3b(qt, h):
                combine(qt, h, 4, 1)
                if h == 1:
                    ob_ = qt % 2
                    for c4 in range(4):
                        P.op("pe", lambda e, c4=c4: e.transpose(out=ps[3][:, c4 * 128:(c4 + 1) * 128], in_=otok[ob_][:, c4 * 128:(c4 + 1) * 128], identity=ident_f[:]),
                             r=[OTOK[ob_], CONST], w=[PS[3]])
                    P.op("dve", n=64, fn=lambda e: e.tensor_copy(out=obt[ob_][:], in_=ps[3][:].rearrange("p (a b) -> p a b", a=4)), r=[PS[3]], w=[OBT[ob_]])
                    P.dma("sp", os_d[qt // 4, :, 0:4, (qt % 4) * 128:(qt % 4 + 1) * 128], obt[ob_][:], OBT[ob_], r=[OBT[ob_]], w=[OS[qt // 4]])

            iters = [(qt, h) for qt in range(QN) for h in range(2)]
            loadq(0)
            if QN > 1:
                loadq(1)
            part1(*iters[0])
            part2(*iters[0])
            for i, (qt, h) in enumerate(iters):
                nxt = iters[i + 1] if i + 1 < len(iters) else None
                if nxt:
                    part1(*nxt)
                part3a(qt, h)
                if nxt:
                    part2(*nxt)
                part3b(qt, h)
                if h == 1 and qt + 2 < QN:
                    loadq(qt + 2)
            if dbg and dbg.startswith("C"):
                dump(0, otok[0][:], 512, [OTOK[0]])
                dump(1, otok[1][:], 512, [OTOK[1]])
                dump(2, imp[:], 64, [IMP])
                dump(3, sc[:], 64, [SC])
                dump(4, selq[:], 128, [SELQ])
                dump(5, den[:], 12, [DEN])
                dump(6, rden[:], 12, [DEN])
                dump(7, cf[:], 12, [DEN])
            P.barrier()

    def emit_x_out():
        P.push()
        with contextlib.ExitStack() as ph:
            xt = ph.enter_context(sb("xdbg", [128, 8, TT]))
            XT = T("xdbg")
            for tt in range(NT):
                ts = slice(tt * TT, (tt + 1) * TT)
                P.dma("sp", xt[:], xr_d[tt], XT, r=[XR[tt]], w=[XT])
                P.dma("sp", out_d[tt], xt[:], XT, r=[XT], w=[OUT])
            P.barrier()
        P.pop()

    def gain(step):
        kind, l = step
        return gffn[:, l, :] if kind == "ffn" else gmix[:, l, :]

    cbos = []
    for jl_ in range(2):
        cbo_ = es.enter_context(sb("cbo%d" % jl_, [128, 8]))
        tb = T("cbo")
        P.dma("sp", cbo_[:], dr["cbout"][:, jl_], tb, w=[tb], perm=True)
        CONST.w.update(tb.w)
        cbos.append(cbo_)
    phase_resid("init", None, 0, None, None, None, gain(plan[0]), False)
    for si, (kind, l) in enumerate(plan):
        jl = l // 2
        is_last = si == len(plan) - 1
        last = is_last and final
        gn = gfin[:, :] if last else (gain(plan[si + 1]) if not is_last else gfin[:, :])
        kc_ = NPAIR if kind == "ffn" else 8
        w_dram = {"even": dr["awout"], "odd": dr["cwout"], "ffn": dr["wdn"]}[kind][jl if kind != "ffn" else l]
        with contextlib.ExitStack() as lay:
            P.push()
            wt = lay.enter_context(sb("wres", [128, kc_, D], BF16))
            WT = T("wres")

            def prefetch(wt=wt, WT=WT, w_dram=w_dram, kc_=kc_):
                half = kc_ // 2
                P.dma("pool", wt[:, :half, :], w_dram[:, :half, :], WT, w=[WT], max_dma_last_dim=4096)
                P.dma("pool", wt[:, half:, :], w_dram[:, half:, :], WT, w=[WT], max_dma_last_dim=4096)
            if kind == "even":
                phase_even(jl, dbg, prefetch)
                if dbg:
                    P.pop()
                    break
                with nc.named_scope("resmix%d" % l):
                    phase_resid("mix", None, 8, os_d, OS, None, gn, last, (wt, WT))
            elif kind == "odd":
                cbo = cbos[jl]
                with nc.named_scope("odd%d" % l):
                    phase_odd(jl, prefetch)
                with nc.named_scope("resmix%d" % l):
                    phase_resid("mix", None, 8, os_d, OS, cbo, gn, last, (wt, WT))
            else:
                with nc.named_scope("ffn%d" % l):
                    phase_ffn1(l, prefetch)
                with nc.named_scope("resffn%d" % l):
                    phase_resid("ffn", None, NPAIR, as_d, AS, None, gn, last, (wt, WT))
            P.pop()
    if not final and not dbg:
        emit_x_out()
    P.barrier()
    es.close()
    return nc


SHAPES = None


def _shapes(shared):
    s = {k: v.shape for k, v in shared.items()}
    s["xT"] = (NT, 128, 8, TT)
    return s


def kernel(**inputs):
    shared = prep_shared(inputs)
    x = np.asarray(inputs["x"], dtype=np.float32)
    nb = x.shape[0]
    nc = build(_shapes(shared))
    in_maps = []
    for b in range(nb):
        m = dict(shared)
        m["xT"] = _c(x[b].T.reshape(8, 128, NT, TT).transpose(2, 1, 0, 3))
        in_maps.append(m)
    res = run_bass_kernel_spmd(nc, in_maps, core_ids=list(range(nb)))
    out = np.stack([np.asarray(r["out"]).reshape(NT, 128, 8, TT).transpose(2, 1, 0, 3).reshape(D, S).T for r in res.results], axis=0)
    return np.ascontiguousarray(out.astype(np.float32))
```

```python
import contextlib
import os
import numpy as np
import concourse.bass as bass
import concourse.mybir as mybir
from concourse.bass_utils import run_bass_kernel_spmd

F32 = mybir.dt.float32
BF16 = mybir.dt.bfloat16
ALU = mybir.AluOpType
AF = mybir.ActivationFunctionType

S = 4096
D = 1024
NT = 8
TT = 512
DFF = 2816
NPAIR = 22
BIG = 30000.0
EPS = 1e-6


class T:
    __slots__ = ("name", "w", "r", "ds", "dram")

    def __init__(self, name="", dram=False):
        self.name = name
        self.w = {}
        self.r = {}
        self.ds = None
        self.dram = dram
        if not dram:
            _SCOPES[-1].append(self)


_SCOPES = [[]]


class Prog:
    def __init__(self, nc, es, n_dsem=72):
        self.nc = nc
        self.E = {"pe": nc.tensor, "dve": nc.vector, "act": nc.scalar, "pool": nc.gpsimd, "sp": nc.sync}
        self.sem = {k: es.enter_context(nc.semaphore("e_" + k)) for k in self.E}
        self.cnt = {k: 0 for k in self.E}
        self.seen = {k: {} for k in self.E}
        self.dpool = [[es.enter_context(nc.semaphore("d%d" % i)), 0] for i in range(n_dsem)]
        self.dperm = set()
        self.dfree = list(range(n_dsem))

    def dsem(self, t, perm=False):
        if t.ds is None:
            t.ds = self.dfree.pop()
            if perm:
                self.dperm.add(t.ds)
        return self.dpool[t.ds]

    def _waits(self, eng, r, w):
        need = {}
        seen = self.seen[eng]

        def add(evs):
            for key, (sem, val, e, small) in evs.items():
                if e == eng and not (small and self.cnt[eng] - val < 4):
                    continue
                if seen.get(key, 0) >= val:
                    continue
                if key not in need or need[key][1] < val:
                    need[key] = (sem, val)

        for t in r:
            add(t.w)
        for t in w:
            if not t.dram:
                add(t.w)
            add(t.r)
        for key, (sem, val) in need.items():
            self.E[eng].wait_ge(sem, val)
            seen[key] = val

    def op(self, eng, fn, r=(), w=(), n=512):
        self._waits(eng, r, w)
        ins = fn(self.E[eng])
        self.cnt[eng] += 1
        ins.then_inc(self.sem[eng], 1)
        key = "e_" + eng
        ev = (self.sem[eng], self.cnt[eng], eng, n < 256)
        for t in w:
            t.w = {key: ev}
            t.r = {}
        for t in r:
            t.r[key] = ev

    def dma(self, q, out, in_, sb, r=(), w=(), perm=False, **kw):
        self._waits(q, r, w)
        d = self.dsem(sb, perm)
        d[1] += 16
        self.E[q].dma_start(out=out, in_=in_, **kw).then_inc(d[0], 16)
        key = "d%d" % sb.ds
        ev = (d[0], d[1], "dma", False)
        for t in w:
            if t.dram:
                if t.r:
                    t.w = {}
                    t.r = {}
                t.w[key] = ev
            else:
                t.w = {key: ev}
                t.r = {}
        for t in r:
            t.r[key] = ev

    def barrier(self):
        sp = self.E["sp"]
        seen = self.seen["sp"]
        for k in self.E:
            if k != "sp" and self.cnt[k] > seen.get("e_" + k, 0):
                sp.wait_ge(self.sem[k], self.cnt[k])
        for i, (s, c) in enumerate(self.dpool):
            if c > seen.get("d%d" % i, 0):
                sp.wait_ge(s, c)
        self.cnt["sp"] += 1
        sp.sem_inc(self.sem["sp"], 1)
        for k in self.E:
            if k != "sp":
                self.E[k].wait_ge(self.sem["sp"], self.cnt["sp"])
            sn = self.seen[k]
            for k2 in self.E:
                sn["e_" + k2] = self.cnt[k2]
            for i, (s, c) in enumerate(self.dpool):
                sn["d%d" % i] = c

    def push(self):
        _SCOPES.append([])

    def pop(self):
        for t in _SCOPES.pop():
            if t.ds is not None and t.ds not in self.dperm:
                self.dfree.append(t.ds)
                t.ds = None


def _c(a):
    return np.ascontiguousarray(a, dtype=np.float32)


def _vec(a):
    a = np.asarray(a, dtype=np.float32)
    lead = a.shape[:-1]
    c = a.shape[-1] // 128
    a = a.reshape(lead + (c, 128))
    return _c(np.moveaxis(a, -1, 0))


def _wchunks(w, cols):
    k = w.shape[0]
    out = np.empty((len(cols), 128, k // 128, 128), np.float32)
    for i, ci in enumerate(cols):
        out[i] = w[:, ci].reshape(k // 128, 128, 128).transpose(1, 0, 2)
    return out


def _rows(w):
    k, n = w.shape
    return _c(w.reshape(k // 128, 128, n).transpose(1, 0, 2))


def make_consts():
    c = {}
    inv = 1.0 / (10000.0 ** (np.arange(0, 64, 2, dtype=np.float32) / 64.0))
    ang = np.arange(S, dtype=np.float32)[:, None] * inv[None, :]
    ang = np.concatenate([ang, ang], axis=-1).astype(np.float32)
    cos = np.cos(ang).astype(np.float32).T
    sin = np.sin(ang).astype(np.float32).T
    sgn = np.where(np.arange(64) < 32, -1.0, 1.0).astype(np.float32)[:, None]
    c["k_cos"] = _c(np.concatenate([cos, cos], 0))
    c["k_sin"] = _c(np.concatenate([sin * sgn, sin * sgn], 0))
    psw = np.zeros((128, 128), np.float32)
    for cp in range(128):
        base = (cp // 64) * 64
        psw[base + ((cp % 64) + 32) % 64, cp] = 1.0
    c["k_psw"] = psw
    c["k_ident"] = np.eye(128, dtype=np.float32)
    c["k_ones"] = np.ones((128, 128), np.float32)
    expb = np.zeros((128, S), np.float32)
    kk = np.arange(S)
    expb[kk // 64, kk] = BIG
    c["k_expb"] = expb
    j = np.arange(128)[:, None]
    i = np.arange(128)[None, :]
    tri = np.zeros((128, 2, 128), np.float32)
    tri[:, 0, :] = np.where(j > i, -BIG, 0.0)
    tri[:, 1, :] = np.where(j <= i, -BIG, 0.0)
    c["k_tri"] = tri
    shw = np.zeros((128, 512), np.float32)
    r = np.arange(512) - 256
    shw[0, r <= -2] = 1.0
    for jj in range(1, 9):
        shw[jj, r == jj - 2] = 1.0
    shw[9, r >= 7] = 1.0
    c["k_shw"] = shw
    pat = np.zeros((128, 128), np.float32)
    ii = np.arange(128)
    for jj in range(1, 9):
        pat[jj] = np.where(ii >= 31 + 16 * (jj - 2), 0.0, -BIG)
    pat[9] = -BIG
    c["k_pat"] = pat
    n = np.arange(256)[:, None]
    b = np.arange(64)[None, :]
    ov = ((16 * n <= 64 * b + 63) & (16 * n + 31 >= 64 * b)).astype(np.float32)
    ovc = np.zeros((128, 2, 65), np.float32)
    for jc in range(2):
        ovc[:, jc, :64] = ov[jc * 128:(jc + 1) * 128]
        ovc[:, jc, 64] = 1.0
    c["k_ovc"] = ovc
    pa = np.zeros((128, 128), np.float32)
    pb = np.zeros((128, 128), np.float32)
    for q in range(128):
        cr = q // 64
        for cc in range(127):
            rr = cc - 63
            if rr <= cr - 2:
                pa[q, cc] = 1.0
            elif rr <= cr:
                pb[q, cc] = 1e4
            else:
                pb[q, cc] = -1e4
    c["k_pa"] = pa
    c["k_pb"] = pb
    return c


def prep_shared(inp):
    g = {}
    f = lambda k: np.asarray(inp[k], dtype=np.float32)
    g["gmix"] = _vec(f("norm_mix"))
    g["gffn"] = _vec(f("norm_ffn"))
    g["gfin"] = _vec(f("norm_final"))
    wup = f("f_w_up")
    cols = [np.arange(j * 128, (j + 1) * 128) for j in range(44)]
    g["wup"] = np.stack([_wchunks(wup[l], cols) for l in range(4)])
    g["fcw"] = _c(np.moveaxis(f("f_conv_w").reshape(4, 3, 44, 128), 3, 0).transpose(0, 1, 3, 2))
    g["fcb"] = _vec(f("f_conv_b"))
    g["wdn"] = np.stack([_rows(f("f_w_down")[l]) for l in range(4)])
    cw = f("c_w_in")
    ccols = [np.concatenate([np.arange(n * 128, (n + 1) * 128), 1024 + np.arange(n * 128, (n + 1) * 128)]) for n in range(8)]
    cwin = np.empty((2, 8, 128, 8, 256), np.float32)
    for l in range(2):
        for n in range(8):
            cwin[l, n] = cw[l][:, ccols[n]].reshape(8, 128, 256).transpose(1, 0, 2)
    g["cwin"] = cwin
    g["cbin"] = _vec(f("c_b_in"))
    g["ccw"] = _c(np.moveaxis(f("c_conv_w").reshape(2, 4, 8, 128), 3, 0).transpose(0, 1, 3, 2))
    g["ccb"] = _vec(f("c_conv_b"))
    g["cwa"] = _c(f("c_w_a"))
    g["cwi"] = _c(f("c_w_i"))
    g["cba"] = _vec(f("c_b_a"))
    g["cbi"] = _vec(f("c_b_i"))
    g["clam"] = _vec(f("c_lambda"))
    g["cwout"] = np.stack([_rows(f("c_w_out")[l]) for l in range(2)])
    g["cbout"] = _vec(f("c_b_out"))
    aw = f("a_w_in")
    acols = []
    for i in range(4):
        acols.append(np.concatenate([np.arange(i * 64, (i + 1) * 64), np.arange((4 + i) * 64, (5 + i) * 64)]))
    for t in (0, 1, 2, 4):
        acols.append(512 + t * 128 + np.arange(128))
    for t in range(12):
        acols.append(1304 + t * 128 + np.arange(128))
    g["awin"] = np.stack([_wchunks(aw[l], acols) for l in range(2)])
    vcols = np.concatenate([512 + 3 * 128 + np.arange(128), 512 + 5 * 128 + np.arange(128), 1280 + np.arange(24)])
    g["awv"] = np.stack([_rows(aw[l][:, vcols]) for l in range(2)])
    g["acw"] = _c(np.moveaxis(f("a_conv_w").reshape(2, 3, 4, 128), 3, 0).transpose(0, 1, 3, 2))
    w1 = f("a_cmp_w1").reshape(2, 2, 32, 64, 128).transpose(0, 1, 3, 2, 4)
    g["aw1"] = _c(np.concatenate([w1, w1], axis=2))
    pe = f("a_cmp_pe").transpose(0, 1, 3, 2)
    g["ape"] = _c(np.concatenate([pe, np.zeros_like(pe)], axis=2))
    g["aw2"] = _c(f("a_cmp_w2"))
    g["awout"] = np.stack([_rows(f("a_w_out")[l]) for l in range(2)])
    g.update(make_consts())
    return {k: _c(v) for k, v in g.items()}


FULL_PLAN = [("even", 0), ("ffn", 0), ("odd", 1), ("ffn", 1), ("even", 2), ("ffn", 2), ("odd", 3), ("ffn", 3)]


def build(shapes, plan=None, final=True, dbg=None):
    plan = plan or FULL_PLAN
    nc = bass.Bass("TRN2", target_bir_lowering=False)
    es = contextlib.ExitStack()
    P = Prog(nc, es)
    dr = {}
    for name, shp in shapes.items():
        dr[name] = nc.dram_tensor(name, list(shp), F32, kind="ExternalInput").ap()
    out_d = nc.dram_tensor("out", [NT, 128, 8, TT], F32, kind="ExternalOutput").ap()
    xr_d = nc.dram_tensor("xr", [NT, 128, 8, TT], F32, kind="Internal").ap()
    os_d = nc.dram_tensor("osc", [NT, 128, 8, TT], BF16, kind="Internal").ap()
    as_d = nc.dram_tensor("asc", [NT, 128, NPAIR, TT], BF16, kind="Internal").ap()
    qs_d = nc.dram_tensor("qsc", [4, 128, S], BF16, kind="Internal").ap()
    x_d = dr["xT"]
    XR = [T("xr%d" % t, dram=True) for t in range(NT)]
    OS = [T("os%d" % t, dram=True) for t in range(NT)]
    AS = [T("as%d" % t, dram=True) for t in range(NT)]
    QS = [T("qs%d" % t, dram=True) for t in range(NT)]
    XIN = T("xin", dram=True)
    OUT = T("out", dram=True)

    ps = [es.enter_context(nc.psum_tensor("ps%d" % i, [128, 512], F32)) for i in range(8)]
    PS = [T("ps%d" % i) for i in range(8)]

    uid = [0]

    def sb(name, shape, dt=F32):
        uid[0] += 1
        return nc.sbuf_tensor("s%d_%s" % (uid[0], name), list(shape), dt)

    hT = es.enter_context(sb("hT", [128, 8, S], BF16))
    HT = [[T("h%d_%d" % (c, t)) for t in range(NT)] for c in range(8)]
    ones_f = es.enter_context(sb("ones_f", [128, 128]))
    ident_f = es.enter_context(sb("ident_f", [128, 128]))
    gmix = es.enter_context(sb("gmix", [128, 4, 8]))
    gffn = es.enter_context(sb("gffn", [128, 4, 8]))
    gfin = es.enter_context(sb("gfin", [128, 8]))
    CONST = T("const")
    for t_, nm in ((ones_f, "k_ones"), (ident_f, "k_ident"), (gmix, "gmix"), (gffn, "gffn"), (gfin, "gfin")):
        tt_ = T(nm)
        P.dma("sp", t_[:], dr[nm], tt_, w=[tt_], perm=True)
        CONST.w.update(tt_.w)

    def mm(out, lhsT, rhs, start, stop):
        return lambda e: e.matmul(out, lhsT, rhs, start=start, stop=stop, skip_group_check=True)

    def phase_resid(mode, w_dram, kc, o_dram, OT, bias_ap, g_ap, final, w_pre=None):
        P.push()
        with contextlib.ExitStack() as ph:
            xt = [ph.enter_context(sb("xt%d" % i, [128, 8, TT])) for i in range(2)]
            XT = [[T("xt%d_%d" % (i, m)) for m in range(8)] for i in range(2)]
            sq = [ph.enter_context(sb("sq%d" % i, [128, TT], BF16)) for i in range(2)]
            SQ = [T("sq%d" % i) for i in range(2)]
            rt = ph.enter_context(sb("rt", [128, TT]))
            RT = T("rt")
            rstd = [ph.enter_context(sb("rstd%d" % i, [128, TT])) for i in range(2)]
            RSTD = [T("rstd%d" % i) for i in range(2)]
            pend_tiles = []
            if mode != "init":
                wt, WT = w_pre
                nob = 2
                ot = [ph.enter_context(sb("ot%d" % i, [128, kc, TT], BF16)) for i in range(nob)]
                OTL = [T("ot%d" % i) for i in range(nob)]
            for tt in range(NT):
                ts = slice(tt * TT, (tt + 1) * TT)
                b = tt % 2
                if mode == "init":
                    P.dma("sp", xt[b][:], x_d[tt], XT[b][0], r=[XIN], w=XT[b])
                else:
                    ob = tt % nob
                    if not (os.environ.get("NOOT") and tt > 1):
                        P.dma("sp", ot[ob][:], o_dram[tt], OTL[ob], r=[OT[tt]], w=[OTL[ob]])
                    P.dma("sp", xt[b][:], xr_d[tt], XT[b][0], r=[XR[tt]], w=XT[b])
                def stat(m):
                    sb_ = m % 2
                    P.op("pe", mm(ps[4][:], ones_b[:], sq[sb_][:], m == 0, m == 7), r=[SQ[sb_], CONST], w=[PS[4]])

                for m in range(8):
                    if mode != "init":
                        pb_ = m % 4
                        for k in range(kc):
                            P.op("pe", mm(ps[pb_][:], wt[:, k, m * 128:(m + 1) * 128], ot[ob][:, k, :], k == 0, k == kc - 1),
                                 r=[WT, OTL[ob]], w=[PS[pb_]])
                        bsc = bias_ap[:, m:m + 1] if bias_ap is not None else 0.0
                        P.op("dve", lambda e, m=m, pb_=pb_, bsc=bsc: e.scalar_tensor_tensor(
                            out=xt[b][:, m, :], in0=ps[pb_][:], scalar=bsc, in1=xt[b][:, m, :], op0=ALU.add, op1=ALU.add),
                            r=[PS[pb_], CONST], w=[XT[b][m]])
                    sb_ = m % 2
                    P.op("act", lambda e, m=m, sb_=sb_: e.activation(out=sq[sb_][:], in_=xt[b][:, m, :], func=AF.Square),
                         r=[XT[b][m]], w=[SQ[sb_]])
                    if pend_tiles:
                        pend_tiles[-1](m)
                    if m >= 1:
                        stat(m - 1)
                stat(7)
                P.op("act", lambda e: e.activation(out=rt[:], in_=ps[4][:], func=AF.Sqrt, scale=1.0 / D, bias=eps_t[:, 0:1]),
                     r=[PS[4], CONST], w=[RT])
                P.op("dve", lambda e, b=b: e.reciprocal(out=rstd[b][:], in_=rt[:]), r=[RT], w=[RSTD[b]])
                if not final:
                    P.dma("pool", xr_d[tt], xt[b][:], XT[b][0], r=XT[b], w=[XR[tt]])

                def hop(m, b=b, ts=ts, tt=tt):
                    if not final:
                        P.op("dve", lambda e: e.scalar_tensor_tensor(
                            out=hT[:, m, ts], in0=xt[b][:, m, :], scalar=g_ap[:, m:m + 1], in1=rstd[b][:], op0=ALU.mult, op1=ALU.mult),
                            r=[XT[b][m], RSTD[b], CONST], w=[HT[m][tt]])
                    else:
                        P.op("dve", lambda e: e.scalar_tensor_tensor(
                            out=xt[b][:, m, :], in0=xt[b][:, m, :], scalar=g_ap[:, m:m + 1], in1=rstd[b][:], op0=ALU.mult, op1=ALU.mult),
                            r=[XT[b][m], RSTD[b], CONST], w=[XT[b][m]])
                        if m == 7:
                            P.dma("pool", out_d[tt], xt[b][:], XT[b][0], r=XT[b], w=[OUT])
                pend_tiles.append(hop)
            for m in range(8):
                pend_tiles[-1](m)
            P.barrier()
        P.pop()

    ones_b = es.enter_context(sb("ones_b", [128, 128], BF16))
    P.op("dve", lambda e: e.memset(ones_b[:], 1.0), w=[CONST])
    eps_t = es.enter_context(sb("eps_t", [128, 1]))
    one_t = es.enter_context(sb("one_t", [128, 1]))
    P.op("dve", lambda e: e.memset(eps_t[:], EPS), w=[CONST])
    P.op("dve", lambda e: e.memset(one_t[:], 1.0), w=[CONST])

    def phase_ffn1(l, prefetch=None):
        P.push()
        with contextlib.ExitStack() as ph:
            cw = ph.enter_context(sb("fcw", [128, 44, 3]))
            cb = ph.enter_context(sb("fcb", [128, 44]))
            CW = T("fcw")
            P.dma("sp", cw[:], dr["fcw"][:, l], CW, w=[CW])
            P.dma("sp", cb[:], dr["fcb"][:, l], CW, w=[CW])
            wg = [ph.enter_context(sb("wg%d" % i, [128, 2, 8, 128], BF16)) for i in range(2)]
            WG = [T("wg%d" % i) for i in range(2)]
            U = [[ph.enter_context(sb("u%d_%d" % (k, i), [128, TT + 2])) for i in range(2)] for k in range(2)]
            UT = [[T("u%d_%d" % (k, i)) for i in range(2)] for k in range(2)]
            t1 = [ph.enter_context(sb("t1_%d" % k, [128, TT])) for k in range(2)]
            T1 = [T("t1_%d" % k) for k in range(2)]
            y = [ph.enter_context(sb("y_%d" % k, [128, TT])) for k in range(2)]
            Y = [T("y_%d" % k) for k in range(2)]
            sg = ph.enter_context(sb("sg", [128, TT]))
            SG = T("sg")
            ao = [ph.enter_context(sb("ao%d" % i, [128, TT], BF16)) for i in range(3)]
            AO = [T("ao%d" % i) for i in range(3)]

            def loadw(c):
                b = c % 2
                P.dma("pool", wg[b][:, 0], dr["wup"][l, c], WG[b], w=[WG[b]])
                P.dma("pool", wg[b][:, 1], dr["wup"][l, NPAIR + c], WG[b], w=[WG[b]])

            loadw(0)
            it = 0
            for c in range(NPAIR):
                if c + 1 < NPAIR:
                    loadw(c + 1)
                if c == 0 and prefetch:
                    prefetch()
                wb = c % 2
                for tt in range(NT):
                    ts = slice(tt * TT, (tt + 1) * TT)
                    ub = tt % 2
                    for k in range(2):
                        pb_ = (2 * it + k) % 4
                        j = c + k * NPAIR
                        for kk in range(8):
                            P.op("pe", mm(ps[pb_][:], wg[wb][:, k, kk, :], hT[:, kk, ts], kk == 0, kk == 7),
                                 r=[WG[wb], HT[kk][tt]], w=[PS[pb_]])
                        u = U[k][ub]
                        if tt == 0:
                            P.op("pool", lambda e, u=u: e.memset(u[:, 0:2], 0.0), w=[UT[k][ub]])
                        else:
                            pu = U[k][1 - ub]
                            P.op("pool", lambda e, u=u, pu=pu: e.tensor_copy(out=u[:, 0:2], in_=pu[:, TT:TT + 2]),
                                 r=[UT[k][1 - ub]], w=[UT[k][ub]])
                        P.op("act", lambda e, u=u, pb_=pb_: e.activation(out=u[:, 2:TT + 2], in_=ps[pb_][:], func=AF.Identity),
                             r=[PS[pb_]], w=[UT[k][ub]])
                        P.op("act", lambda e, pb_=pb_, j=j, k=k: e.activation(out=t1[k][:], in_=ps[pb_][:], func=AF.Identity,
                                                                            scale=cw[:, j, 2:3], bias=cb[:, j:j + 1]),
                             r=[PS[pb_], CW], w=[T1[k]])
                        P.op("dve", lambda e, u=u, j=j, k=k: e.scalar_tensor_tensor(
                            out=t1[k][:], in0=u[:, 1:TT + 1], scalar=cw[:, j, 1:2], in1=t1[k][:], op0=ALU.mult, op1=ALU.add),
                            r=[UT[k][ub], CW], w=[T1[k]])
                        P.op("dve", lambda e, u=u, j=j, k=k: e.scalar_tensor_tensor(
                            out=y[k][:], in0=u[:, 0:TT], scalar=cw[:, j, 0:1], in1=t1[k][:], op0=ALU.mult, op1=ALU.add),
                            r=[UT[k][ub], T1[k], CW], w=[Y[k]])
                    P.op("act", lambda e: e.activation(out=sg[:], in_=y[0][:], func=AF.Silu), r=[Y[0]], w=[SG])
                    ab = it % 3
                    P.op("dve", lambda e, ab=ab: e.tensor_tensor(out=ao[ab][:], in0=sg[:], in1=y[1][:], op=ALU.mult),
                         r=[SG, Y[1]], w=[AO[ab]])
                    P.dma("sp", as_d[tt, :, c, :], ao[ab][:], AO[ab], r=[AO[ab]], w=[AS[tt]])
                    it += 1
            P.barrier()
        P.pop()

    def phase_odd(jl, prefetch=None):
        P.push()
        with contextlib.ExitStack() as ph:
            def vt(name, shape):
                return ph.enter_context(sb(name, shape))
            bin_ = vt("cbin", [128, 16]); ccw = vt("ccw", [128, 8, 4]); ccb = vt("ccb", [128, 8])
            cba = vt("cba", [128, 8]); cbi = vt("cbi", [128, 8]); lam = vt("clam", [128, 8]); cl = vt("cl", [128, 8]); bhy = vt("bhy", [128, 8])
            CV = T("cvec")
            for t_, nm in ((bin_, "cbin"), (ccw, "ccw"), (ccb, "ccb"), (cba, "cba"), (cbi, "cbi"), (lam, "clam")):
                P.dma("sp", t_[:], dr[nm][:, jl], CV, w=[CV])
            P.op("act", lambda e: e.activation(out=cl[:], in_=lam[:], func=AF.Exp, scale=-1.0), r=[CV], w=[CV], n=8)
            P.op("act", lambda e: e.activation(out=cl[:], in_=cl[:], func=AF.Ln, bias=one_t[:, 0:1]), r=[CV], w=[CV], n=8)
            P.op("dve", lambda e: e.tensor_scalar(out=cl[:], in0=cl[:], scalar1=-4.0, scalar2=None, op0=ALU.mult), r=[CV], w=[CV], n=8)
            P.op("dve", lambda e: e.tensor_scalar(out=cba[:], in0=cba[:], scalar1=0.5, scalar2=None, op0=ALU.mult), r=[CV], w=[CV], n=8)
            P.op("dve", lambda e: e.tensor_scalar(out=cbi[:], in0=cbi[:], scalar1=0.5, scalar2=None, op0=ALU.mult), r=[CV], w=[CV], n=8)
            P.op("dve", lambda e: e.tensor_scalar(out=bhy[:], in0=bin_[:, 0:8], scalar1=0.5, scalar2=None, op0=ALU.mult), r=[CV], w=[CV], n=8)
            win = [ph.enter_context(sb("cwin%d" % i, [128, 8, 256], BF16)) for i in range(2)]
            WIN = [T("cwin%d" % i) for i in range(2)]
            wai = [ph.enter_context(sb("cwai%d" % i, [128, 2, 128])) for i in range(2)]
            WAI = [T("cwai%d" % i) for i in range(2)]
            U = [vt("cu%d" % i, [128, TT + 3]) for i in range(2)]
            UT = [T("cu%d" % i) for i in range(2)]
            xs = [vt("xs%d" % i, [128, TT]) for i in range(3)]; XS = [T("xs%d" % i) for i in range(3)]
            x2l = [vt("x2_%d" % i, [128, TT]) for i in range(2)]; X2L = [T("x2_%d" % i) for i in range(2)]
            sgm = [vt("sgm%d" % i, [128, TT]) for i in range(2)]; SGM = [T("sgm%d" % i) for i in range(2)]
            uu = [vt("uu%d" % i, [128, TT]) for i in range(2)]; UU = [T("uu%d" % i) for i in range(2)]
            rr = vt("rr", [128, TT]); RR = T("rr")
            ig = vt("ig", [128, TT]); IG = T("ig")
            aa = vt("aa", [128, TT]); AA = T("aa")
            mmul = vt("mmul", [128, TT]); MM = T("mmul")
            hs = [vt("hs%d" % i, [128, TT]) for i in range(2)]
            HS = [T("hs%d" % i) for i in range(2)]
            oo = [ph.enter_context(sb("oo%d" % i, [128, TT], BF16)) for i in range(3)]
            OO = [T("oo%d" % i) for i in range(3)]

            def loadw(n):
                b = n % 2
                P.dma("pool", win[b][:], dr["cwin"][jl, n], WIN[b], w=[WIN[b]])
                P.dma("sp", wai[b][:, 0, :], dr["cwa"][jl, n], WAI[b], w=[WAI[b]])
                P.dma("sp", wai[b][:, 1, :], dr["cwi"][jl, n], WAI[b], w=[WAI[b]])

            def stage1(it, n, tt):
                ts = slice(tt * TT, (tt + 1) * TT)
                wb = n % 2
                ub = tt % 2
                db = it % 2
                xb = it % 3
                x2 = x2l[db]
                X2 = X2L[db]
                py, pu_ = (2 * it) % 4, (2 * it + 1) % 4
                for k, pb_ in ((0, py), (1, pu_)):
                    for kk in range(8):
                        P.op("pe", mm(ps[pb_][:], win[wb][:, kk, k * 128:(k + 1) * 128], hT[:, kk, ts], kk == 0, kk == 7),
                             r=[WIN[wb], HT[kk][tt]], w=[PS[pb_]])
                P.op("act", lambda e: e.activation(out=xs[xb][:], in_=ps[py][:], func=AF.Identity, scale=0.5, bias=bhy[:, n:n + 1]),
                     r=[PS[py], CV], w=[XS[xb]])
                P.op("act", lambda e: e.activation(out=x2[:], in_=ps[py][:], func=AF.Square, bias=bin_[:, n:n + 1]),
                     r=[PS[py], CV], w=[X2])
                u = U[ub]
                if tt == 0:
                    P.op("pool", lambda e: e.memset(u[:, 0:3], 0.0), w=[UT[ub]])
                else:
                    pu = U[1 - ub]
                    P.op("pool", lambda e: e.tensor_copy(out=u[:, 0:3], in_=pu[:, TT:TT + 3]), r=[UT[1 - ub]], w=[UT[ub]])
                P.op("act", lambda e: e.activation(out=u[:, 3:TT + 3], in_=ps[pu_][:], func=AF.Identity, bias=bin_[:, 8 + n:9 + n]),
                     r=[PS[pu_], CV], w=[UT[ub]])

            def stage2(it, n, tt):
                ts = slice(tt * TT, (tt + 1) * TT)
                wb = n % 2
                ub = tt % 2
                db = it % 2
                xb = it % 3
                x2 = x2l[db]
                X2 = X2L[db]
                u = U[ub]
                P.op("dve", lambda e: e.tensor_scalar(out=x2[:], in0=x2[:], scalar1=0.044715, scalar2=1.0, op0=ALU.mult, op1=ALU.add),
                     r=[X2], w=[X2])
                P.op("dve", lambda e: e.tensor_tensor(out=x2[:], in0=x2[:], in1=xs[xb][:], op=ALU.mult), r=[X2, XS[xb]], w=[X2])
                P.op("dve", lambda e: e.tensor_scalar(out=uu[db][:], in0=u[:, 3:TT + 3], scalar1=ccw[:, n, 3:4], scalar2=ccb[:, n:n + 1],
                                                      op0=ALU.mult, op1=ALU.add), r=[UT[ub], CV], w=[UU[db]])
                for j in range(3):
                    P.op("dve", lambda e, j=j: e.scalar_tensor_tensor(out=uu[db][:], in0=u[:, j:TT + j], scalar=ccw[:, n, j:j + 1], in1=uu[db][:],
                                                                      op0=ALU.mult, op1=ALU.add), r=[UT[ub], CV], w=[UU[db]])

                P.op("act", lambda e: e.activation(out=sgm[db][:], in_=x2[:], func=AF.Tanh, scale=1.5957691216), r=[X2], w=[SGM[db]])
                pr, pi_ = 4 + (2 * it) % 4, 4 + (2 * it + 1) % 4
                P.op("pe", mm(ps[pr][:], wai[wb][:, 0, :], uu[db][:], True, True), r=[WAI[wb], UU[db]], w=[PS[pr]])
                P.op("pe", mm(ps[pi_][:], wai[wb][:, 1, :], uu[db][:], True, True), r=[WAI[wb], UU[db]], w=[PS[pi_]])

            def stage3(it, n, tt):
                ts = slice(tt * TT, (tt + 1) * TT)
                wb = n % 2
                db = it % 2
                xb = it % 3
                pr, pi_ = 4 + (2 * it) % 4, 4 + (2 * it + 1) % 4
                P.op("act", lambda e: e.activation(out=rr[:], in_=ps[pr][:], func=AF.Tanh, scale=0.5, bias=cba[:, n:n + 1]), r=[PS[pr], CV], w=[RR])
                P.op("act", lambda e: e.activation(out=ig[:], in_=ps[pi_][:], func=AF.Tanh, scale=0.5, bias=cbi[:, n:n + 1]), r=[PS[pi_], CV], w=[IG])
                P.op("act", lambda e: e.activation(out=aa[:], in_=rr[:], func=AF.Exp, scale=cl[:, n:n + 1], bias=cl[:, n:n + 1]), r=[RR, CV], w=[AA])
                P.op("act", lambda e: e.activation(out=mmul[:], in_=aa[:], func=AF.Square), r=[AA], w=[MM])
                P.op("act", lambda e: e.activation(out=mmul[:], in_=mmul[:], func=AF.Sqrt, scale=-1.0, bias=one_t[:, 0:1]), r=[MM], w=[MM])
                P.op("dve", lambda e: e.scalar_tensor_tensor(out=ig[:], in0=ig[:], scalar=1.0, in1=uu[db][:], op0=ALU.add, op1=ALU.mult), r=[IG, UU[db]], w=[IG])
                P.op("dve", lambda e: e.scalar_tensor_tensor(out=ig[:], in0=ig[:], scalar=0.5, in1=mmul[:], op0=ALU.mult, op1=ALU.mult), r=[IG, MM], w=[IG])
                hb = tt % 2
                init = 0.0 if tt == 0 else hs[1 - hb][:, TT - 1:TT]
                rl = [AA, IG] + ([HS[1 - hb]] if tt else [])
                P.op("dve", lambda e: e.tensor_tensor_scan(out=hs[hb][:], data0=aa[:], data1=ig[:], initial=init,
                                                           op0=ALU.mult, op1=ALU.add), r=rl, w=[HS[hb]])
                P.op("dve", lambda e: e.tensor_tensor(out=xs[xb][:], in0=xs[xb][:], in1=hs[hb][:], op=ALU.mult), r=[XS[xb], HS[hb]], w=[XS[xb]])
                ob = it % 3
                P.op("dve", lambda e: e.scalar_tensor_tensor(out=oo[ob][:], in0=sgm[db][:], scalar=1.0, in1=xs[xb][:], op0=ALU.add, op1=ALU.mult),
                     r=[XS[xb], SGM[db]], w=[OO[ob]])
                P.dma("sp", os_d[tt, :, n, :], oo[ob][:], OO[ob], r=[OO[ob]], w=[OS[tt]])

            loadw(0)
            if prefetch:
                prefetch()
            its = [(n, tt) for n in range(8) for tt in range(NT)]
            NI = len(its)
            for k in range(NI + 2):
                if k < NI:
                    stage1(k, *its[k])
                if 1 <= k <= NI:
                    stage2(k - 1, *its[k - 1])
                if 2 <= k:
                    stage3(k - 2, *its[k - 2])
                if k >= 2 and k - 2 < NI and its[k - 2][1] == NT - 1 and its[k - 2][0] + 2 < 8:
                    loadw(its[k - 2][0] + 2)
                if k == 0:
                    loadw(1)
            P.barrier()
        P.pop()

    def dump(idx, ap, n, TL):
        P.push()
        with contextlib.ExitStack() as dd:
            stg = dd.enter_context(sb("stg", [128, n]))
            STG = T("stg")
            P.op("dve", lambda e: e.tensor_copy(out=stg[:, 0:n], in_=ap), r=TL, w=[STG])
            P.dma("sp", out_d[idx].rearrange("p c s -> p (c s)")[:, 0:n], stg[:, 0:n], STG, r=[STG], w=[OUT])
            P.barrier()
        P.pop()

    def phase_even(jl, dbg=None, prefetch=None):
        P.push()
        try:
            _phase_even(jl, dbg, prefetch)
        finally:
            P.pop()

    def _phase_even(jl, dbg=None, prefetch=None):
        rot = {"s": 0}

        def sbank():
            rot["s"] = (rot["s"] + 1) % 3
            return rot["s"]

        with contextlib.ExitStack() as ph:
            def vt(name, shape, dt=F32):
                return ph.enter_context(sb(name, shape, dt))
            psw = vt("psw", [128, 128]); PSW = T("psw")
            P.dma("sp", psw[:], dr["k_psw"], PSW, w=[PSW])
            kcP = vt("kcP", [128, 2, 256], BF16); KCP = T("kcP")
            rc = vt("rc", [128, 2, 2, 129], BF16); RC = T("rc")
            P.op("dve", lambda e: e.memset(kcP[:], 0.0), w=[KCP])
            P.op("dve", lambda e: e.memset(rc[:], 0.0), w=[RC])
            for h in range(2):
                P.dma("pool", rc[:, h, :, 64:129], dr["k_ovc"], RC, r=[], w=[RC])
            cs = [vt("cs%d" % i, [128, 2, TT]) for i in range(2)]
            CS = [T("cs%d" % i) for i in range(2)]
            wq = [vt("wq%d" % i, [128, 8, 128], BF16) for i in range(3)]
            WQ = [T("wq%d" % i) for i in range(3)]
            zt = [vt("zt%d" % i, [128, TT]) for i in range(2)]
            ZT = [T("zt%d" % i) for i in range(2)]
            r1 = vt("r1", [128, TT]); R1 = T("r1")
            r2 = vt("r2", [128, TT]); R2 = T("r2")
            wcnt = {"n": 0, "it": 0}

            def inproj(chunk_ids, post, group=1):
                ng = len(chunk_ids) // group

                def loadw(gi):
                    bl = []
                    for k in range(group):
                        b = wcnt["n"] % 3
                        wcnt["n"] += 1
                        P.dma("pool", wq[b][:], dr["awin"][jl, chunk_ids[gi * group + k]], WQ[b], w=[WQ[b]])
                        bl.append(b)
                    return bl
                cur = loadw(0)
                for gi in range(ng):
                    nxt = loadw(gi + 1) if (gi + 1 < ng and group == 1) else None
                    for tt in range(NT):
                        ts = slice(tt * TT, (tt + 1) * TT)
                        banks = []
                        for k in range(group):
                            pb_ = (wcnt["it"]) % 6 if group > 1 else sbank()
                            wcnt["it"] += 1
                            for kk in range(8):
                                P.op("pe", mm(ps[pb_][:], wq[cur[k]][:, kk, :], hT[:, kk, ts], kk == 0, kk == 7),
                                     r=[WQ[cur[k]], HT[kk][tt]], w=[PS[pb_]])
                            banks.append(pb_)
                        post(chunk_ids[gi * group:(gi + 1) * group], tt, banks)
                    if group > 1 and gi + 1 < ng:
                        nxt = loadw(gi + 1)
                    cur = nxt

            ropec = {"n": 0}

            def rope_post(dst):
                def post(cil, tt, banks):
                    ts = slice(tt * TT, (tt + 1) * TT)
                    pb_ = banks[0]
                    n = ropec["n"]
                    ropec["n"] += 1
                    cb_ = n % 2
                    P.dma("sp", cs[cb_][:, 0, :], dr["k_cos"][:, ts], CS[cb_], w=[CS[cb_]])
                    P.dma("sp", cs[cb_][:, 1, :], dr["k_sin"][:, ts], CS[cb_], w=[CS[cb_]])
                    z = zt[cb_]
                    P.op("act", lambda e: e.activation(out=z[:], in_=ps[pb_][:], func=AF.Identity), r=[PS[pb_]], w=[ZT[cb_]])
                    sw = 6 + cb_
                    P.op("pe", mm(ps[sw][:], psw[:], z[:], True, True), r=[PSW, ZT[cb_]], w=[PS[sw]])
                    P.op("pool", lambda e: e.tensor_tensor(out=r1[:], in0=z[:], in1=cs[cb_][:, 0, :], op=ALU.mult), r=[ZT[cb_], CS[cb_]], w=[R1])
                    P.op("dve", lambda e: e.tensor_tensor(out=r2[:], in0=ps[sw][:], in1=cs[cb_][:, 1, :], op=ALU.mult), r=[PS[sw], CS[cb_]], w=[R2])
                    dst(cil[0], tt, ts)
                return post

            P.push()
            with contextlib.ExitStack() as st:
                st.enter_context(nc.named_scope("evA%d" % jl))
                def va(name, shape, dt=F32):
                    return st.enter_context(sb(name, shape, dt))
                kin = va("kin", [128, 2, 2, S], BF16)
                KIN = [[T("kin%d_%d" % (kv, t)) for t in range(NT)] for kv in range(2)]
                for kv in range(2):
                    P.op("dve", lambda e, kv=kv: e.memset(kin[:, kv], 0.0), w=KIN[kv])
                w1t = [va("w1t%d" % kv, [128, 32, 128], BF16) for kv in range(2)]
                W1T = [T("w1t%d" % kv) for kv in range(2)]
                pet = va("pet", [128, 2, 32], BF16); PET = T("pet")
                w2p = va("w2p", [128, 2, 128], BF16); W2P = T("w2p")
                w2v = va("w2v", [128, 64], BF16); W2V = T("w2v")
                P.op("dve", lambda e: e.memset(w2p[:], 0.0), w=[W2P])
                for kv in range(2):
                    P.dma("pool", w1t[kv][:], dr["aw1"][jl, kv], W1T[kv], w=[W1T[kv]], max_dma_last_dim=4096)
                    P.dma("pool", pet[:, kv, :], dr["ape"][jl, kv], PET, w=[PET])
                for h in range(2):
                    P.dma("pool", w2p[:, h, 64 * h:64 * h + 64], dr["aw2"][jl, 0], W2P, w=[W2P])
                P.dma("pool", w2v[:], dr["aw2"][jl, 1], W2V, w=[W2V])

                def dst_kc(ci, tt, ts):
                    for h in range(2):
                        hs_ = slice(64 * h, 64 * h + 64)
                        P.op("dve", lambda e, h=h, hs_=hs_: e.tensor_tensor(out=kin[hs_, 0, h, ts], in0=r1[hs_, :], in1=r2[hs_, :], op=ALU.add),
                             r=[R1, R2], w=[KIN[0][tt]])
                inproj([4], rope_post(dst_kc))

                def post_vc(cil, tt, banks):
                    ts = slice(tt * TT, (tt + 1) * TT)
                    for h in range(2):
                        hs_ = slice(64 * h, 64 * h + 64)
                        P.op("act", lambda e, h=h, hs_=hs_: e.activation(out=kin[hs_, 1, h, ts], in_=ps[banks[0]][hs_, :], func=AF.Identity),
                             r=[PS[banks[0]]], w=[KIN[1][tt]])
                inproj([5], post_vc)

                hx = va("hx", [128, 256]); HX = T("hx")
                h2 = va("h2", [128, 256]); H2 = T("h2")
                hg = va("hg", [128, 256]); HG = T("hg")
                hidb = va("hidb", [128, 256], BF16); HB = T("hidb")
                P.op("dve", lambda e: e.memset(hidb[:], 0.0), w=[HB])
                for kv in range(2):
                    for h in range(2):
                        pb_ = sbank()
                        for l in range(32):
                            P.op("pe", mm(ps[pb_][:, 0:255], w1t[kv][:, l, :], kin[:, kv, h, l:l + 4065:16], l == 0, False),
                                 r=[W1T[kv]] + KIN[kv], w=[PS[pb_]])
                        for l in range(32):
                            P.op("pe", mm(ps[pb_][:, 0:255], w1t[kv][:, l, :], pet[:, kv, l:l + 1].to_broadcast([128, 255]), False, l == 31),
                                 r=[W1T[kv], PET], w=[PS[pb_]])
                        P.op("act", lambda e: e.activation(out=hx[:, 0:255], in_=ps[pb_][:, 0:255], func=AF.Identity), r=[PS[pb_]], w=[HX])
                        P.op("act", lambda e: e.activation(out=h2[:, 0:255], in_=ps[pb_][:, 0:255], func=AF.Square), r=[PS[pb_]], w=[H2])
                        P.op("dve", lambda e: e.tensor_scalar(out=h2[:, 0:255], in0=h2[:, 0:255], scalar1=0.044715, scalar2=1.0, op0=ALU.mult, op1=ALU.add),
                             r=[H2], w=[H2])
                        P.op("dve", lambda e: e.tensor_tensor(out=h2[:, 0:255], in0=h2[:, 0:255], in1=hx[:, 0:255], op=ALU.mult), r=[H2, HX], w=[H2])
                        P.op("act", lambda e: e.activation(out=hg[:, 0:255], in_=h2[:, 0:255], func=AF.Sigmoid, scale=1.5957691216), r=[H2], w=[HG])
                        P.op("dve", lambda e: e.tensor_tensor(out=hidb[:, 0:255], in0=hx[:, 0:255], in1=hg[:, 0:255], op=ALU.mult), r=[HX, HG], w=[HB])
                        if kv == 0:
                            P.op("pe", mm(ps[3][:, 0:255], w2p[:, h, :], hidb[:, 0:255], True, True), r=[W2P, HB], w=[PS[3]])
                            P.op("act", lambda e, h=h: e.activation(out=kcP[:, h, 0:255], in_=ps[3][:, 0:255], func=AF.Identity), r=[PS[3]], w=[KCP])
                        else:
                            for jc in range(2):
                                P.op("pe", mm(ps[3][:, jc * 64:(jc + 1) * 64], hidb[:, jc * 128:(jc + 1) * 128], w2v[:], True, True),
                                     r=[W2V, HB], w=[PS[3]])
                            P.op("act", lambda e, h=h: e.activation(out=rc[:, h, :, 0:64], in_=ps[3][:, 0:128].rearrange("p (a b) -> p a b", a=2),
                                                                    func=AF.Identity), r=[PS[3]], w=[RC])
                if dbg == "A":
                    dump(0, kcP[:].rearrange("p a b -> p (a b)"), 512, [KCP])
                    dump(1, rc[:].rearrange("p a b c -> p (a b c)"), 516, [RC])
                    dump(2, kin[:, 0, 0, :], 4096, KIN[0])
                    dump(3, kin[:, 1, 1, :], 4096, KIN[1])
                P.barrier()
            P.pop()
            if dbg == "A":
                return

            scB = nc.named_scope("evB%d" % jl)
            scB.__enter__()
            if prefetch:
                prefetch()
            ksP = vt("ksP", [128, 2, S], BF16); KS = [T("ks%d" % t) for t in range(NT)]
            kwP = vt("kwP", [128, 2, S], BF16); KW = [T("kw%d" % t) for t in range(NT)]
            P.op("dve", lambda e: e.memset(ksP[:], 0.0), w=KS)
            P.op("dve", lambda e: e.memset(kwP[:], 0.0), w=KW)
            vtok = vt("vtok", [128, 32, 2, 2, 65], BF16); VT = [T("vt%d" % t) for t in range(NT)]
            P.op("dve", lambda e: e.memset(vtok[:], 1.0), w=VT)
            gsig = vt("gsig", [128, 32, 24]); GS = [T("gs%d" % t) for t in range(NT)]
            expb = vt("expb", [128, S], BF16); tri = vt("tri", [128, 2, 128], BF16)
            shw = vt("shw", [128, 512], BF16); pat = vt("pat", [128, 128], BF16)
            identb = vt("identb", [128, 128], BF16)
            pa = vt("pa", [128, 128]); pbt = vt("pbt", [128, 128])
            AC = T("attconst")
            for t_, nm, q_ in ((expb, "k_expb", "pool"), (tri, "k_tri", "pool"), (shw, "k_shw", "pool"), (pat, "k_pat", "pool"),
                               (identb, "k_ident", "pool"), (pa, "k_pa", "sp"), (pbt, "k_pb", "sp")):
                tq_ = T(nm)
                P.dma(q_, t_[:], dr[nm], tq_, w=[tq_], max_dma_last_dim=4096)
                AC.w.update(tq_.w)
            wv = vt("wv", [128, 8, 280], BF16); WV = T("wv")
            P.dma("pool", wv[:], dr["awv"][jl], WV, w=[WV], max_dma_last_dim=4096)

            def dst_k(buf, TL):
                def dst(ci, tt, ts):
                    for h in range(2):
                        hs_ = slice(64 * h, 64 * h + 64)
                        P.op("dve", lambda e, h=h, hs_=hs_: e.tensor_tensor(out=buf[hs_, h, ts], in0=r1[hs_, :], in1=r2[hs_, :], op=ALU.add),
                             r=[R1, R2], w=[TL[tt]])
                return dst
            inproj([6], rope_post(dst_k(ksP, KS)))
            inproj([7], rope_post(dst_k(kwP, KW)))
            qo = [vt("qo%d" % i, [128, TT], BF16) for i in range(2)]
            QO = [T("qo%d" % i) for i in range(2)]
            qc = {"n": 0}

            def dst_q(ci, tt, ts):
                b = qc["n"] % 2
                qc["n"] += 1
                P.op("dve", lambda e: e.tensor_tensor(out=qo[b][:], in0=r1[:], in1=r2[:], op=ALU.add), r=[R1, R2], w=[QO[b]])
                P.dma("sp", qs_d[ci, :, ts], qo[b][:], QO[b], r=[QO[b]], w=[QS[tt]])
            inproj([0, 1, 2, 3], rope_post(dst_q))
            for tq in range(32):
                pb_ = sbank()
                for kk in range(8):
                    P.op("pe", mm(ps[pb_][:, 0:280], hT[:, kk, tq * 128:(tq + 1) * 128], wv[:, kk, :], kk == 0, kk == 7),
                         r=[WV, HT[kk][tq // 4]], w=[PS[pb_]])
                P.op("act", lambda e, tq=tq, pb_=pb_: e.activation(out=vtok[:, tq, :, :, 0:64],
                                                                  in_=ps[pb_][:, 0:256].rearrange("p (a b c) -> p a b c", a=2, b=2), func=AF.Identity),
                     r=[PS[pb_]], w=[VT[tq // 4]])
                P.op("act", lambda e, tq=tq, pb_=pb_: e.activation(out=gsig[:, tq, :], in_=ps[pb_][:, 256:280], func=AF.Sigmoid),
                     r=[PS[pb_]], w=[GS[tq // 4]])
            acw = vt("acw", [128, 4, 3]); ACW = T("acw")
            P.dma("sp", acw[:], dr["acw"][:, jl], ACW, w=[ACW])
            xg = vt("xg", [128, TT]); XG = T("xg")
            Ub = [vt("ub%d" % i, [128, TT + 2]) for i in range(2)]
            UB = [T("ub%d" % i) for i in range(2)]
            yb = vt("yb", [128, TT]); YB = T("yb")
            obo = [vt("obo%d" % i, [128, TT], BF16) for i in range(2)]
            OBO = [T("obo%d" % i) for i in range(2)]
            cc = {"n": 0}

            def post_conv(cil, tt, banks):
                ts = slice(tt * TT, (tt + 1) * TT)
                i = cil[0] - 8
                pbb, pbc, pbx = banks
                ub = tt % 2
                u = Ub[ub]
                P.op("act", lambda e: e.activation(out=xg[:], in_=ps[pbx][:], func=AF.Identity), r=[PS[pbx]], w=[XG])
                if tt == 0:
                    P.op("pool", lambda e: e.memset(u[:, 0:2], 0.0), w=[UB[ub]])
                else:
                    pu = Ub[1 - ub]
                    P.op("pool", lambda e: e.tensor_copy(out=u[:, 0:2], in_=pu[:, TT:TT + 2]), r=[UB[1 - ub]], w=[UB[ub]])
                P.op("dve", lambda e: e.tensor_tensor(out=u[:, 2:TT + 2], in0=ps[pbc][:], in1=xg[:], op=ALU.mult), r=[PS[pbc], XG], w=[UB[ub]])
                P.op("dve", lambda e: e.tensor_scalar(out=yb[:], in0=u[:, 2:TT + 2], scalar1=acw[:, i, 2:3], scalar2=None, op0=ALU.mult),
                     r=[UB[ub], ACW], w=[YB])
                for j in range(2):
                    P.op("dve", lambda e, j=j: e.scalar_tensor_tensor(out=yb[:], in0=u[:, j:TT + j], scalar=acw[:, i, j:j + 1], in1=yb[:],
                                                                      op0=ALU.mult, op1=ALU.add), r=[UB[ub], ACW], w=[YB])
                b = cc["n"] % 2
                cc["n"] += 1
                P.op("dve", lambda e: e.tensor_tensor(out=obo[b][:], in0=ps[pbb][:], in1=yb[:], op=ALU.mult), r=[PS[pbb], YB], w=[OBO[b]])
                P.dma("sp", os_d[tt, :, 4 + i, :], obo[b][:], OBO[b], r=[OBO[b]], w=[OS[tt]])
            inproj([8, 12, 16, 9, 13, 17, 10, 14, 18, 11, 15, 19], post_conv, group=3)

            if dbg == "B":
                dump(0, ksP[:, 0, :], 4096, KS)
                dump(1, kwP[:, 1, :], 4096, KW)
                dump(2, vtok[:, 0:15].rearrange("p a b c d -> p (a b c d)"), 3900, VT)
                dump(3, gsig[:].rearrange("p a b -> p (a b)"), 768, GS)
                return
            scB.__exit__(None, None, None)
            ph.enter_context(nc.named_scope("evC%d" % jl))
            qt_ = [vt("qt%d" % i, [128, 4, 128], BF16) for i in range(2)]
            QT = [T("qt%d" % i) for i in range(2)]
            et = [vt("et%d" % i, [128, 512], BF16) for i in range(4)]
            ET = [T("et%d" % i) for i in range(4)]
            ec = {"n": 0}
            den = vt("den", [128, 12]); rden = vt("rden", [128, 12]); cf = vt("cf", [128, 12])
            DEN = T("den")
            imp = vt("imp", [128, 64]); IMP = T("imp")
            sc = vt("sc", [128, 64]); SC = T("sc")
            m8 = vt("m8", [128, 8]); M8 = T("m8")
            selq = vt("selq", [128, 128]); SELQ = T("selq")
            P.op("dve", n=64, fn=lambda e: e.memset(selq[:], 0.0), w=[SELQ])
            selT = vt("selT", [128, 128], BF16); SELT = T("selT")
            otok = [vt("otok%d" % i, [128, 512]) for i in range(2)]
            OTOK = [T("otok%d" % i) for i in range(2)]
            obt = [vt("obt%d" % i, [128, 4, 128], BF16) for i in range(2)]
            OBT = [T("obt%d" % i) for i in range(2)]

            def bc4(ap):
                return ap.unsqueeze(1).to_broadcast([128, 4, 128])

            def loadq(qt):
                b = qt % 2
                P.dma("sp", qt_[b][:], qs_d[:, :, qt * 128:(qt + 1) * 128].rearrange("c p s -> p c s"), QT[b], r=[QS[qt // 4]], w=[QT[b]])

            def score_exp(kP, KT_, kc, h, qb, masks):
                sb_ = sbank()
                q512 = qt_[qb][:]
                P.op("pe", mm(ps[sb_][:], kP, q512, True, len(masks) == 0), r=KT_ + [QT[qb]], w=[PS[sb_]])
                for mi, (lh, rh, rl) in enumerate(masks):
                    P.op("pe", mm(ps[sb_][:], lh, rh, False, mi == len(masks) - 1), r=[AC] + rl, w=[PS[sb_]])
                eb = ec["n"] % 4
                ec["n"] += 1
                P.op("act", lambda e: e.activation(out=et[eb][:], in_=ps[sb_][:], func=AF.Exp, scale=0.125), r=[PS[sb_]], w=[ET[eb]])
                return eb

            fifo = []

            def push_pv(fn):
                fifo.append(fn)
                while len(fifo) > 3:
                    fifo.pop(0)()

            def flush():
                while fifo:
                    fifo.pop(0)()

            QN = 2 if (dbg and dbg.startswith("C")) else 32

            def combine(qt, h, bank, br):
                c0 = 4 * br
                tq4 = qt // 4
                ot_ = otok[qt % 2]
                OT_ = OTOK[qt % 2]
                if dbg in ("C0", "C1", "C2") and dbg != "C%d" % br:
                    return
                P.op("dve", n=64, fn=lambda e: e.tensor_scalar(out=den[:, c0:c0 + 4], in0=ps[bank][:, 0:260].rearrange("p (a b) -> p a b", a=4)[:, :, 64],
                                                      scalar1=1e-30, scalar2=None, op0=ALU.max), r=[PS[bank]], w=[DEN])
                P.op("dve", n=64, fn=lambda e: e.reciprocal(out=rden[:, c0:c0 + 4], in_=den[:, c0:c0 + 4]), r=[DEN], w=[DEN])
                P.op("dve", n=64, fn=lambda e: e.tensor_tensor(out=cf[:, c0:c0 + 4], in0=rden[:, c0:c0 + 4], in1=gsig[:, qt, h * 12 + br:h * 12 + 12:3], op=ALU.mult),
                     r=[DEN, GS[tq4]], w=[DEN])
                for g in range(4):
                    oc = h * 256 + g * 64
                    P.op("dve", n=64, fn=lambda e, g=g, oc=oc: e.scalar_tensor_tensor(out=ot_[:, oc:oc + 64], in0=ps[bank][:, g * 65:g * 65 + 64],
                                                                            scalar=cf[:, c0 + g:c0 + g + 1], in1=ot_[:, oc:oc + 64],
                                                                            op0=ALU.mult, op1=ALU.add), r=[PS[bank], DEN], w=[OT_])

            def part1(qt, h):
                qb = qt % 2
                tq4 = qt // 4
                ot_ = otok[qt % 2]
                OT_ = OTOK[qt % 2]
                jcs = [0, 1] if qt >= 16 else [0]
                for jc in jcs:
                    st_ = 128 * jc - 8 * qt + 256
                    eb = score_exp(kcP[:, h, jc * 128:(jc + 1) * 128], [KCP], jc, h, qb,
                                   [(shw[:, st_:st_ + 128], bc4(pat[:]), [])])

                    def pv_cmp(eb=eb, jc=jc, h=h, last=(jc == jcs[-1])):
                        for g in range(4):
                            bk = 6 + g // 2
                            reg = slice((g % 2) * 129, (g % 2) * 129 + 129)
                            P.op("pe", mm(ps[bk][:, reg], et[eb][:, g * 128:(g + 1) * 128], rc[:, h, jc, :],
                                          jc == 0 and g % 2 == 0, last), r=[ET[eb], RC], w=[PS[bk]])
                    push_pv(pv_cmp)
                kcs = list(range(max(0, qt - 4), qt + 1))
                for kc in kcs:
                    masks = []
                    if kc == qt - 4:
                        masks.append((identb[:], bc4(tri[:, 1, :]), []))
                    if kc == qt:
                        masks.append((identb[:], bc4(tri[:, 0, :]), []))
                    eb = score_exp(kwP[:, h, kc * 128:(kc + 1) * 128], [KW[kc // 4]], kc, h, qb, masks)

                    def pv_win(eb=eb, kc=kc, h=h, first=(kc == kcs[0]), last=(kc == qt)):
                        for g in range(4):
                            P.op("pe", mm(ps[5][:, g * 65:(g + 1) * 65], et[eb][:, g * 128:(g + 1) * 128], vtok[:, kc, 1, h, :],
                                          first and g == 0, last), r=[ET[eb], VT[kc // 4]], w=[PS[5]])
                    push_pv(pv_win)
                flush()
                for bk in range(2):
                    P.op("dve", n=64, fn=lambda e, bk=bk: e.tensor_scalar(out=den[:, 2 * bk:2 * bk + 2],
                                                                in0=ps[6 + bk][:, 0:258].rearrange("p (a b) -> p a b", a=2)[:, :, 128],
                                                                scalar1=1e-30, scalar2=None, op0=ALU.max), r=[PS[6 + bk]], w=[DEN])
                P.op("dve", n=64, fn=lambda e: e.reciprocal(out=rden[:, 0:4], in_=den[:, 0:4]), r=[DEN], w=[DEN])
                for g in range(4):
                    bk = 6 + g // 2
                    o0 = (g % 2) * 129
                    if g == 0:
                        P.op("dve", n=64, fn=lambda e: e.tensor_scalar(out=imp[:], in0=ps[6][:, 64:128], scalar1=rden[:, 0:1], scalar2=None, op0=ALU.mult),
                             r=[PS[6], DEN], w=[IMP])
                    else:
                        P.op("dve", n=64, fn=lambda e, bk=bk, o0=o0, g=g: e.scalar_tensor_tensor(out=imp[:], in0=ps[bk][:, o0 + 64:o0 + 128], scalar=rden[:, g:g + 1],
                                                                                       in1=imp[:], op0=ALU.mult, op1=ALU.add), r=[PS[bk], DEN], w=[IMP])
                w0 = 63 - 2 * qt
                P.op("dve", n=64, fn=lambda e: e.tensor_tensor(out=sc[:], in0=imp[:], in1=pa[:, w0:w0 + 64], op=ALU.mult), r=[IMP, AC], w=[SC])
                P.op("dve", n=64, fn=lambda e: e.tensor_tensor(out=sc[:], in0=sc[:], in1=pbt[:, w0:w0 + 64], op=ALU.add), r=[SC, AC], w=[SC])
                P.op("dve", n=64, fn=lambda e: e.memset(sc[:, 0:1], 1e4), w=[SC])
                P.op("dve", n=64, fn=lambda e: e.max(out=m8[:], in_=sc[:]), r=[SC], w=[M8])
                P.op("dve", n=64, fn=lambda e: e.tensor_scalar(out=selq[:, 0:64], in0=sc[:], scalar1=m8[:, 7:8], scalar2=1.0, op0=ALU.is_ge, op1=ALU.subtract),
                     r=[SC, M8], w=[SELQ])
                P.op("dve", n=64, fn=lambda e: e.tensor_tensor(out=cf[:, 0:4], in0=rden[:, 0:4], in1=gsig[:, qt, h * 12 + 0:h * 12 + 12:3], op=ALU.mult),
                     r=[DEN, GS[tq4]], w=[DEN])
                for g in range(4):
                    bk = 6 + g // 2
                    o0 = (g % 2) * 129
                    oc = h * 256 + g * 64
                    P.op("dve", n=64, fn=lambda e, bk=bk, o0=o0, g=g, oc=oc: e.tensor_scalar(out=ot_[:, oc:oc + 64], in0=ps[bk][:, o0:o0 + 64],
                                                                                   scalar1=cf[:, g:g + 1], scalar2=None, op0=ALU.mult),
                         r=[PS[bk], DEN], w=[OT_])
                if dbg in ("C1", "C2"):
                    P.op("dve", n=64, fn=lambda e: e.memset(ot_[:, h * 256:h * 256 + 256], 0.0), w=[OT_])
                combine(qt, h, 5, 2)

            def part2(qt, h):
                P.op("pe", n=128, fn=lambda e: e.transpose(out=ps[3][:, 0:128], in_=selq[:], identity=ident_f[:]), r=[SELQ, CONST], w=[PS[3]])
                P.op("dve", n=64, fn=lambda e: e.tensor_copy(out=selT[:], in_=ps[3][:, 0:128]), r=[PS[3]], w=[SELT])

            def part3a(qt, h):
                qb = qt % 2
                for kc in range(qt + 1):
                    masks = [(expb[:, kc * 128:(kc + 1) * 128], bc4(selT[:]), [SELT])]
                    if kc == qt:
                        masks.append((identb[:], bc4(tri[:, 0, :]), []))
                    eb = score_exp(ksP[:, h, kc * 128:(kc + 1) * 128], [KS[kc // 4]], kc, h, qb, masks)

                    def pv_slc(eb=eb, kc=kc, h=h, last=(kc == qt)):
                        for g in range(4):
                            P.op("pe", mm(ps[4][:, g * 65:(g + 1) * 65], et[eb][:, g * 128:(g + 1) * 128], vtok[:, kc, 0, h, :],
                                          kc == 0 and g == 0, last), r=[ET[eb], VT[kc // 4]], w=[PS[4]])
                    push_pv(pv_slc)
                flush()

            def part3b(qt, h):
                combine(qt, h, 4, 1)
                if h == 1:
                    ob_ = qt % 2
                    for c4 in range(4):
                        P.op("pe", lambda e, c4=c4: e.transpose(out=ps[3][:, c4 * 128:(c4 + 1) * 128], in_=otok[ob_][:, c4 * 128:(c4 + 1) * 128], identity=ident_f[:]),
                             r=[OTOK[ob_], CONST], w=[PS[3]])
                    P.op("dve", n=64, fn=lambda e: e.tensor_copy(out=obt[ob_][:], in_=ps[3][:].rearrange("p (a b) -> p a b", a=4)), r=[PS[3]], w=[OBT[ob_]])
                    P.dma("sp", os_d[qt // 4, :, 0:4, (qt % 4) * 128:(qt % 4 + 1) * 128], obt[ob_][:], OBT[ob_], r=[OBT[ob_]], w=[OS[qt // 4]])

            iters = [(qt, h) for qt in range(QN) for h in range(2)]
            loadq(0)
            if QN > 1:
                loadq(1)
            part1(*iters[0])
            part2(*iters[0])
            for i, (qt, h) in enumerate(iters):
                nxt = iters[i + 1] if i + 1 < len(iters) else None
                if nxt:
                    part1(*nxt)
                part3a(qt, h)
                if nxt:
                    part2(*nxt)
                part3b(qt, h)
                if h == 1 and qt + 2 < QN:
                    loadq(qt + 2)
            if dbg and dbg.startswith("C"):
                dump(0, otok[0][:], 512, [OTOK[0]])
                dump(1, otok[1][:], 512, [OTOK[1]])
                dump(2, imp[:], 64, [IMP])
                dump(3, sc[:], 64, [SC])
                dump(4, selq[:], 128, [SELQ])
                dump(5, den[:], 12, [DEN])
                dump(6, rden[:], 12, [DEN])
                dump(7, cf[:], 12, [DEN])
            P.barrier()

    def emit_x_out():
        P.push()
        with contextlib.ExitStack() as ph:
            xt = ph.enter_context(sb("xdbg", [128, 8, TT]))
            XT = T("xdbg")
            for tt in range(NT):
                ts = slice(tt * TT, (tt + 1) * TT)
                P.dma("sp", xt[:], xr_d[tt], XT, r=[XR[tt]], w=[XT])
                P.dma("sp", out_d[tt], xt[:], XT, r=[XT], w=[OUT])
            P.barrier()
        P.pop()

    def gain(step):
        kind, l = step
        return gffn[:, l, :] if kind == "ffn" else gmix[:, l, :]

    cbos = []
    for jl_ in range(2):
        cbo_ = es.enter_context(sb("cbo%d" % jl_, [128, 8]))
        tb = T("cbo")
        P.dma("sp", cbo_[:], dr["cbout"][:, jl_], tb, w=[tb], perm=True)
        CONST.w.update(tb.w)
        cbos.append(cbo_)
    phase_resid("init", None, 0, None, None, None, gain(plan[0]), False)
    for si, (kind, l) in enumerate(plan):
        jl = l // 2
        is_last = si == len(plan) - 1
        last = is_last and final
        gn = gfin[:, :] if last else (gain(plan[si + 1]) if not is_last else gfin[:, :])
        kc_ = NPAIR if kind == "ffn" else 8
        w_dram = {"even": dr["awout"], "odd": dr["cwout"], "ffn": dr["wdn"]}[kind][jl if kind != "ffn" else l]
        with contextlib.ExitStack() as lay:
            P.push()
            wt = lay.enter_context(sb("wres", [128, kc_, D], BF16))
            WT = T("wres")

            def prefetch(wt=wt, WT=WT, w_dram=w_dram, kc_=kc_):
                half = kc_ // 2
                P.dma("pool", wt[:, :half, :], w_dram[:, :half, :], WT, w=[WT], max_dma_last_dim=4096)
                P.dma("pool", wt[:, half:, :], w_dram[:, half:, :], WT, w=[WT], max_dma_last_dim=4096)
            if kind == "even":
                phase_even(jl, dbg, prefetch)
                if dbg:
                    P.pop()
                    break
                with nc.named_scope("resmix%d" % l):
                    phase_resid("mix", None, 8, os_d, OS, None, gn, last, (wt, WT))
            elif kind == "odd":
                cbo = cbos[jl]
                with nc.named_scope("odd%d" % l):
                    phase_odd(jl, prefetch)
                with nc.named_scope("resmix%d" % l):
                    phase_resid("mix", None, 8, os_d, OS, cbo, gn, last, (wt, WT))
            else:
                with nc.named_scope("ffn%d" % l):
                    phase_ffn1(l, prefetch)
                with nc.named_scope("resffn%d" % l):
                    phase_resid("ffn", None, NPAIR, as_d, AS, None, gn, last, (wt, WT))
            P.pop()
    if not final and not dbg:
        emit_x_out()
    P.barrier()
    es.close()
    return nc


SHAPES = None


def _shapes(shared):
    s = {k: v.shape for k, v in shared.items()}
    s["xT"] = (NT, 128, 8, TT)
    return s


def kernel(**inputs):
    shared = prep_shared(inputs)
    x = np.asarray(inputs["x"], dtype=np.float32)
    nb = x.shape[0]
    nc = build(_shapes(shared))
    in_maps = []
    for b in range(nb):
        m = dict(shared)
        m["xT"] = _c(x[b].T.reshape(8, 128, NT, TT).transpose(2, 1, 0, 3))
        in_maps.append(m)
    res = run_bass_kernel_spmd(nc, in_maps, core_ids=list(range(nb)))
    out = np.stack([np.asarray(r["out"]).reshape(NT, 128, 8, TT).transpose(2, 1, 0, 3).reshape(D, S).T for r in res.results], axis=0)
    return np.ascontiguousarray(out.astype(np.float32))
```
